# Optimizing a Trainium2 kernel written in Bass

```python
import math
import jax
import jax.numpy as jnp
from jax import lax
import numpy as np


D_MODEL = 1024
BATCH = 8
SEQ = 4096
DEPTH = 2

HEAD_DIM = 64
A_HEADS = 8
A_WIDTH = A_HEADS * HEAD_DIM
DILATED_PATTERNS = ((128, 1), (512, 4), (2048, 16))
DILATED_BLOCK = 128
B_HEADS = 4
B_HEAD_DIM = 64
B_QK_WIDTH = B_HEADS * 2 * B_HEAD_DIM
B_V_WIDTH = B_HEADS * 2 * B_HEAD_DIM
HYB_IN_WIDTH = 3 * A_WIDTH + 2 * B_QK_WIDTH + B_V_WIDTH
HYB_MIX_WIDTH = A_WIDTH + B_V_WIDTH
ROPE_THETA = 500000.0
PARTIAL_ROT_DIM = HEAD_DIM // 4
MLA_HEADS = 16
MLA_Q_RANK = 256
MLA_KV_RANK = 128
MLA_NOPE_DIM = 64
MLA_ROPE_DIM = 32
MLA_V_DIM = 64
MLA_ROPE_THETA = 10000.0
MLA_IN_WIDTH = MLA_Q_RANK + MLA_KV_RANK + MLA_ROPE_DIM
MLA_MIX_WIDTH = MLA_HEADS * MLA_V_DIM
FFN_DIM = 2816
CONV_WIDTH = 3
Q_BLOCK = 128
NORM_EPS = 1e-6
MAX_POS_OFFSET = 4096

kernel_name = 'hybrid_dilated_diff_mla_convffn'


def rms_norm(x, g):
    xf = x.astype(jnp.float32)
    y = xf * lax.rsqrt(jnp.mean(xf * xf, axis=-1, keepdims=True) + NORM_EPS)
    return (y * g.astype(jnp.float32)).astype(x.dtype)


def rope_cos_sin(positions, rot_dim, theta):
    inv_freq = theta ** (-jnp.arange(0, rot_dim, 2, dtype=jnp.float32) / rot_dim)
    ang = positions.astype(jnp.float32)[..., None] * inv_freq
    return jnp.cos(ang)[:, :, None, :], jnp.sin(ang)[:, :, None, :]


def apply_rotary(x, cos, sin):
    half = cos.shape[-1]
    r = 2 * half
    xf = x[..., :r].astype(jnp.float32)
    x1, x2 = xf[..., :half], xf[..., half:]
    rot = jnp.concatenate([x1 * cos - x2 * sin, x2 * cos + x1 * sin], axis=-1).astype(x.dtype)
    return jnp.concatenate([rot, x[..., r:]], axis=-1)


def dilated_branch(q, k, v, window, dilation):
    B, S, H, D = q.shape
    blk = DILATED_BLOCK
    n_back = window // dilation
    pad = (-S) % (blk * dilation)
    Sp = S + pad
    L = Sp // dilation
    nb = L // blk

    def to_sub(t):
        t = jnp.pad(t, ((0, 0), (0, pad), (0, 0), (0, 0)))
        return t.reshape(B, L, dilation, H, D).swapaxes(1, 2).reshape(B, dilation, nb, blk, H, D)

    def with_prev(t):
        prev = jnp.pad(t[:, :, :-1], ((0, 0), (0, 0), (1, 0), (0, 0), (0, 0), (0, 0)))
        return jnp.concatenate([prev, t], axis=3)

    qs = to_sub(q)
    kk = with_prev(to_sub(k))
    vv = with_prev(to_sub(v))
    s = jnp.einsum('brnqhd,brnkhd->brnhqk', qs, kk).astype(jnp.float32) * (D ** -0.5)
    qi = jnp.arange(blk)[:, None]
    kj = jnp.arange(2 * blk)[None, :]
    dist = qi + blk - kj
    band = (dist >= 0) & (dist <= n_back)
    has_prev = jnp.arange(nb)[:, None, None] > 0
    valid = band[None] & (has_prev | (kj[None] >= blk))
    s = jnp.where(valid[None, None, :, None], s, -jnp.inf)
    m = jnp.max(s, axis=-1)
    p = jnp.exp(s - m[..., None])
    l = jnp.sum(p, axis=-1)
    o = jnp.einsum('brnhqk,brnkhd->brnqhd', p.astype(v.dtype), vv).astype(jnp.float32)
    o = o / jnp.swapaxes(l, -1, -2)[..., None]

    def from_sub(t):
        rest = t.shape[5:]
        return t.reshape(B, dilation, L, H, *rest).swapaxes(1, 2).reshape(B, Sp, H, *rest)[:, :S]

    return from_sub(o), from_sub(jnp.swapaxes(m, -1, -2)), from_sub(jnp.swapaxes(l, -1, -2))


def dilated_mixture(q, k, v):
    stats = [dilated_branch(q, k, v, w, d) for (w, d) in DILATED_PATTERNS]
    o = jnp.stack([st[0] for st in stats])
    m = jnp.stack([st[1] for st in stats])
    l = jnp.stack([st[2] for st in stats])
    wgt = l * jnp.exp(m - jnp.max(m, axis=0, keepdims=True))
    return jnp.sum(wgt[..., None] * o, axis=0) / jnp.sum(wgt, axis=0)[..., None]


def diff_attention(q1, q2, k1, k2, v, lam):
    B, S, H, D = q1.shape
    nb = S // Q_BLOCK
    scale = D ** -0.5
    kpos = jnp.arange(S)

    def blockify(t):
        return t.reshape(B, nb, Q_BLOCK, H, t.shape[-1]).swapaxes(0, 1)

    def one_block(args):
        i, qa, qb = args
        qpos = i * Q_BLOCK + jnp.arange(Q_BLOCK)
        causal = kpos[None, :] <= qpos[:, None]

        def probs(qx, kx):
            s = jnp.einsum('bqhd,bkhd->bhqk', qx, kx).astype(jnp.float32) * scale
            return jax.nn.softmax(jnp.where(causal, s, -jnp.inf), axis=-1)

        w = probs(qa, k1) - lam * probs(qb, k2)
        return jnp.einsum('bhqk,bkhd->bqhd', w.astype(v.dtype), v)

    o = lax.map(one_block, (jnp.arange(nb), blockify(q1), blockify(q2)))
    return o.swapaxes(0, 1).reshape(B, S, H, v.shape[-1])


def causal_attention(q, k, v):
    B, S, H, D = q.shape
    nb = S // Q_BLOCK
    scale = D ** -0.5
    kpos = jnp.arange(S)
    qb = q.reshape(B, nb, Q_BLOCK, H, D).swapaxes(0, 1)

    def one_block(args):
        i, qi = args
        qpos = i * Q_BLOCK + jnp.arange(Q_BLOCK)
        causal = kpos[None, :] <= qpos[:, None]
        s = jnp.einsum('bqhd,bkhd->bhqk', qi, k).astype(jnp.float32) * scale
        p = jax.nn.softmax(jnp.where(causal, s, -jnp.inf), axis=-1)
        return jnp.einsum('bhqk,bkhd->bqhd', p.astype(v.dtype), v)

    o = lax.map(one_block, (jnp.arange(nb), qb))
    return o.swapaxes(0, 1).reshape(B, S, H, v.shape[-1])


def hybrid_mixer(h, layer_idx, cos, sin, w_in, w_out, lq1, lk1, lq2, lk2, subln):
    B, S, _ = h.shape
    p = h @ w_in
    cuts = [A_WIDTH, 2 * A_WIDTH, 3 * A_WIDTH, 3 * A_WIDTH + B_QK_WIDTH, 3 * A_WIDTH + 2 * B_QK_WIDTH]
    qa, ka, va, qb, kb, vb = jnp.split(p, cuts, axis=-1)
    qa = apply_rotary(qa.reshape(B, S, A_HEADS, HEAD_DIM), cos, sin)
    ka = apply_rotary(ka.reshape(B, S, A_HEADS, HEAD_DIM), cos, sin)
    va = va.reshape(B, S, A_HEADS, HEAD_DIM)
    out_a = dilated_mixture(qa, ka, va).astype(h.dtype).reshape(B, S, A_WIDTH)
    qb = qb.reshape(B, S, B_HEADS, 2, B_HEAD_DIM)
    kb = kb.reshape(B, S, B_HEADS, 2, B_HEAD_DIM)
    q1 = apply_rotary(qb[..., 0, :], cos, sin)
    q2 = apply_rotary(qb[..., 1, :], cos, sin)
    k1 = apply_rotary(kb[..., 0, :], cos, sin)
    k2 = apply_rotary(kb[..., 1, :], cos, sin)
    vb = vb.reshape(B, S, B_HEADS, 2 * B_HEAD_DIM)
    lam_init = 0.8 - 0.6 * math.exp(-0.3 * layer_idx)
    f32 = jnp.float32
    lam = (jnp.exp(jnp.sum(lq1.astype(f32) * lk1.astype(f32)))
           - jnp.exp(jnp.sum(lq2.astype(f32) * lk2.astype(f32))) + lam_init)
    ob = diff_attention(q1, q2, k1, k2, vb, lam)
    ob = (rms_norm(ob, subln) * (1.0 - lam_init)).reshape(B, S, B_V_WIDTH)
    return jnp.concatenate([out_a, ob], axis=-1) @ w_out


def mla_mixer(h, cos, sin, w_in, q_norm, w_uq, kv_norm, w_ukv, w_out):
    B, S, _ = h.shape
    p = h @ w_in
    c_q, c_kv, k_pe = jnp.split(p, [MLA_Q_RANK, MLA_Q_RANK + MLA_KV_RANK], axis=-1)
    q = (rms_norm(c_q, q_norm) @ w_uq).reshape(B, S, MLA_HEADS, MLA_NOPE_DIM + MLA_ROPE_DIM)
    q = jnp.concatenate([q[..., :MLA_NOPE_DIM], apply_rotary(q[..., MLA_NOPE_DIM:], cos, sin)], axis=-1)
    kv = (rms_norm(c_kv, kv_norm) @ w_ukv).reshape(B, S, MLA_HEADS, MLA_NOPE_DIM + MLA_V_DIM)
    k_nope, v = kv[..., :MLA_NOPE_DIM], kv[..., MLA_NOPE_DIM:]
    k_pe = apply_rotary(k_pe[:, :, None, :], cos, sin)
    k = jnp.concatenate([k_nope, jnp.broadcast_to(k_pe, (B, S, MLA_HEADS, MLA_ROPE_DIM))], axis=-1)
    o = causal_attention(q, k, v).reshape(B, S, MLA_MIX_WIDTH)
    return o @ w_out


def conv_ffn(h, w_up, conv_w, conv_b, w_down):
    u = h @ w_up
    u = lax.conv_general_dilated(u, conv_w[:, None, :], window_strides=(1,),
                                 padding=[(CONV_WIDTH - 1, 0)],
                                 dimension_numbers=('NWC', 'WIO', 'NWC'),
                                 feature_group_count=u.shape[-1]) + conv_b
    g, val = jnp.split(u, 2, axis=-1)
    return (jax.nn.silu(g) * val) @ w_down


def setup_inputs(seed: int = 0) -> dict:
    key = jax.random.key(seed)
    ks = jax.random.split(key, 24)
    f32 = jnp.float32
    n_even = (DEPTH + 1) // 2
    n_odd = DEPTH // 2

    def w(k, shape, fan_in):
        return jax.random.normal(k, shape, f32) * (fan_in ** -0.5)

    def gain(k, shape):
        return 1.0 + 0.02 * jax.random.normal(k, shape, f32)

    x = jax.random.normal(ks[0], (BATCH, SEQ, D_MODEL), f32)
    offsets = jax.random.randint(ks[1], (BATCH, 1), 0, MAX_POS_OFFSET, dtype=jnp.int32)
    positions = offsets + jnp.arange(SEQ, dtype=jnp.int32)[None, :]
    return {
        'x': x,
        'positions': positions,
        'attn_norm': gain(ks[2], (DEPTH, D_MODEL)),
        'ffn_norm': gain(ks[3], (DEPTH, D_MODEL)),
        'final_norm': gain(ks[4], (D_MODEL,)),
        'hyb_w_in': w(ks[5], (n_even, D_MODEL, HYB_IN_WIDTH), D_MODEL),
        'hyb_w_out': w(ks[6], (n_even, HYB_MIX_WIDTH, D_MODEL), HYB_MIX_WIDTH),
        'diff_lambda_q1': 0.1 * jax.random.normal(ks[7], (n_even, B_HEAD_DIM), f32),
        'diff_lambda_k1': 0.1 * jax.random.normal(ks[8], (n_even, B_HEAD_DIM), f32),
        'diff_lambda_q2': 0.1 * jax.random.normal(ks[9], (n_even, B_HEAD_DIM), f32),
        'diff_lambda_k2': 0.1 * jax.random.normal(ks[10], (n_even, B_HEAD_DIM), f32),
        'diff_subln': gain(ks[11], (n_even, 2 * B_HEAD_DIM)),
        'mla_w_in': w(ks[12], (n_odd, D_MODEL, MLA_IN_WIDTH), D_MODEL),
        'mla_q_norm': gain(ks[13], (n_odd, MLA_Q_RANK)),
        'mla_w_uq': w(ks[14], (n_odd, MLA_Q_RANK, MLA_HEADS * (MLA_NOPE_DIM + MLA_ROPE_DIM)), MLA_Q_RANK),
        'mla_kv_norm': gain(ks[15], (n_odd, MLA_KV_RANK)),
        'mla_w_ukv': w(ks[16], (n_odd, MLA_KV_RANK, MLA_HEADS * (MLA_NOPE_DIM + MLA_V_DIM)), MLA_KV_RANK),
        'mla_w_out': w(ks[17], (n_odd, MLA_MIX_WIDTH, D_MODEL), MLA_MIX_WIDTH),
        'ffn_w_up': w(ks[18], (DEPTH, D_MODEL, 2 * FFN_DIM), D_MODEL),
        'ffn_conv_w': w(ks[19], (DEPTH, CONV_WIDTH, 2 * FFN_DIM), CONV_WIDTH),
        'ffn_conv_b': 0.01 * jax.random.normal(ks[20], (DEPTH, 2 * FFN_DIM), f32),
        'ffn_w_down': w(ks[21], (DEPTH, FFN_DIM, D_MODEL), FFN_DIM),
    }


def reference(x, positions, attn_norm, ffn_norm, final_norm, hyb_w_in, hyb_w_out,
              diff_lambda_q1, diff_lambda_k1, diff_lambda_q2, diff_lambda_k2, diff_subln,
              mla_w_in, mla_q_norm, mla_w_uq, mla_kv_norm, mla_w_ukv, mla_w_out,
              ffn_w_up, ffn_conv_w, ffn_conv_b, ffn_w_down):
    cos_p, sin_p = rope_cos_sin(positions, PARTIAL_ROT_DIM, ROPE_THETA)
    cos_m, sin_m = rope_cos_sin(positions, MLA_ROPE_DIM, MLA_ROPE_THETA)
    for i in range(DEPTH):
        h = rms_norm(x, attn_norm[i])
        j = i // 2
        if i % 2 == 0:
            x = x + hybrid_mixer(h, i, cos_p, sin_p, hyb_w_in[j], hyb_w_out[j],
                                 diff_lambda_q1[j], diff_lambda_k1[j], diff_lambda_q2[j],
                                 diff_lambda_k2[j], diff_subln[j])
        else:
            x = x + mla_mixer(h, cos_m, sin_m, mla_w_in[j], mla_q_norm[j], mla_w_uq[j],
                              mla_kv_norm[j], mla_w_ukv[j], mla_w_out[j])
        x = x + conv_ffn(rms_norm(x, ffn_norm[i]), ffn_w_up[i], ffn_conv_w[i], ffn_conv_b[i], ffn_w_down[i])
    return rms_norm(x, final_norm)
```

```python
import math
import os
import numpy as np
import ml_dtypes
import concourse.bass as bass
import concourse.mybir as mybir
from concourse.bass_utils import run_bass_kernel_spmd

F32 = mybir.dt.float32
BF16 = mybir.dt.bfloat16
I32 = mybir.dt.int32
AF = mybir.ActivationFunctionType
ALU = mybir.AluOpType
AX = mybir.AxisListType

S = 4096
D = 1024
NT = 32
NCH = 8
FFN = 2816
EPS = 1e-6
PI = math.pi


class Sched:
    def __init__(self, nc, stack):
        self.nc = nc
        self.stack = stack
        self.eng = {"pe": nc.tensor, "act": nc.scalar, "dve": nc.vector, "pool": nc.gpsimd, "sp": nc.sync}
        self.sem = {}
        self.cnt = {}
        for e in self.eng:
            self._new_sem(e)
        self.nslots = 12
        self.dslots = {}
        for e in ("sp", "pool", "act"):
            self.dslots[e] = [[stack.enter_context(nc.semaphore(f"d_{e}_{i}")), 0] for i in range(self.nslots)]
        self.dnext = {"sp": 0, "pool": 0, "act": 0}
        self.sig = []
        self.seen = {e: {} for e in self.eng}
        self.last_writer = {}
        self.readers = {}
        self.ps_readers = {}
        self.last_op = {e: None for e in self.eng}
        self.dma_ops = []
        self.nsem = 0
        self.nwaits = 0

    def _new_sem(self, e):
        self.nsem = getattr(self, "nsem", 0) + 1
        self.sem[e] = self.stack.enter_context(self.nc.semaphore(f"s_{e}_{self.nsem}"))
        self.cnt[e] = 0

    def _deps(self, reads, writes, e=None):
        deps = set()
        for t in reads:
            w = self.last_writer.get(t)
            if w is not None:
                deps.add(w)
            if isinstance(t, tuple) and t[0] == "ps":
                for (rid, re_) in self.ps_readers.get(t, []):
                    if re_ != e:
                        deps.add(rid)
        for t in writes:
            w = self.last_writer.get(t)
            if w is not None:
                deps.add(w)
            r = self.readers.get(t)
            if r:
                deps.update(r)
        return deps

    def _commit(self, oid, reads, writes, e=None):
        for t in reads:
            self.readers.setdefault(t, []).append(oid)
            if isinstance(t, tuple) and t[0] == "ps":
                self.ps_readers.setdefault(t, []).append((oid, e))
        for t in writes:
            self.last_writer[t] = oid
            self.readers[t] = []
            if isinstance(t, tuple) and t[0] == "ps":
                self.ps_readers[t] = []

    def _emit_waits(self, e, deps, attach=False):
        engobj = self.eng[e]
        need = {}
        for d in deps:
            s = self.sig[d]
            if s is None:
                continue
            sem, val, seng = s
            if seng == "pe" and e == "pe":
                continue
            k = id(sem)
            if self.seen[e].get(k, 0) >= val:
                continue
            if k not in need or need[k][1] < val:
                need[k] = (sem, val)
        items = list(need.items())
        last = None
        if attach and items:
            last = items.pop()
        for k, (sem, val) in items:
            engobj.wait_ge(sem, val)
            self.seen[e][k] = val
            self.nwaits += 1
        if last is not None:
            self.seen[e][last[0]] = last[1][1]
            return last[1]
        return None

    def op(self, e, fn, reads=(), writes=(), obs=True, after=()):
        deps = self._deps(reads, writes, e)
        deps.update(after)
        last = self._emit_waits(e, deps, attach=True)
        ins = fn(self.eng[e])
        if last is not None:
            ins._wait_ge(last[0], last[1])
        oid = len(self.sig)
        if obs:
            if self.cnt[e] >= 30000:
                self._new_sem(e)
            self.cnt[e] += 1
            ins.then_inc(self.sem[e], 1)
            self.sig.append((self.sem[e], self.cnt[e], e))
        else:
            self.sig.append(None)
        self._commit(oid, reads, writes, e)
        if obs:
            self.last_op[e] = oid
        return oid

    def dma(self, e, out, in_, reads=(), writes=(), after=()):
        deps = self._deps(reads, writes)
        deps.update(after)
        self._emit_waits(e, deps)
        si = self.dnext[e]
        self.dnext[e] = (si + 1) % self.nslots
        slot = self.dslots[e][si]
        sem, uses = slot
        engobj = self.eng[e]
        k = id(sem)
        if uses > 0 and self.seen[e].get(k, 0) < 16 * uses:
            engobj.wait_ge(sem, 16 * uses)
            self.seen[e][k] = 16 * uses
        if uses >= 1800:
            raise RuntimeError("dma sem overflow")
        ins = engobj.dma_start(out=out, in_=in_)
        slot[1] = uses + 1
        ins.then_inc(sem, 16)
        oid = len(self.sig)
        self.sig.append((sem, 16 * (uses + 1), "dma_" + e))
        self._commit(oid, reads, writes)
        self.dma_ops.append(oid)
        return oid

    def barrier(self):
        deps = set(self.dma_ops)
        for e in self.eng:
            if self.last_op[e] is not None:
                deps.add(self.last_op[e])
        self.dma_ops = []
        self._emit_waits_barrier("sp", deps)
        ins = self.eng["sp"].nop()
        self.cnt["sp"] += 1
        ins.then_inc(self.sem["sp"], 1)
        oid = len(self.sig)
        self.sig.append((self.sem["sp"], self.cnt["sp"], "sp"))
        self.last_op["sp"] = oid
        for e in self.eng:
            if e == "sp":
                continue
            self.eng[e].wait_ge(self.sem["sp"], self.cnt["sp"])
            self.seen[e][id(self.sem["sp"])] = self.cnt["sp"]
        self.last_writer = {}
        self.readers = {}
        self.ps_readers = {}

    def _emit_waits_barrier(self, e, deps):
        engobj = self.eng[e]
        need = {}
        for d in deps:
            s = self.sig[d]
            if s is None:
                continue
            sem, val, seng = s
            k = id(sem)
            if self.seen[e].get(k, 0) >= val:
                continue
            if k not in need or need[k][1] < val:
                need[k] = (sem, val)
        for k, (sem, val) in need.items():
            engobj.wait_ge(sem, val)
            self.seen[e][k] = val


def mkap(base, off, dims):
    return bass.AP(tensor=base.tensor, offset=base.offset + off, ap=[list(d) for d in dims])


def build_program(debug=False, stop_after=None, only=None):
    nc = bass.Bass("TRN2", target_bir_lowering=False)
    from contextlib import ExitStack

    def din(name, shape, dt=F32):
        return nc.dram_tensor(name, list(shape), dt, kind="ExternalInput").ap()

    def dscr(name, shape, dt):
        skind = "ExternalOutput" if (debug and name in debug) else "Internal"
        return nc.dram_tensor(name, list(shape), dt, kind=skind).ap()

    x_d = din("x", [S, D])
    pos_d = din("pos", [128, NT], I32)
    attn_norm_d = din("attn_norm", [2, 128, 8])
    ffn_norm_d = din("ffn_norm", [2, 128, 8])
    final_norm_d = din("final_norm", [D])
    hyb_w_in_d = din("hyb_w_in", [D, 3072])
    hyb_w_out_d = din("hyb_w_out", [D, D])
    lq1_d = din("lq1", [64])
    lk1_d = din("lk1", [64])
    lq2_d = din("lq2", [64])
    lk2_d = din("lk2", [64])
    subln_d = din("subln", [128])
    mla_w_in_d = din("mla_w_in", [D, 416])
    mla_q_norm_d = din("mla_q_norm", [256])
    mla_w_uq_d = din("mla_w_uq", [256, 1536])
    mla_kv_norm_d = din("mla_kv_norm", [128])
    mla_w_ukv_d = din("mla_w_ukv", [128, 2048])
    mla_w_out_d = din("mla_w_out", [D, D])
    ffn_w_up_d = din("ffn_w_up", [2, D, 2 * FFN])
    ffn_conv_w_d = din("ffn_conv_w", [2, 128, 3, 44])
    ffn_conv_b_d = din("ffn_conv_b", [2, 128, 44])
    ffn_w_down_d = din("ffn_w_down", [2, FFN, D])
    masks_d = din("masks", [24, 128, 512], BF16)
    ident_d = din("ident", [128, 128], BF16)

    out_d = nc.dram_tensor("out", [S, D], F32, kind="ExternalOutput").ap()

    xres_d = dscr("xres", [S, D], F32)
    qkT0_d = dscr("qkT0", [16, 128, S], BF16)
    v0_d = dscr("v0", [S, 1024], BF16)
    oT_d = dscr("oT", [8, 128, S], BF16)
    qT1_d = dscr("qT1", [16, 96, S], BF16)
    kT1_d = dscr("kT1", [16, 96, S], BF16)
    v1_d = dscr("v1", [S, 1024], BF16)

    inv_freq_p = [500000.0 ** (-(2 * i) / 16.0) for i in range(8)]
    inv_freq_m = [10000.0 ** (-(2 * i) / 32.0) for i in range(16)]

    with ExitStack() as top:
        sc = Sched(nc, top)
        ps_t = top.enter_context(nc.psum_tensor("ps", [128, 8, 512], F32))
        ps = ps_t[:]

        def bank(b):
            return ps[:, b, :]

        def bank_bf(b, n=1):
            return ps[:, b, :].bitcast(BF16)

        tmpst = ExitStack()
        ident = top.enter_context(nc.sbuf_tensor("c_ident", [128, 128], BF16))[:]
        cosP = top.enter_context(nc.sbuf_tensor("c_cosP", [128, NT, 8], F32))[:]
        sinP = top.enter_context(nc.sbuf_tensor("c_sinP", [128, NT, 8], F32))[:]
        cosM = top.enter_context(nc.sbuf_tensor("c_cosM", [128, NT, 16], F32))[:]
        sinM = top.enter_context(nc.sbuf_tensor("c_sinM", [128, NT, 16], F32))[:]
        ones_bf = top.enter_context(nc.sbuf_tensor("c_ones_bf", [128, 128], BF16))[:]
        ones_f = top.enter_context(nc.sbuf_tensor("c_ones_f", [128, 128], F32))[:]
        negpi = top.enter_context(nc.sbuf_tensor("c_negpi", [128, 1], F32))[:]

        epsc = top.enter_context(nc.sbuf_tensor("c_epsc", [128, 4], F32))[:]
        for k_, v_ in enumerate([D * EPS, 256 * EPS, 128 * EPS, EPS]):
            sc.op("dve", lambda e, k_=k_, v_=v_: e.memset(epsc[:, k_:k_ + 1], float(v_)), writes=["epsc"])
        sc.dma("sp", ident, ident_d, writes=["ident"])
        sc.op("dve", lambda e: e.memset(ones_bf, 1.0), writes=["ones_bf"])
        sc.op("dve", lambda e: e.memset(ones_f, 1.0), writes=["ones_f"])
        sc.op("dve", lambda e: e.memset(negpi, -PI), writes=["negpi"])

        posi = tmpst.enter_context(nc.sbuf_tensor("c_posi", [128, NT], I32))[:]
        posf = tmpst.enter_context(nc.sbuf_tensor("c_posf", [128, NT], F32))[:]
        angt = tmpst.enter_context(nc.sbuf_tensor("c_angt", [128, NT, 16], F32))[:]
        ang2 = tmpst.enter_context(nc.sbuf_tensor("c_ang2", [128, NT, 16], F32))[:]
        rr_ki = tmpst.enter_context(nc.sbuf_tensor("c_rr_ki", [128, NT, 16], I32))[:]
        rr_kf = tmpst.enter_context(nc.sbuf_tensor("c_rr_kf", [128, NT, 16], F32))[:]
        rr_m = tmpst.enter_context(nc.sbuf_tensor("c_rr_m", [128, NT, 16], F32))[:]
        sc.dma("sp", posi, pos_d, writes=["posi"])
        sc.op("dve", lambda e: e.tensor_copy(out=posf, in_=posi), reads=["posi"], writes=["posf"])
        C1 = 6.28125
        C2 = 2 * PI - C1

        def reduced_sin(dst, nf, add):
            x = ang2[:, :, 0:nf]
            ki = rr_ki[:, :, 0:nf]
            kf = rr_kf[:, :, 0:nf]
            m = rr_m[:, :, 0:nf]
            sc.op("dve", lambda e: e.tensor_scalar(out=x, in0=angt[:, :, 0:nf], scalar1=float(add), scalar2=None,
                                                   op0=ALU.add), reads=["angt"], writes=["ang2"])
            sc.op("dve", lambda e: e.tensor_scalar(out=kf, in0=x, scalar1=float(1.0 / (2 * PI)), scalar2=None,
                                                   op0=ALU.mult), reads=["ang2"], writes=["rr_kf"])
            sc.op("dve", lambda e: e.tensor_copy(out=ki, in_=kf), reads=["rr_kf"], writes=["rr_ki"])
            sc.op("dve", lambda e: e.tensor_copy(out=kf, in_=ki), reads=["rr_ki"], writes=["rr_kf"])
            sc.op("dve", lambda e: e.scalar_tensor_tensor(out=x, in0=kf, scalar=float(-C1), in1=x,
                                                          op0=ALU.mult, op1=ALU.add),
                  reads=["rr_kf", "ang2"], writes=["ang2"])
            sc.op("dve", lambda e: e.scalar_tensor_tensor(out=x, in0=kf, scalar=float(-C2), in1=x,
                                                          op0=ALU.mult, op1=ALU.add),
                  reads=["rr_kf", "ang2"], writes=["ang2"])
            sc.op("dve", lambda e: e.tensor_scalar(out=m, in0=x, scalar1=float(PI), scalar2=float(-2 * PI),
                                                   op0=ALU.is_gt, op1=ALU.mult), reads=["ang2"], writes=["rr_m"])
            sc.op("dve", lambda e: e.tensor_tensor(out=x, in0=x, in1=m, op=ALU.add),
                  reads=["ang2", "rr_m"], writes=["ang2"])
            sc.op("dve", lambda e: e.tensor_scalar(out=m, in0=x, scalar1=float(-PI), scalar2=float(2 * PI),
                                                   op0=ALU.is_lt, op1=ALU.mult), reads=["ang2"], writes=["rr_m"])
            sc.op("dve", lambda e: e.tensor_tensor(out=x, in0=x, in1=m, op=ALU.add),
                  reads=["ang2", "rr_m"], writes=["ang2"])
            sc.op("act", lambda e: e.activation(out=dst, in_=x, func=AF.Sin), reads=["ang2"], writes=["ropetab"])

        def rope_tables(cos_t, sin_t, freqs, name):
            nf = len(freqs)
            for i, f in enumerate(freqs):
                sc.op("dve", lambda e, i=i, f=f: e.tensor_scalar(
                    out=angt[:, :, i], in0=posf, scalar1=float(f), scalar2=None, op0=ALU.mult),
                    reads=["posf", "ropetab"], writes=["angt"])
            reduced_sin(sin_t, nf, 0.0)
            reduced_sin(cos_t, nf, 0.5 * PI)

        rope_tables(cosP, sinP, inv_freq_p, "ropeP")
        rope_tables(cosM, sinM, inv_freq_m, "ropeM")
        sc.barrier()
        tmpst.close()

        def norm_part1(ph, i, src_d, xt, hn, ss, rs, ldq="sp"):
            r = i % 2
            sc.dma(ldq, xt[r], src_d[i * 128:(i + 1) * 128, :], reads=[("dram", src_d.tensor.name, i)],
                   writes=[(ph, "xt", r)])
            sc.op("act", lambda e: e.activation(out=hn[r], in_=xt[r], func=AF.Square, accum_out=ss[r]),
                  reads=[(ph, "xt", r)], writes=[(ph, "hn", r), (ph, "ss", r)])
            sc.op("act", lambda e: e.activation(out=rs[r], in_=ss[r], func=AF.Sqrt, bias=epsc[:, 0:1]),
                  reads=[(ph, "ss", r)], writes=[(ph, "rs", r)])
            sc.op("dve", lambda e: e.reciprocal(out=rs[r], in_=rs[r]),
                  reads=[(ph, "rs", r)], writes=[(ph, "rs", r)])
            sc.op("act", lambda e: e.activation(out=hn[r], in_=xt[r], func=AF.Copy, scale=rs[r]),
                  reads=[(ph, "xt", r), (ph, "rs", r)], writes=[(ph, "hn", r)])

        def norm_part2(ph, i, hn, hT, gcol, tb):
            r = i % 2
            tbv = bank_bf(tb)
            for kc in range(8):
                sc.op("pe", lambda e, kc=kc: e.transpose(out=tbv[:, kc * 128:(kc + 1) * 128],
                                                         in_=hn[r][:, kc * 128:(kc + 1) * 128], identity=ident),
                      reads=[(ph, "hn", r), "ident"], writes=[("ps", tb)], obs=(kc == 7))
            sc.op("dve", lambda e: e.tensor_tensor(
                out=hT[r], in0=tbv.rearrange("p (c t) -> p c t", c=8),
                in1=mkap(gcol, 0, [[8, 128], [1, 8], [0, 128]]), op=ALU.mult),
                reads=[("ps", tb), (ph, "gcol")], writes=[(ph, "hT", r)])

        def norm_tile(ph, i, src_d, xt, hn, ss, rs, hT, gcol, tb, ldq="sp"):
            norm_part1(ph, i, src_d, xt, hn, ss, rs, ldq)
            norm_part2(ph, i, hn, hT, gcol, tb)

        def load_gcol(ph, gcol, norm_row_ap, mult):
            sc.dma("sp", gcol, norm_row_ap, writes=[(ph, "gcol")])
            sc.op("dve", lambda e: e.tensor_scalar(out=gcol, in0=gcol, scalar1=float(mult), scalar2=None,
                                                   op0=ALU.mult),
                  reads=[(ph, "gcol")], writes=[(ph, "gcol")])

        def rope_free(ph, src, dst, nslot, half, cos_ap, sin_ap, tmp, r, src_tok, dst_tok):
            cb = mkap(cos_ap, 0, [list(cos_ap.ap[0]), [0, nslot], [1, half]])
            sb = mkap(sin_ap, 0, [list(sin_ap.ap[0]), [0, nslot], [1, half]])
            x1 = src[:, :, 0:half]
            x2 = src[:, :, half:2 * half]
            n = nslot * half
            t = [tmp[r][:, k, 0:n].rearrange("p (s h) -> p s h", s=nslot) for k in range(4)]
            tt = (ph, "ropetmp", r)
            sc.op("dve", lambda e: e.tensor_tensor(out=t[0], in0=x1, in1=cb, op=ALU.mult),
                  reads=src_tok + ["ropeP", "ropeM"], writes=[tt])
            sc.op("dve", lambda e: e.tensor_tensor(out=t[1], in0=x2, in1=sb, op=ALU.mult),
                  reads=src_tok, writes=[tt])
            sc.op("dve", lambda e: e.tensor_tensor(out=t[2], in0=x2, in1=cb, op=ALU.mult),
                  reads=src_tok, writes=[tt])
            sc.op("dve", lambda e: e.tensor_tensor(out=t[3], in0=x1, in1=sb, op=ALU.mult),
                  reads=src_tok, writes=[tt])
            sc.op("dve", lambda e: e.tensor_tensor(out=dst[:, :, 0:half], in0=t[0], in1=t[1], op=ALU.subtract),
                  reads=[tt], writes=dst_tok)
            sc.op("dve", lambda e: e.tensor_tensor(out=dst[:, :, half:2 * half], in0=t[2], in1=t[3], op=ALU.add),
                  reads=[tt], writes=dst_tok)

        def phase_P0():
            ph = "P0"
            with ExitStack() as st:
                def sb(name, shape, dt):
                    return st.enter_context(nc.sbuf_tensor(ph + name, shape, dt))[:]
                win = sb("win", [128, 8, 3072], BF16)
                gcol = sb("gcol", [128, 8], F32)
                xt = [sb(f"xt{r}", [128, D], F32) for r in range(2)]
                hn = [sb(f"hn{r}", [128, D], BF16) for r in range(2)]
                ss = [sb(f"ss{r}", [128, 1], F32) for r in range(2)]
                rs = [sb(f"rs{r}", [128, 1], F32) for r in range(2)]
                hT = [sb(f"hT{r}", [128, 8, 128], BF16) for r in range(2)]
                qk = [sb(f"qk{r}", [128, 2048], BF16) for r in range(2)]
                vs = [sb(f"vs{r}", [128, 1024], BF16) for r in range(2)]
                tmp = [sb(f"tmp{r}", [128, 4, 256], F32) for r in range(2)]
                stage = [sb(f"stage{r}", [128, 16, 512], BF16) for r in range(2)]
                ropein = [sb(f"ropein{r}", [128, 32, 16], F32) for r in range(2)]
                for kc in range(8):
                    sc.dma("pool", win[:, kc, :], hyb_w_in_d[kc * 128:(kc + 1) * 128, :], writes=[(ph, "win", kc)])
                load_gcol(ph, gcol, attn_norm_d[0], math.sqrt(D))
                ntile = int(os.environ.get("DBG_NT", NT))

                def stage_A1(i):
                    norm_part1(ph, i, x_d, xt, hn, ss, rs, ldq="pool")

                def stage_A2(i):
                    norm_part2(ph, i, hn, hT, gcol, 0)

                def stage_MM(i):
                    r = i % 2
                    for kc in range(8):
                        for n in range(6):
                            sc.op("pe", lambda e, kc=kc, n=n: e.matmul(
                                bank(2 + n), lhsT=hT[r][:, kc, :], rhs=win[:, kc, n * 512:(n + 1) * 512],
                                start=(kc == 0), stop=(kc == 7)),
                                reads=[(ph, "hT", r), (ph, "win", kc)], writes=[("ps", 2 + n)], obs=(kc == 7))

                def stage_B(i):
                    r = i % 2
                    rbanks = [(2, 0), (3, 512), (5, 1024), (6, 1536)]
                    for bi, (b0, off) in enumerate(rbanks):
                        src = bank(b0).rearrange("p (s d) -> p s d", d=64)[:, :, 0:16]
                        sc.op("dve", lambda e, bi=bi, src=src: e.tensor_copy(out=ropein[r][:, bi * 8:(bi + 1) * 8, :], in_=src),
                              reads=[("ps", b0)], writes=[(ph, "ropein", r)])
                    for n, (dst, off) in enumerate([(qk, 0), (qk, 512), (vs, 0), (qk, 1024), (qk, 1536), (vs, 512)]):
                        tok = (ph, "qk", r) if dst is qk else (ph, "vs", r)
                        if n % 2 == 0:
                            sc.op("act", lambda e, n=n, dst=dst, off=off: e.activation(
                                out=dst[r][:, off:off + 512], in_=bank(2 + n), func=AF.Copy),
                                reads=[("ps", 2 + n)], writes=[tok])
                        else:
                            sc.op("dve", lambda e, n=n, dst=dst, off=off: e.tensor_copy(
                                out=dst[r][:, off:off + 512], in_=bank(2 + n)),
                                reads=[("ps", 2 + n)], writes=[tok])
                    dst = qk[r].rearrange("p (s d) -> p s d", d=64)[:, :, 0:16]
                    rope_free(ph, ropein[r], dst, 32, 8, cosP[:, i, :], sinP[:, i, :], tmp, r,
                              [(ph, "ropein", r)], [(ph, "qk", r)])
                    sc.dma("sp", v0_d[i * 128:(i + 1) * 128, :], vs[r], reads=[(ph, "vs", r)],
                           writes=[("dram", "v0", i)])

                def stage_C(i):
                    r = i % 2
                    sr = (i // 4) % 2
                    for hb in (1, 0):
                        tbv = bank_bf(hb)
                        for c in range(8):
                            ch = hb * 8 + c
                            sc.op("pe", lambda e, c=c, ch=ch, tbv=tbv: e.transpose(
                                out=tbv[:, c * 128:(c + 1) * 128], in_=qk[r][:, ch * 128:(ch + 1) * 128],
                                identity=ident),
                                reads=[(ph, "qk", r), "ident"], writes=[("ps", hb)], obs=(c == 7))
                        dstv = stage[sr][:, hb * 8:(hb + 1) * 8, (i % 4) * 128:(i % 4 + 1) * 128]
                        srcv = tbv.rearrange("p (c t) -> p c t", c=8)
                        if hb == 0:
                            sc.op("act", lambda e, dstv=dstv, srcv=srcv: e.activation(out=dstv, in_=srcv, func=AF.Copy),
                                  reads=[("ps", hb)], writes=[(ph, "stage", sr)])
                        else:
                            sc.op("dve", lambda e, dstv=dstv, srcv=srcv: e.tensor_copy(out=dstv, in_=srcv),
                                  reads=[("ps", hb)], writes=[(ph, "stage", sr)])
                    if i % 4 == 3:
                        c0 = (i // 4) * 512
                        sc.dma("sp", qkT0_d[:, :, c0:c0 + 512].rearrange("c p t -> p c t"), stage[sr],
                               reads=[(ph, "stage", sr)], writes=[("dram", "qkT0", i // 4)])

                stage_A1(0)
                stage_A2(0)
                if ntile > 1:
                    stage_A1(1)
                for i in range(ntile):
                    if i + 2 < ntile:
                        stage_A1(i + 2)
                    stage_MM(i)
                    if i + 1 < ntile:
                        stage_A2(i + 1)
                    stage_B(i)
                    if i >= 1:
                        stage_C(i - 1)
                stage_C(ntile - 1)
                sc.barrier()

        class AttnCtx:
            pass

        def attention_units(c, causal_only):
            res = []
            if causal_only:
                for kb in range(0, 4 * c + 4):
                    o = kb - 4 * c
                    if o < 0:
                        res.append((kb, None, 0, 512))
                    else:
                        res.append((kb, 20 + o, 128 * o, 512))
            else:
                full, part = [], []
                for kb in range(max(0, 4 * c - 16), 4 * c + 4):
                    o = kb - 4 * c
                    c0 = 128 * o if o > 0 else 0
                    c1 = min(512, 2048 + 128 * o + 128)
                    (full if (c0 == 0 and c1 == 512) else part).append((kb, o + 16, c0, c1))
                res = full + part
            assert res[0][2] == 0 and res[0][3] == 512
            return res

        fin_pending = []

        def norm64_pieces(ph, ab, k, recip, dst_fn):
            for j in range(4):
                cs = slice(j * 128, (j + 1) * 128)
                fin_pending.append((ab, lambda cs=cs, j=j: sc.op(
                    "dve", lambda e: e.reciprocal(out=recip[k][0:64, cs], in_=ps[64:128, ab, cs]),
                    reads=[("ps", ab)], writes=[(ph, "recip", k, j)])))
            for j2 in range(2):
                cs = slice(j2 * 256, (j2 + 1) * 256)
                fin_pending.append((ab, lambda cs=cs, j2=j2: sc.op(
                    "dve", lambda e: e.tensor_tensor(out=dst_fn(cs), in0=ps[0:64, ab, cs], in1=recip[k][0:64, cs],
                                                     op=ALU.mult),
                    reads=[("ps", ab), (ph, "recip", k, 2 * j2), (ph, "recip", k, 2 * j2 + 1)],
                    writes=[(ph, "ostw", k, j2)])))

        def run_attention(ph, streams, masks_sb, pbuf, scale, finalize, sbanks=(0, 1, 2, 3)):
            units = []
            for (grp, c, items) in streams:
                for it_i, it in enumerate(items):
                    ul = it["units"](c)
                    for ui, (kb, mid, c0, c1) in enumerate(ul):
                        units.append(dict(it=it, c=c, kb=kb, mid=mid, c0=c0, c1=c1, first=(ui == 0), last=(ui == len(ul) - 1),
                                          fin=(ui == len(ul) - 1 and it_i == len(items) - 1), grp=grp, items=items))
            LA = len(sbanks) - 1
            NP = len(pbuf)
            n = len(units)
            pending = fin_pending

            def flush_bank(pb):
                keep = []
                for (bk, fn_) in pending:
                    if bk == pb:
                        fn_()
                    else:
                        keep.append((bk, fn_))
                pending[:] = keep

            for i in range(n + LA):
                if i < n:
                    u = units[i]
                    it = u["it"]
                    sbk = sbanks[i % len(sbanks)]
                    pi = i % NP
                    c = u["c"]
                    kb = u["kb"]
                    c0, c1 = u["c0"], u["c1"]
                    sc.op("pe", lambda e, it=it, c=c, kb=kb, sbk=sbk, c0=c0, c1=c1: e.matmul(
                        ps[:, sbk, c0:c1], lhsT=it["kT"][:, kb * 128:(kb + 1) * 128],
                        rhs=it["qT"][:, c * 512 + c0:c * 512 + c1], start=True, stop=True),
                        reads=it["toks"], writes=[("ps", sbk)])
                    sc.op("act", lambda e, sbk=sbk, pi=pi, c0=c0, c1=c1: e.activation(
                        out=pbuf[pi][:, c0:c1], in_=ps[:, sbk, c0:c1], func=AF.Exp, scale=float(scale)),
                        reads=[("ps", sbk)], writes=[(ph, "pbuf", pi)])
                    if u["mid"] is not None:
                        mid = u["mid"]
                        sc.op("pool" if (i % 5 in (1, 3)) else "dve", lambda e, pi=pi, mid=mid, c0=c0, c1=c1: e.tensor_tensor(
                            out=pbuf[pi][:, c0:c1], in0=pbuf[pi][:, c0:c1], in1=masks_sb[:, mid, c0:c1], op=ALU.mult),
                            reads=[(ph, "pbuf", pi), (ph, "masks")], writes=[(ph, "pbuf", pi)])
                    if pending:
                        pending.pop(0)[1]()
                j = i - LA
                if j >= 0:
                    u = units[j]
                    it = u["it"]
                    pi = j % NP
                    kb = u["kb"]
                    c0, c1 = u["c0"], u["c1"]
                    for (lfn, pb) in it["pv"]:
                        if u["first"]:
                            flush_bank(pb)
                        sc.op("pe", lambda e, lfn=lfn, pb=pb, kb=kb, pi=pi, u=u, c0=c0, c1=c1: e.matmul(
                            ps[:, pb, c0:c1], lhsT=lfn(kb), rhs=pbuf[pi][:, c0:c1], start=u["first"], stop=u["last"]),
                            reads=[(ph, "pbuf", pi)] + it["vtoks"], writes=[("ps", pb)], obs=u["last"])
                    if u["last"] and it.get("fin_item") is not None:
                        it["fin_item"](u["c"], it)
                    if u["fin"]:
                        finalize(u["grp"], u["c"], u["items"])
            while pending:
                pending.pop(0)[1]()

        def phase_A0():
            ph = "A0"
            with ExitStack() as st:
                def sb(name, shape, dt):
                    return st.enter_context(nc.sbuf_tensor(ph + name, shape, dt))[:]
                masks_sb = sb("masks", [128, 24, 512], BF16)
                pbuf = [sb(f"pbuf{k}", [128, 512], BF16) for k in range(8)]
                qTc = [sb(f"qT{r}", [128, S], BF16) for r in range(2)]
                kz = [[sb(f"kz{r}{e_}", [128, S], BF16) for e_ in range(2)] for r in range(2)]
                vaug = [sb(f"va{r}", [128, NT, 2, 128], BF16) for r in range(2)]
                ost = [sb(f"ost{r}", [128, S], BF16) for r in range(2)]
                recip = [sb(f"rc{r}", [128, 512], F32) for r in range(2)]
                o1 = sb("o1", [128, 512], F32)
                t2 = sb("t2", [128, 512], F32)
                r2 = sb("r2", [128, 512], F32)
                sq = sb("sq", [128, 512], F32)
                lam_in = sb("lam_in", [128, 4, 64], F32)
                lam_p = sb("lam_p", [128, 2, 64], F32)
                lam_s = sb("lam_s", [128, 2], F32)
                lam_e = sb("lam_e", [128, 2], F32)
                neglam = sb("neglam", [128, 1], F32)
                sgain = sb("sgain", [128, 1], F32)

                sc.dma("sp", masks_sb, masks_d.rearrange("m p t -> p m t"), writes=[(ph, "masks")])
                for r in range(2):
                    sc.op("pool", lambda e, r=r: e.memset(vaug[r][:, :, :, 64:128], 1.0), writes=[(ph, "vaug1", r)])
                    sc.op("pool", lambda e, r=r: e.memset(kz[r][0][64:128, :], 0.0), writes=[(ph, "kzz", r, 0)])
                    sc.op("pool", lambda e, r=r: e.memset(kz[r][1][0:64, :], 0.0), writes=[(ph, "kzz", r, 1)])
                for k, d_ in enumerate([lq1_d, lk1_d, lq2_d, lk2_d]):
                    sc.dma("sp", lam_in[:, k, :], mkap(d_, 0, [[0, 128], [1, 64]]), writes=[(ph, "lam_in")])
                sc.dma("sp", sgain, mkap(subln_d, 0, [[1, 128], [1, 1]]), writes=[(ph, "sgain")])
                sc.op("dve", lambda e: e.tensor_tensor(out=lam_p[:, 0, :], in0=lam_in[:, 0, :], in1=lam_in[:, 1, :],
                                                       op=ALU.mult), reads=[(ph, "lam_in")], writes=[(ph, "lam_p")])
                sc.op("dve", lambda e: e.tensor_tensor(out=lam_p[:, 1, :], in0=lam_in[:, 2, :], in1=lam_in[:, 3, :],
                                                       op=ALU.mult), reads=[(ph, "lam_in")], writes=[(ph, "lam_p")])
                sc.op("dve", lambda e: e.reduce_sum(out=lam_s, in_=lam_p, axis=AX.X),
                      reads=[(ph, "lam_p")], writes=[(ph, "lam_s")])
                sc.op("act", lambda e: e.activation(out=lam_e, in_=lam_s, func=AF.Exp),
                      reads=[(ph, "lam_s")], writes=[(ph, "lam_e")])
                sc.op("dve", lambda e: e.tensor_tensor(out=neglam, in0=lam_e[:, 1:2], in1=lam_e[:, 0:1],
                                                       op=ALU.subtract), reads=[(ph, "lam_e")], writes=[(ph, "neglam")])
                sc.op("dve", lambda e: e.tensor_scalar(out=neglam, in0=neglam, scalar1=-0.2, scalar2=None,
                                                       op0=ALU.add), reads=[(ph, "neglam")], writes=[(ph, "neglam")])
                sc.op("dve", lambda e: e.tensor_scalar(out=sgain, in0=sgain, scalar1=float(math.sqrt(128.0) * 0.8),
                                                       scalar2=None, op0=ALU.mult),
                      reads=[(ph, "sgain")], writes=[(ph, "sgain")])

                for pr in range(4):
                    r = pr % 2
                    sc.dma("sp", qTc[r], qkT0_d[pr], reads=[("dram", "qkT0", k) for k in range(8)],
                           writes=[(ph, "qT", r)])
                    for e_ in range(2):
                        sc.dma("sp", kz[r][e_][e_ * 64:(e_ + 1) * 64, :], qkT0_d[4 + pr, e_ * 64:(e_ + 1) * 64, :],
                               reads=[("dram", "qkT0", k) for k in range(8)], writes=[(ph, "kT", r, e_)])
                    for e_ in range(2):
                        h = pr * 2 + e_
                        sc.dma("sp", vaug[r][:, :, e_, 0:64],
                               v0_d[:, h * 64:(h + 1) * 64].rearrange("(t p) d -> p t d", p=128),
                               reads=[("dram", "v0", k) for k in range(NT)], writes=[(ph, "vaug", r, e_)])
                    streams = []
                    for e_ in range(2):
                        rows = slice(e_ * 64, (e_ + 1) * 64)
                        for c in range(NCH):
                            ab = 6 + (len(streams) % 2)
                            it = dict(qT=qTc[r], kT=kz[r][e_],
                                      pv=[((lambda kb, r=r, e_=e_: vaug[r][:, kb, e_, :]), ab)],
                                      units=lambda c: attention_units(c, False),
                                      toks=[(ph, "qT", r), (ph, "kT", r, e_), (ph, "kzz", r, e_)],
                                      vtoks=[(ph, "vaug", r, e_), (ph, "vaug1", r)], ab=ab, e=e_, r=r)
                            streams.append(((pr, e_), c, [it]))

                    def fin_a(grp, c, items):
                        it = items[0]
                        ab = it["ab"]
                        e_ = it["e"]
                        rr = it["r"]
                        k = ab - 6
                        norm64_pieces(ph, ab, k, recip,
                                      lambda cs: ost[rr][e_ * 64:(e_ + 1) * 64, c * 512 + cs.start:c * 512 + cs.stop])
                    run_attention(ph, streams, masks_sb, pbuf, 0.125, fin_a, sbanks=(0, 1, 2, 3, 4, 5))
                    sc.dma("pool", oT_d[pr], ost[r], reads=[(ph, "ost", r)] + [(ph, "ostw", k_, j_) for k_ in range(2) for j_ in range(2)],
                           writes=[("dram", "oT", pr)])

                for h in range(4):
                    r = h % 2
                    sc.dma("sp", qTc[r], qkT0_d[8 + h], reads=[("dram", "qkT0", k) for k in range(8)],
                           writes=[(ph, "qT", r)])
                    for m in range(2):
                        sc.dma("sp", kz[r][m][m * 64:(m + 1) * 64, :], qkT0_d[12 + h, m * 64:(m + 1) * 64, :],
                               reads=[("dram", "qkT0", k) for k in range(8)], writes=[(ph, "kT", r, m)])
                    vfull = vaug[r][:, :, 0, :]
                    sc.dma("sp", vfull, v0_d[:, 512 + h * 128:512 + (h + 1) * 128].rearrange("(t p) d -> p t d", p=128),
                           reads=[("dram", "v0", k) for k in range(NT)], writes=[(ph, "vaug", r, 0)])
                    streams = []
                    for c in range(NCH):
                        items = []
                        for m in range(2):
                            rows = slice(m * 64, (m + 1) * 64)
                            items.append(dict(qT=qTc[r], kT=kz[r][m],
                                              pv=[((lambda kb, r=r: vaug[r][:, kb, 0, :]), 4 + 2 * m),
                                                  ((lambda kb: ones_bf), 5 + 2 * m)],
                                              units=lambda c: attention_units(c, True),
                                              toks=[(ph, "qT", r), (ph, "kT", r, m), (ph, "kzz", r, m)],
                                              vtoks=[(ph, "vaug", r, 0), "ones_bf"], r=r, m=m))
                        streams.append((h, c, items))

                    def fin_item_d(c, it):
                        m = it["m"]
                        nb, lb = 4 + 2 * m, 5 + 2 * m
                        rc = recip[0] if m == 0 else r2
                        dst = o1 if m == 0 else t2
                        sc.op("dve", lambda e: e.reciprocal(out=rc, in_=bank(lb)),
                              reads=[("ps", lb)], writes=[(ph, "rc", m)])
                        sc.op("dve", lambda e: e.tensor_tensor(out=dst, in0=bank(nb), in1=rc, op=ALU.mult),
                              reads=[("ps", nb), (ph, "rc", m)], writes=[(ph, "on", m)])

                    def fin_d(grp, c, items):
                        rr = items[0]["r"]
                        sc.op("dve", lambda e: e.scalar_tensor_tensor(out=o1, in0=t2, scalar=neglam, in1=o1,
                                                                      op0=ALU.mult, op1=ALU.add),
                              reads=[(ph, "on", 1), (ph, "on", 0), (ph, "neglam")], writes=[(ph, "on", 0)])
                        sc.op("act", lambda e: e.activation(out=sq, in_=o1, func=AF.Square),
                              reads=[(ph, "on", 0)], writes=[(ph, "sq")])
                        sc.op("pe", lambda e: e.matmul(bank(7), lhsT=ones_f, rhs=sq, start=True, stop=True),
                              reads=[(ph, "sq"), "ones_f"], writes=[("ps", 7)])
                        sc.op("act", lambda e: e.activation(out=r2, in_=bank(7), func=AF.Ln, bias=epsc[:, 2:3]),
                              reads=[("ps", 7)], writes=[(ph, "rc", 1)])
                        sc.op("act", lambda e: e.activation(out=r2, in_=r2, func=AF.Exp, scale=-0.5),
                              reads=[(ph, "rc", 1)], writes=[(ph, "rc", 1)])
                        sc.op("dve", lambda e: e.scalar_tensor_tensor(
                            out=ost[rr][:, c * 512:(c + 1) * 512], in0=o1, scalar=sgain, in1=r2,
                            op0=ALU.mult, op1=ALU.mult),
                            reads=[(ph, "on", 0), (ph, "rc", 1), (ph, "sgain")], writes=[(ph, "ost", rr)])
                    for (_g, _c, _items) in streams:
                        for _it in _items:
                            _it["fin_item"] = fin_item_d
                    run_attention(ph, streams, masks_sb, pbuf, 0.125, fin_d)
                    sc.dma("pool", oT_d[4 + h], ost[r], reads=[(ph, "ost", r)], writes=[("dram", "oT", 4 + h)])
                sc.barrier()

        def phase_O(ph, w_out_ap, xsrc_d, after_loads=None, ss_all=None):
            with ExitStack() as st:
                def sb(name, shape, dt):
                    return st.enter_context(nc.sbuf_tensor(ph + name, shape, dt))[:]
                sqj = sb("sqj", [128, D], BF16)
                wout = sb("wout", [128, 8, D], BF16)
                oTc = [sb(f"oTc{r}", [128, 8, 512], BF16) for r in range(2)]
                xt = [sb(f"xt{r}", [128, D], F32) for r in range(2)]
                for kc in range(8):
                    sc.dma("pool", wout[:, kc, :], w_out_ap[kc * 128:(kc + 1) * 128, :], writes=[(ph, "wout", kc)])
                if after_loads is not None:
                    after_loads()
                def load_oT(c):
                    sc.dma("sp", oTc[c % 2], oT_d[:, :, c * 512:(c + 1) * 512].rearrange("c p t -> p c t"),
                           reads=[("dram", "oT", k) for k in range(8)], writes=[(ph, "oTc", c % 2)])

                load_oT(0)
                for c in range(NCH):
                    cr = c % 2
                    if c + 1 < NCH:
                        load_oT(c + 1)
                    for tt in range(4):
                        i = c * 4 + tt
                        r = i % 2
                        sc.dma("sp", xt[r], xsrc_d[i * 128:(i + 1) * 128, :],
                               reads=[("dram", xsrc_d.tensor.name, i)], writes=[(ph, "xt", r)])
                        for kc in range(8):
                            for hf in range(2):
                                sc.op("pe", lambda e, kc=kc, hf=hf: e.matmul(
                                    bank(2 * r + hf), lhsT=oTc[cr][:, kc, tt * 128:(tt + 1) * 128],
                                    rhs=wout[:, kc, hf * 512:(hf + 1) * 512], start=(kc == 0), stop=(kc == 7)),
                                    reads=[(ph, "oTc", cr), (ph, "wout", kc)], writes=[("ps", 2 * r + hf)],
                                    obs=(kc == 7))
                        for hf in range(2):
                            sc.op("dve", lambda e, hf=hf: e.tensor_tensor(
                                out=xt[r][:, hf * 512:(hf + 1) * 512], in0=bank(2 * r + hf),
                                in1=xt[r][:, hf * 512:(hf + 1) * 512], op=ALU.add),
                                reads=[("ps", 2 * r + hf), (ph, "xt", r)], writes=[(ph, "xt", r)])
                        sc.dma("act", xres_d[i * 128:(i + 1) * 128, :], xt[r], reads=[(ph, "xt", r)],
                               writes=[("dram", "xres", i)])
                        if ss_all is not None:
                            sc.op("act", lambda e: e.activation(out=sqj, in_=xt[r], func=AF.Square,
                                                                accum_out=ss_all[:, i:i + 1]),
                                  reads=[(ph, "xt", r)], writes=[(ph, "sqj"), (ph, "ssall")])
                sc.barrier()

        def phase_F(ph, L, final, opre=None):
            with ExitStack() as st:
                def sb(name, shape, dt):
                    return st.enter_context(nc.sbuf_tensor(ph + name, shape, dt))[:]
                wup = sb("wup", [128, 8, 2 * FFN], BF16)
                wdn = sb("wdn", [128, 22, D], BF16)

                def load_weights():
                    for kc in range(8):
                        for hh in range(2):
                            sc.dma("pool", wup[:, kc, hh * FFN:(hh + 1) * FFN],
                                   ffn_w_up_d[L, kc * 128:(kc + 1) * 128, hh * FFN:(hh + 1) * FFN],
                                   writes=[(ph, "wup", kc, hh)])
                    for j in range(22):
                        sc.dma("pool", wdn[:, j, :], ffn_w_down_d[L, j * 128:(j + 1) * 128, :],
                               writes=[(ph, "wdn", j)])
                ss_all = sb("ssall", [128, NT], F32)
                rs_all = sb("rsall", [128, NT], F32)
                assert opre is not None
                phase_O(*opre, after_loads=load_weights, ss_all=ss_all)
                sc.op("act", lambda e: e.activation(out=rs_all, in_=ss_all, func=AF.Sqrt, bias=epsc[:, 0:1]),
                      writes=[(ph, "rsall")])
                sc.op("dve", lambda e: e.reciprocal(out=rs_all, in_=rs_all), reads=[(ph, "rsall")], writes=[(ph, "rsall")])
                gcol = sb("gcol", [128, 8], F32)
                cw = sb("cw", [128, 3, 44], F32)
                cb = sb("cb", [128, 44], F32)
                halo = sb("halo", [128, 2, 44, 2], F32)
                xt = [sb(f"xt{r}", [128, D], F32) for r in range(2)]
                xe = xt
                hn = [sb(f"hn{r}", [128, D], BF16) for r in range(2)]
                ss = [sb(f"ss{r}", [128, 1], F32) for r in range(2)]
                rs = [sb(f"rs{r}", [128, 1], F32) for r in range(2)]
                hTt = [sb(f"hTt{r}", [128, 8, 128], BF16) for r in range(2)]
                hTc = [sb(f"hTc{q}", [128, 8, 512], BF16) for q in range(2)]
                aT = sb("aT", [128, 22, 512], BF16)
                acc = [sb(f"acc{k}", [128, 512], F32) for k in range(4)]
                ht = [sb(f"ht{k}", [128, 4], F32) for k in range(4)]
                if final:
                    gfin = sb("gfin", [128, D], F32)
                    sc.dma("sp", gfin, mkap(final_norm_d, 0, [[0, 128], [1, D]]), writes=[(ph, "gfin")])
                load_gcol(ph, gcol, ffn_norm_d[L], math.sqrt(D))
                sc.dma("sp", cw, ffn_conv_w_d[L], writes=[(ph, "cw")])
                sc.dma("sp", cb, ffn_conv_b_d[L], writes=[(ph, "cb")])
                sc.op("pool", lambda e: e.memset(halo, 0.0),
                      writes=[(ph, "halo", hp, m) for m in range(44) for hp in range(2)])
                nchunk = int(os.environ.get("DBG_FCH", NCH))

                def emit_norm1(cn, tt):
                    i = cn * 4 + tt
                    r = i % 2
                    sc.dma("sp", xt[r], xres_d[i * 128:(i + 1) * 128, :], reads=[("dram", "xres", i)],
                           writes=[(ph, "xt", r)])
                    sc.op("act", lambda e: e.activation(out=hn[r], in_=xt[r], func=AF.Copy, scale=rs_all[:, i:i + 1]),
                          reads=[(ph, "xt", r), (ph, "rsall")], writes=[(ph, "hn", r)])

                def emit_norm2(cn, tt):
                    i = cn * 4 + tt
                    r = i % 2
                    norm_part2(ph, i, hn, hTt, gcol, 0)
                    sc.op("pool", lambda e: e.tensor_copy(out=hTc[cn % 2][:, :, tt * 128:(tt + 1) * 128], in_=hTt[r]),
                          reads=[(ph, "hT", r)], writes=[(ph, "hTc", cn % 2)])

                for tt in range(4):
                    emit_norm1(0, tt)
                    emit_norm2(0, tt)
                ybank = [7, 0]
                for c in range(nchunk):
                    hq = c % 2
                    for j in range(22):
                        pr = j % 2
                        pr3 = j % 3
                        if j in (1, 6, 11, 16) and c + 1 < nchunk:
                            emit_norm1(c + 1, (j - 1) // 5)
                        if j in (4, 9, 14, 19) and c + 1 < nchunk:
                            emit_norm2(c + 1, (j - 4) // 5)
                        for gv in range(2):
                            m = j + 22 * gv
                            b = 1 + 2 * pr3 + gv
                            for kc in range(8):
                                sc.op("pe", lambda e, kc=kc, m=m, b=b: e.matmul(
                                    bank(b), lhsT=wup[:, kc, m * 128:(m + 1) * 128], rhs=hTc[hq][:, kc, :],
                                    start=(kc == 0), stop=(kc == 7)),
                                    reads=[(ph, "hTc", hq), (ph, "wup", kc, gv)], writes=[("ps", b)], obs=(kc == 7))
                        hw_, hr_ = c % 2, (c + 1) % 2
                        for gv in range(2):
                            m = j + 22 * gv
                            b = 1 + 2 * pr3 + gv
                            k = 2 * pr + gv
                            sc.op("act", lambda e, k=k, b=b, m=m: e.activation(
                                out=acc[k], in_=bank(b), func=AF.Identity, scale=cw[:, 2, m:m + 1], bias=cb[:, m:m + 1]),
                                reads=[("ps", b), (ph, "cw"), (ph, "cb")], writes=[(ph, "acc", k)])
                            sc.op("act", lambda e, b=b, m=m: e.activation(out=halo[:, hw_, m, :], in_=ps[:, b, 510:512],
                                                                          func=AF.Copy),
                                  reads=[("ps", b)], writes=[(ph, "halo", hw_, m)])
                            sc.op("dve", lambda e, k=k, m=m, b=b: e.scalar_tensor_tensor(
                                out=acc[k][:, 1:512], in0=ps[:, b, 0:511], scalar=cw[:, 1, m:m + 1], in1=acc[k][:, 1:512],
                                op0=ALU.mult, op1=ALU.add),
                                reads=[("ps", b), (ph, "acc", k), (ph, "cw")], writes=[(ph, "acc", k)])
                            sc.op("dve", lambda e, k=k, m=m, b=b: e.scalar_tensor_tensor(
                                out=acc[k][:, 2:512], in0=ps[:, b, 0:510], scalar=cw[:, 0, m:m + 1], in1=acc[k][:, 2:512],
                                op0=ALU.mult, op1=ALU.add),
                                reads=[("ps", b), (ph, "acc", k), (ph, "cw")], writes=[(ph, "acc", k)])
                            sc.op("pool", lambda e, k=k, m=m: e.tensor_scalar(
                                out=ht[k][:, 0:2], in0=halo[:, hr_, m, 0:2], scalar1=cw[:, 0, m:m + 1], scalar2=None,
                                op0=ALU.mult),
                                reads=[(ph, "halo", hr_, m), (ph, "cw")], writes=[(ph, "ht", k)])
                            sc.op("pool", lambda e, k=k, m=m: e.tensor_scalar(
                                out=ht[k][:, 2:3], in0=halo[:, hr_, m, 1:2], scalar1=cw[:, 1, m:m + 1], scalar2=None,
                                op0=ALU.mult),
                                reads=[(ph, "halo", hr_, m), (ph, "cw")], writes=[(ph, "ht", k)])
                            sc.op("pool", lambda e, k=k: e.tensor_tensor(
                                out=acc[k][:, 0:2], in0=acc[k][:, 0:2], in1=ht[k][:, 0:2], op=ALU.add),
                                reads=[(ph, "ht", k), (ph, "acc", k)], writes=[(ph, "acc", k)])
                            sc.op("pool", lambda e, k=k: e.tensor_tensor(
                                out=acc[k][:, 0:1], in0=acc[k][:, 0:1], in1=ht[k][:, 2:3], op=ALU.add),
                                reads=[(ph, "ht", k), (ph, "acc", k)], writes=[(ph, "acc", k)])
                        kg = 2 * pr
                        kv = 2 * pr + 1
                        sc.op("act", lambda e, kg=kg: e.activation(out=acc[kg], in_=acc[kg], func=AF.Silu),
                              reads=[(ph, "acc", kg)], writes=[(ph, "acc", kg)])
                        sc.op("pool", lambda e, kg=kg, kv=kv, j=j: e.tensor_tensor(
                            out=aT[:, j, :], in0=acc[kg], in1=acc[kv], op=ALU.mult),
                            reads=[(ph, "acc", kg), (ph, "acc", kv)], writes=[(ph, "aT", j)])
                    for tt in range(4):
                        i = c * 4 + tt
                        r = i % 2
                        sc.dma("sp", xe[r], xres_d[i * 128:(i + 1) * 128, :], reads=[("dram", "xres", i)],
                               writes=[(ph, "xt", r)])
                        for j in range(22):
                            for hf in range(2):
                                sc.op("pe", lambda e, j=j, hf=hf: e.matmul(
                                    bank(ybank[hf]), lhsT=aT[:, j, tt * 128:(tt + 1) * 128],
                                    rhs=wdn[:, j, hf * 512:(hf + 1) * 512], start=(j == 0), stop=(j == 21)),
                                    reads=[(ph, "aT", j), (ph, "wdn", j)], writes=[("ps", ybank[hf])], obs=(j == 21))
                        for hf in range(2):
                            sc.op("dve", lambda e, hf=hf: e.tensor_tensor(
                                out=xe[r][:, hf * 512:(hf + 1) * 512], in0=bank(ybank[hf]),
                                in1=xe[r][:, hf * 512:(hf + 1) * 512], op=ALU.add),
                                reads=[("ps", ybank[hf]), (ph, "xt", r)], writes=[(ph, "xt", r)])
                        if not final:
                            sc.dma("sp", xres_d[i * 128:(i + 1) * 128, :], xe[r], reads=[(ph, "xt", r)],
                                   writes=[("dram", "xres", i)])
                        else:
                            sc.op("act", lambda e: e.activation(out=hn[r], in_=xe[r], func=AF.Square, accum_out=ss[r]),
                                  reads=[(ph, "xt", r)], writes=[(ph, "hn", r), (ph, "ss", r)])
                            sc.op("act", lambda e: e.activation(out=rs[r], in_=ss[r], func=AF.Sqrt, bias=epsc[:, 3:4],
                                                                scale=float(1.0 / D)),
                                  reads=[(ph, "ss", r)], writes=[(ph, "rs", r)])
                            sc.op("dve", lambda e: e.reciprocal(out=rs[r], in_=rs[r]),
                                  reads=[(ph, "rs", r)], writes=[(ph, "rs", r)])
                            sc.op("dve", lambda e: e.scalar_tensor_tensor(out=xe[r], in0=xe[r], scalar=rs[r], in1=gfin,
                                                                          op0=ALU.mult, op1=ALU.mult),
                                  reads=[(ph, "xt", r), (ph, "rs", r), (ph, "gfin")], writes=[(ph, "xt", r)])
                            sc.dma("sp", out_d[i * 128:(i + 1) * 128, :], xe[r], reads=[(ph, "xt", r)],
                                   writes=[("dram", "out", i)])
                sc.barrier()

        def phase_P1():
            ph = "P1"
            with ExitStack() as st:
                def sb(name, shape, dt):
                    return st.enter_context(nc.sbuf_tensor(ph + name, shape, dt))[:]
                win = sb("win", [128, 8, 416], BF16)
                wuq = sb("wuq", [128, 2, 1536], BF16)
                wukv = sb("wukv", [128, 2048], BF16)
                gcol = sb("gcol", [128, 8], F32)
                gq = sb("gq", [128, 384], F32)
                xt = [sb(f"xt{r}", [128, D], F32) for r in range(2)]
                hn = [sb(f"hn{r}", [128, D], BF16) for r in range(2)]
                ss = [sb(f"ss{r}", [128, 1], F32) for r in range(2)]
                rs = [sb(f"rs{r}", [128, 1], F32) for r in range(2)]
                hT = [sb(f"hT{r}", [128, 8, 128], BF16) for r in range(2)]
                junk = sb("junk", [128, 256], F32)
                ssl = [sb(f"ssl{r}", [128, 2], F32) for r in range(2)]
                rsl = [sb(f"rsl{r}", [128, 2], F32) for r in range(2)]
                lat = [sb(f"lat{r}", [128, 384], BF16) for r in range(2)]
                latT = [sb(f"latT{r}", [128, 3, 128], BF16) for r in range(2)]
                qs = [sb(f"qs{r}", [128, 16, 96], BF16) for r in range(2)]
                ks = [sb(f"ks{r}", [128, 16, 96], BF16) for r in range(2)]
                vs = [sb(f"vs{r}", [128, 16, 64], BF16) for r in range(2)]
                kpe = [sb(f"kpe{r}", [128, 1, 32], BF16) for r in range(2)]
                tmp = [sb(f"tmp{r}", [128, 4, 256], F32) for r in range(2)]
                qst = [sb(f"qst{r}", [128, 16, 512], BF16) for r in range(2)]
                kst = [sb(f"kst{r}", [128, 16, 512], BF16) for r in range(2)]
                for kc in range(8):
                    sc.dma("pool", win[:, kc, :], mla_w_in_d[kc * 128:(kc + 1) * 128, :], writes=[(ph, "win", kc)])
                for kc in range(2):
                    sc.dma("pool", wuq[:, kc, :], mla_w_uq_d[kc * 128:(kc + 1) * 128, :], writes=[(ph, "wuq")])
                sc.dma("pool", wukv, mla_w_ukv_d, writes=[(ph, "wukv")])
                load_gcol(ph, gcol, attn_norm_d[1], math.sqrt(D))
                sc.dma("sp", gq[:, 0:256], mkap(mla_q_norm_d, 0, [[0, 128], [1, 256]]), writes=[(ph, "gq")])
                sc.dma("sp", gq[:, 256:384], mkap(mla_kv_norm_d, 0, [[0, 128], [1, 128]]), writes=[(ph, "gq")])
                sc.op("dve", lambda e: e.tensor_scalar(out=gq[:, 0:256], in0=gq[:, 0:256], scalar1=16.0, scalar2=None,
                                                       op0=ALU.mult), reads=[(ph, "gq")], writes=[(ph, "gq")])
                sc.op("dve", lambda e: e.tensor_scalar(out=gq[:, 256:384], in0=gq[:, 256:384],
                                                       scalar1=float(math.sqrt(128.0)), scalar2=None, op0=ALU.mult),
                      reads=[(ph, "gq")], writes=[(ph, "gq")])
                kvb = [5, 6, 7, 1]
                ropq = [sb(f"ropq{r}", [128, 16, 32], F32) for r in range(2)]
                ropk = [sb(f"ropk{r}", [128, 1, 32], F32) for r in range(2)]
                ntile = int(os.environ.get("DBG_NT", NT))

                def stage_A1(i):
                    norm_part1(ph, i, xres_d, xt, hn, ss, rs, ldq="pool")

                def stage_A2(i):
                    norm_part2(ph, i, hn, hT, gcol, 0)

                def stage_B1(i):
                    r = i % 2
                    for kc in range(8):
                        sc.op("pe", lambda e, kc=kc: e.matmul(ps[:, 1, 0:416], lhsT=hT[r][:, kc, :], rhs=win[:, kc, :],
                                                             start=(kc == 0), stop=(kc == 7)),
                              reads=[(ph, "hT", r), (ph, "win", kc)], writes=[("ps", 1)], obs=(kc == 7))
                    sc.op("act", lambda e: e.activation(out=junk[:, 0:256], in_=ps[:, 1, 0:256], func=AF.Square,
                                                        accum_out=ssl[r][:, 0:1]),
                          reads=[("ps", 1)], writes=[(ph, "junk"), (ph, "ssl", r)])
                    sc.op("act", lambda e: e.activation(out=junk[:, 0:128], in_=ps[:, 1, 256:384], func=AF.Square,
                                                        accum_out=ssl[r][:, 1:2]),
                          reads=[("ps", 1)], writes=[(ph, "junk"), (ph, "ssl", r)])
                    sc.op("dve", lambda e: e.tensor_copy(out=ropk[r], in_=ps[:, 1:2, 384:416]),
                          reads=[("ps", 1)], writes=[(ph, "ropk", r)])
                    sc.op("act", lambda e: e.activation(out=rsl[r][:, 0:1], in_=ssl[r][:, 0:1], func=AF.Sqrt,
                                                        bias=epsc[:, 1:2]),
                          reads=[(ph, "ssl", r)], writes=[(ph, "rsl", r)])
                    sc.op("act", lambda e: e.activation(out=rsl[r][:, 1:2], in_=ssl[r][:, 1:2], func=AF.Sqrt,
                                                        bias=epsc[:, 2:3]),
                          reads=[(ph, "ssl", r)], writes=[(ph, "rsl", r)])
                    sc.op("dve", lambda e: e.reciprocal(out=rsl[r], in_=rsl[r]),
                          reads=[(ph, "rsl", r)], writes=[(ph, "rsl", r)])
                    sc.op("dve", lambda e: e.scalar_tensor_tensor(out=lat[r][:, 0:256], in0=ps[:, 1, 0:256],
                                                                  scalar=rsl[r][:, 0:1], in1=gq[:, 0:256],
                                                                  op0=ALU.mult, op1=ALU.mult),
                          reads=[("ps", 1), (ph, "rsl", r), (ph, "gq")], writes=[(ph, "lat", r)])
                    sc.op("dve", lambda e: e.scalar_tensor_tensor(out=lat[r][:, 256:384], in0=ps[:, 1, 256:384],
                                                                  scalar=rsl[r][:, 1:2], in1=gq[:, 256:384],
                                                                  op0=ALU.mult, op1=ALU.mult),
                          reads=[("ps", 1), (ph, "rsl", r), (ph, "gq")], writes=[(ph, "lat", r)])
                    rope_free(ph, ropk[r], kpe[r], 1, 16, cosM[:, i, :], sinM[:, i, :], tmp, r,
                              [(ph, "ropk", r)], [(ph, "kpe", r)])

                def stage_B2(i):
                    r = i % 2
                    tbv = bank_bf(0)
                    for k3 in range(3):
                        sc.op("pe", lambda e, k3=k3: e.transpose(out=tbv[:, k3 * 128:(k3 + 1) * 128],
                                                                 in_=lat[r][:, k3 * 128:(k3 + 1) * 128], identity=ident),
                              reads=[(ph, "lat", r), "ident"], writes=[("ps", 0)], obs=(k3 == 2))
                    sc.op("act", lambda e: e.activation(out=latT[r], in_=tbv[:, 0:384].rearrange("p (c t) -> p c t", c=3),
                                                        func=AF.Copy),
                          reads=[("ps", 0)], writes=[(ph, "latT", r)])
                    for n in range(3):
                        for kc in range(2):
                            sc.op("pe", lambda e, n=n, kc=kc: e.matmul(
                                bank(2 + n), lhsT=latT[r][:, kc, :], rhs=wuq[:, kc, n * 512:(n + 1) * 512],
                                start=(kc == 0), stop=(kc == 1)),
                                reads=[(ph, "latT", r), (ph, "wuq")], writes=[("ps", 2 + n)], obs=(kc == 1))
                    for n in range(4):
                        sc.op("pe", lambda e, n=n: e.matmul(bank(kvb[n]), lhsT=latT[r][:, 2, :],
                                                            rhs=wukv[:, n * 512:(n + 1) * 512], start=True, stop=True),
                              reads=[(ph, "latT", r), (ph, "wukv")], writes=[("ps", kvb[n])])
                    for (bq, h0, nh) in [(2, 0, 5), (3, 5, 5), (4, 10, 6)]:
                        qsrc_r = mkap(ps, 2 * 512 + 96 * h0 + 64, [list(ps.ap[0]), [96, nh], [1, 32]])
                        sc.op("dve", lambda e, qsrc_r=qsrc_r, h0=h0, nh=nh: e.tensor_copy(out=ropq[r][:, h0:h0 + nh, :],
                                                                                         in_=qsrc_r),
                              reads=[("ps", bq)], writes=[(ph, "ropq", r)])
                    qflat = qs[r].rearrange("p h d -> p (h d)")
                    for n in range(3):
                        sc.op("act", lambda e, n=n: e.activation(out=qflat[:, n * 512:(n + 1) * 512], in_=bank(2 + n),
                                                                 func=AF.Copy),
                              reads=[("ps", 2 + n)], writes=[(ph, "qs", r)])
                    for n in range(4):
                        kvv = bank(kvb[n]).rearrange("p (h d) -> p h d", d=128)
                        sc.op("act", lambda e, n=n, kvv=kvv: e.activation(out=ks[r][:, 4 * n:4 * n + 4, 0:64],
                                                                          in_=kvv[:, :, 0:64], func=AF.Copy),
                              reads=[("ps", kvb[n])], writes=[(ph, "ks", r)])
                        sc.op("dve", lambda e, n=n, kvv=kvv: e.tensor_copy(out=vs[r][:, 4 * n:4 * n + 4, :],
                                                                           in_=kvv[:, :, 64:128]),
                              reads=[("ps", kvb[n])], writes=[(ph, "vs", r)])
                    rope_free(ph, ropq[r], qs[r][:, :, 64:96], 16, 16, cosM[:, i, :], sinM[:, i, :], tmp, r,
                              [(ph, "ropq", r)], [(ph, "qs", r)])
                    sc.op("pool", lambda e: e.tensor_copy(
                        out=ks[r][:, :, 64:96], in_=mkap(kpe[r], 0, [list(kpe[r].ap[0]), [0, 16], [1, 32]])),
                        reads=[(ph, "kpe", r)], writes=[(ph, "ks", r)])
                    sc.dma("sp", v1_d[i * 128:(i + 1) * 128, :], vs[r].rearrange("p h d -> p (h d)"),
                           reads=[(ph, "vs", r)], writes=[("dram", "v1", i)])

                def stage_C(i):
                    r = i % 2
                    sr = (i // 4) % 2
                    cbanks = [0, 2, 3, 4]
                    rnd = 0
                    for (src, dstst, nm) in [(qs, qst, "qst"), (ks, kst, "kst")]:
                        for hb in range(2):
                            tbk = cbanks[rnd]
                            rnd += 1
                            tbv = bank_bf(tbk)
                            for hh in range(8):
                                h = hb * 8 + hh
                                sc.op("pe", lambda e, h=h, hh=hh, src=src, tbv=tbv: e.transpose(
                                    out=tbv[0:96, hh * 128:(hh + 1) * 128], in_=src[r][:, h, :], identity=ident),
                                    reads=[(ph, nm[0] + "s", r), "ident"], writes=[("ps", tbk)], obs=(hh == 7))
                            dstv = dstst[sr][0:96, hb * 8:(hb + 1) * 8, (i % 4) * 128:(i % 4 + 1) * 128]
                            srcv = tbv[0:96, :].rearrange("p (c t) -> p c t", c=8)
                            if hb == 0:
                                sc.op("act", lambda e, dstv=dstv, srcv=srcv: e.activation(out=dstv, in_=srcv, func=AF.Copy),
                                      reads=[("ps", tbk)], writes=[(ph, nm, sr)])
                            else:
                                sc.op("dve", lambda e, dstv=dstv, srcv=srcv: e.tensor_copy(out=dstv, in_=srcv),
                                      reads=[("ps", tbk)], writes=[(ph, nm, sr)])
                    if i % 4 == 3:
                        c0 = (i // 4) * 512
                        sc.dma("sp", qT1_d[:, :, c0:c0 + 512].rearrange("h p t -> p h t"), qst[sr][0:96],
                               reads=[(ph, "qst", sr)], writes=[("dram", "qT1", i // 4)])
                        sc.dma("sp", kT1_d[:, :, c0:c0 + 512].rearrange("h p t -> p h t"), kst[sr][0:96],
                               reads=[(ph, "kst", sr)], writes=[("dram", "kT1", i // 4)])

                stage_A1(0)
                stage_A2(0)
                if ntile > 1:
                    stage_A1(1)
                    stage_A2(1)
                if ntile > 2:
                    stage_A1(2)
                stage_B1(0)
                for i in range(ntile):
                    if i + 3 < ntile:
                        stage_A1(i + 3)
                    if i + 2 < ntile:
                        stage_A2(i + 2)
                    if i + 1 < ntile:
                        stage_B1(i + 1)
                    stage_B2(i)
                    if i >= 1:
                        stage_C(i - 1)
                stage_C(ntile - 1)
                sc.barrier()

        def phase_A1():
            ph = "A1"
            with ExitStack() as st:
                def sb(name, shape, dt):
                    return st.enter_context(nc.sbuf_tensor(ph + name, shape, dt))[:]
                masks_sb = sb("masks", [128, 24, 512], BF16)
                pbuf = [sb(f"pbuf{k}", [128, 512], BF16) for k in range(8)]
                qTh = [sb(f"qT{r}", [128, S], BF16) for r in range(2)]
                kTh = [sb(f"kT{r}", [128, S], BF16) for r in range(2)]
                vaug = [sb(f"va{r}", [128, NT, 128], BF16) for r in range(2)]
                ost = [sb(f"ost{r}", [128, S], BF16) for r in range(2)]
                recip = [sb(f"rc{r}", [128, 512], F32) for r in range(2)]
                sc.dma("sp", masks_sb, masks_d.rearrange("m p t -> p m t"), writes=[(ph, "masks")])
                for r in range(2):
                    sc.op("pool", lambda e, r=r: e.memset(vaug[r][:, :, 64:128], 1.0), writes=[(ph, "vaug1", r)])
                scale = 96.0 ** -0.5
                for h in range(int(os.environ.get("DBG_NH", 16))):
                    r = h % 2
                    pr = h // 2
                    orr = pr % 2
                    sc.dma("sp", qTh[r][0:96, :], qT1_d[h], reads=[("dram", "qT1", k) for k in range(8)],
                           writes=[(ph, "qT", r)])
                    sc.dma("sp", kTh[r][0:96, :], kT1_d[h], reads=[("dram", "kT1", k) for k in range(8)],
                           writes=[(ph, "kT", r)])
                    sc.dma("sp", vaug[r][:, :, 0:64], v1_d[:, h * 64:(h + 1) * 64].rearrange("(t p) d -> p t d", p=128),
                           reads=[("dram", "v1", k) for k in range(NT)], writes=[(ph, "vaug", r)])
                    streams = []
                    for c in range(int(os.environ.get("DBG_NC", NCH))):
                        ab = 6 + (c % 2)
                        it = dict(qT=qTh[r][0:96, :], kT=kTh[r][0:96, :],
                                  pv=[((lambda kb, r=r: vaug[r][:, kb, :]), ab)],
                                  units=lambda c: attention_units(c, True),
                                  toks=[(ph, "qT", r), (ph, "kT", r)],
                                  vtoks=[(ph, "vaug", r), (ph, "vaug1", r)], ab=ab, e=h % 2, r=orr)
                        streams.append((h, c, [it]))

                    def fin_m(grp, c, items):
                        it = items[0]
                        ab = it["ab"]
                        e_ = it["e"]
                        rr = it["r"]
                        k = ab - 6
                        norm64_pieces(ph, ab, k, recip,
                                      lambda cs: ost[rr][e_ * 64:(e_ + 1) * 64, c * 512 + cs.start:c * 512 + cs.stop])
                    run_attention(ph, streams, masks_sb, pbuf, scale, fin_m, sbanks=(0, 1, 2, 3, 4, 5))
                    if h % 2 == 1:
                        sc.dma("pool", oT_d[pr], ost[orr], reads=[(ph, "ost", orr)] + [(ph, "ostw", k_, j_) for k_ in range(2) for j_ in range(2)],
                               writes=[("dram", "oT", pr)])
                sc.barrier()

        phases = [
            ("P0", phase_P0),
            ("A0", phase_A0),
            ("F0", lambda: phase_F("F0", 0, False, opre=("O0", hyb_w_out_d, x_d))),
            ("P1", phase_P1),
            ("A1", phase_A1),
            ("F1", lambda: phase_F("F1", 1, True, opre=("O1", mla_w_out_d, xres_d))),
        ]
        for name, fn in phases:
            if stop_after == "init":
                break
            if only is not None and name not in only:
                continue
            fn()
            if stop_after == name:
                break
        sc.barrier()
    return nc


_MASKS = None


def _make_masks():
    global _MASKS
    if _MASKS is not None:
        return _MASKS
    m = np.zeros((24, 128, 512), dtype=np.float32)
    j = np.arange(128)[:, None]
    i = np.arange(512)[None, :]
    for idx in range(20):
        off = idx - 16
        delta = (i - j) - 128 * off
        cnt = ((delta >= 0) & (delta <= 128)).astype(np.float32)
        cnt += ((delta >= 0) & (delta <= 512) & (delta % 4 == 0)).astype(np.float32)
        cnt += ((delta >= 0) & (delta <= 2048) & (delta % 16 == 0)).astype(np.float32)
        m[idx] = cnt
    for jj in range(4):
        delta = (i - j) - 128 * jj
        m[20 + jj] = (delta >= 0).astype(np.float32)
    _MASKS = m.astype(ml_dtypes.bfloat16)
    return _MASKS


def make_in_maps(inputs):
    f = lambda a: np.ascontiguousarray(np.asarray(a, dtype=np.float32))
    shared = {
        "attn_norm": f(np.asarray(inputs["attn_norm"]).reshape(2, 8, 128).transpose(0, 2, 1)),
        "ffn_norm": f(np.asarray(inputs["ffn_norm"]).reshape(2, 8, 128).transpose(0, 2, 1)),
        "final_norm": f(inputs["final_norm"]),
        "hyb_w_in": f(inputs["hyb_w_in"][0]),
        "hyb_w_out": f(inputs["hyb_w_out"][0]),
        "lq1": f(inputs["diff_lambda_q1"][0]),
        "lk1": f(inputs["diff_lambda_k1"][0]),
        "lq2": f(inputs["diff_lambda_q2"][0]),
        "lk2": f(inputs["diff_lambda_k2"][0]),
        "subln": f(inputs["diff_subln"][0]),
        "mla_w_in": f(inputs["mla_w_in"][0]),
        "mla_q_norm": f(inputs["mla_q_norm"][0]),
        "mla_w_uq": f(inputs["mla_w_uq"][0]),
        "mla_kv_norm": f(inputs["mla_kv_norm"][0]),
        "mla_w_ukv": f(inputs["mla_w_ukv"][0]),
        "mla_w_out": f(inputs["mla_w_out"][0]),
        "ffn_w_up": f(inputs["ffn_w_up"]),
        "ffn_conv_w": f(np.asarray(inputs["ffn_conv_w"]).reshape(2, 3, 44, 128).transpose(0, 3, 1, 2)),
        "ffn_conv_b": f(np.asarray(inputs["ffn_conv_b"]).reshape(2, 44, 128).transpose(0, 2, 1)),
        "ffn_w_down": f(inputs["ffn_w_down"]),
        "masks": _make_masks(),
        "ident": np.eye(128, dtype=np.float32).astype(ml_dtypes.bfloat16),
    }
    x = np.asarray(inputs["x"], dtype=np.float32)
    pos = np.asarray(inputs["positions"], dtype=np.int32)
    maps = []
    for b in range(8):
        m = dict(shared)
        m["x"] = np.ascontiguousarray(x[b])
        m["pos"] = np.ascontiguousarray(pos[b].reshape(NT, 128).T)
        maps.append(m)
    return maps


def kernel(**inputs):
    nc = build_program()
    in_maps = make_in_maps(inputs)
    res = run_bass_kernel_spmd(nc, in_maps, core_ids=list(range(8)))
    out = np.stack([np.asarray(r["out"], dtype=np.float32) for r in res.results], axis=0)
    return out
```

```python
import math
import os
import numpy as np
import ml_dtypes
import concourse.bass as bass
import concourse.mybir as mybir
from concourse.bass_utils import run_bass_kernel_spmd

F32 = mybir.dt.float32
BF16 = mybir.dt.bfloat16
I32 = mybir.dt.int32
AF = mybir.ActivationFunctionType
ALU = mybir.AluOpType
AX = mybir.AxisListType

S = 4096
D = 1024
NT = 32
NCH = 8
FFN = 2816
EPS = 1e-6
PI = math.pi


class Sched:
    def __init__(self, nc, stack):
        self.nc = nc
        self.stack = stack
        self.eng = {"pe": nc.tensor, "act": nc.scalar, "dve": nc.vector, "pool": nc.gpsimd, "sp": nc.sync}
        self.sem = {}
        self.cnt = {}
        for e in self.eng:
            self._new_sem(e)
        self.nslots = 12
        self.dslots = {}
        for e in ("sp", "pool", "act"):
            self.dslots[e] = [[stack.enter_context(nc.semaphore(f"d_{e}_{i}")), 0] for i in range(self.nslots)]
        self.dnext = {"sp": 0, "pool": 0, "act": 0}
        self.sig = []
        self.seen = {e: {} for e in self.eng}
        self.last_writer = {}
        self.readers = {}
        self.ps_readers = {}
        self.last_op = {e: None for e in self.eng}
        self.dma_ops = []
        self.nsem = 0
        self.nwaits = 0

    def _new_sem(self, e):
        self.nsem = getattr(self, "nsem", 0) + 1
        self.sem[e] = self.stack.enter_context(self.nc.semaphore(f"s_{e}_{self.nsem}"))
        self.cnt[e] = 0

    def _deps(self, reads, writes, e=None):
        deps = set()
        for t in reads:
            w = self.last_writer.get(t)
            if w is not None:
                deps.add(w)
            if isinstance(t, tuple) and t[0] == "ps":
                for (rid, re_) in self.ps_readers.get(t, []):
                    if re_ != e:
                        deps.add(rid)
        for t in writes:
            w = self.last_writer.get(t)
            if w is not None:
                deps.add(w)
            r = self.readers.get(t)
            if r:
                deps.update(r)
        return deps

    def _commit(self, oid, reads, writes, e=None):
        for t in reads:
            self.readers.setdefault(t, []).append(oid)
            if isinstance(t, tuple) and t[0] == "ps":
                self.ps_readers.setdefault(t, []).append((oid, e))
        for t in writes:
            self.last_writer[t] = oid
            self.readers[t] = []
            if isinstance(t, tuple) and t[0] == "ps":
                self.ps_readers[t] = []

    def _emit_waits(self, e, deps, attach=False):
        engobj = self.eng[e]
        need = {}
        for d in deps:
            s = self.sig[d]
            if s is None:
                continue
            sem, val, seng = s
            if seng == "pe" and e == "pe":
                continue
            k = id(sem)
            if self.seen[e].get(k, 0) >= val:
                continue
            if k not in need or need[k][1] < val:
                need[k] = (sem, val)
        items = list(need.items())
        last = None
        if attach and items:
            last = items.pop()
        for k, (sem, val) in items:
            engobj.wait_ge(sem, val)
            self.seen[e][k] = val
            self.nwaits += 1
        if last is not None:
            self.seen[e][last[0]] = last[1][1]
            return last[1]
        return None

    def op(self, e, fn, reads=(), writes=(), obs=True, after=()):
        deps = self._deps(reads, writes, e)
        deps.update(after)
        last = self._emit_waits(e, deps, attach=True)
        ins = fn(self.eng[e])
        if last is not None:
            ins._wait_ge(last[0], last[1])
        oid = len(self.sig)
        if obs:
            if self.cnt[e] >= 30000:
                self._new_sem(e)
            self.cnt[e] += 1
            ins.then_inc(self.sem[e], 1)
            self.sig.append((self.sem[e], self.cnt[e], e))
        else:
            self.sig.append(None)
        self._commit(oid, reads, writes, e)
        if obs:
            self.last_op[e] = oid
        return oid

    def dma(self, e, out, in_, reads=(), writes=(), after=()):
        deps = self._deps(reads, writes)
        deps.update(after)
        self._emit_waits(e, deps)
        si = self.dnext[e]
        self.dnext[e] = (si + 1) % self.nslots
        slot = self.dslots[e][si]
        sem, uses = slot
        engobj = self.eng[e]
        k = id(sem)
        if uses > 0 and self.seen[e].get(k, 0) < 16 * uses:
            engobj.wait_ge(sem, 16 * uses)
            self.seen[e][k] = 16 * uses
        if uses >= 1800:
            raise RuntimeError("dma sem overflow")
        ins = engobj.dma_start(out=out, in_=in_)
        slot[1] = uses + 1
        ins.then_inc(sem, 16)
        oid = len(self.sig)
        self.sig.append((sem, 16 * (uses + 1), "dma_" + e))
        self._commit(oid, reads, writes)
        self.dma_ops.append(oid)
        return oid

    def barrier(self):
        deps = set(self.dma_ops)
        for e in self.eng:
            if self.last_op[e] is not None:
                deps.add(self.last_op[e])
        self.dma_ops = []
        self._emit_waits_barrier("sp", deps)
        ins = self.eng["sp"].nop()
        self.cnt["sp"] += 1
        ins.then_inc(self.sem["sp"], 1)
        oid = len(self.sig)
        self.sig.append((self.sem["sp"], self.cnt["sp"], "sp"))
        self.last_op["sp"] = oid
        for e in self.eng:
            if e == "sp":
                continue
            self.eng[e].wait_ge(self.sem["sp"], self.cnt["sp"])
            self.seen[e][id(self.sem["sp"])] = self.cnt["sp"]
        self.last_writer = {}
        self.readers = {}
        self.ps_readers = {}

    def _emit_waits_barrier(self, e, deps):
        engobj = self.eng[e]
        need = {}
        for d in deps:
            s = self.sig[d]
            if s is None:
                continue
            sem, val, seng = s
            k = id(sem)
            if self.seen[e].get(k, 0) >= val:
                continue
            if k not in need or need[k][1] < val:
                need[k] = (sem, val)
        for k, (sem, val) in need.items():
            engobj.wait_ge(sem, val)
            self.seen[e][k] = val


def mkap(base, off, dims):
    return bass.AP(tensor=base.tensor, offset=base.offset + off, ap=[list(d) for d in dims])


def build_program(debug=False, stop_after=None, only=None):
    nc = bass.Bass("TRN2", target_bir_lowering=False)
    from contextlib import ExitStack

    def din(name, shape, dt=F32):
        return nc.dram_tensor(name, list(shape), dt, kind="ExternalInput").ap()

    def dscr(name, shape, dt):
        skind = "ExternalOutput" if (debug and name in debug) else "Internal"
        return nc.dram_tensor(name, list(shape), dt, kind=skind).ap()

    x_d = din("x", [S, D])
    pos_d = din("pos", [128, NT], I32)
    attn_norm_d = din("attn_norm", [2, 128, 8])
    ffn_norm_d = din("ffn_norm", [2, 128, 8])
    final_norm_d = din("final_norm", [D])
    hyb_w_in_d = din("hyb_w_in", [D, 3072])
    hyb_w_out_d = din("hyb_w_out", [D, D])
    lq1_d = din("lq1", [64])
    lk1_d = din("lk1", [64])
    lq2_d = din("lq2", [64])
    lk2_d = din("lk2", [64])
    subln_d = din("subln", [128])
    mla_w_in_d = din("mla_w_in", [D, 416])
    mla_q_norm_d = din("mla_q_norm", [256])
    mla_w_uq_d = din("mla_w_uq", [256, 1536])
    mla_kv_norm_d = din("mla_kv_norm", [128])
    mla_w_ukv_d = din("mla_w_ukv", [128, 2048])
    mla_w_out_d = din("mla_w_out", [D, D])
    ffn_w_up_d = din("ffn_w_up", [2, D, 2 * FFN])
    ffn_conv_w_d = din("ffn_conv_w", [2, 128, 3, 44])
    ffn_conv_b_d = din("ffn_conv_b", [2, 128, 44])
    ffn_w_down_d = din("ffn_w_down", [2, FFN, D])
    masks_d = din("masks", [24, 128, 512], BF16)
    ident_d = din("ident", [128, 128], BF16)

    out_d = nc.dram_tensor("out", [S, D], F32, kind="ExternalOutput").ap()

    xres_d = dscr("xres", [S, D], F32)
    qkT0_d = dscr("qkT0", [16, 128, S], BF16)
    v0_d = dscr("v0", [S, 1024], BF16)
    oT_d = dscr("oT", [8, 128, S], BF16)
    qT1_d = dscr("qT1", [16, 96, S], BF16)
    kT1_d = dscr("kT1", [16, 96, S], BF16)
    v1_d = dscr("v1", [S, 1024], BF16)

    inv_freq_p = [500000.0 ** (-(2 * i) / 16.0) for i in range(8)]
    inv_freq_m = [10000.0 ** (-(2 * i) / 32.0) for i in range(16)]

    with ExitStack() as top:
        sc = Sched(nc, top)
        ps_t = top.enter_context(nc.psum_tensor("ps", [128, 8, 512], F32))
        ps = ps_t[:]

        def bank(b):
            return ps[:, b, :]

        def bank_bf(b, n=1):
            return ps[:, b, :].bitcast(BF16)

        tmpst = ExitStack()
        ident = top.enter_context(nc.sbuf_tensor("c_ident", [128, 128], BF16))[:]
        cosP = top.enter_context(nc.sbuf_tensor("c_cosP", [128, NT, 8], F32))[:]
        sinP = top.enter_context(nc.sbuf_tensor("c_sinP", [128, NT, 8], F32))[:]
        cosM = top.enter_context(nc.sbuf_tensor("c_cosM", [128, NT, 16], F32))[:]
        sinM = top.enter_context(nc.sbuf_tensor("c_sinM", [128, NT, 16], F32))[:]
        ones_bf = top.enter_context(nc.sbuf_tensor("c_ones_bf", [128, 128], BF16))[:]
        ones_f = top.enter_context(nc.sbuf_tensor("c_ones_f", [128, 128], F32))[:]
        negpi = top.enter_context(nc.sbuf_tensor("c_negpi", [128, 1], F32))[:]

        epsc = top.enter_context(nc.sbuf_tensor("c_epsc", [128, 4], F32))[:]
        for k_, v_ in enumerate([D * EPS, 256 * EPS, 128 * EPS, EPS]):
            sc.op("dve", lambda e, k_=k_, v_=v_: e.memset(epsc[:, k_:k_ + 1], float(v_)), writes=["epsc"])
        sc.dma("sp", ident, ident_d, writes=["ident"])
        sc.op("dve", lambda e: e.memset(ones_bf, 1.0), writes=["ones_bf"])
        sc.op("dve", lambda e: e.memset(ones_f, 1.0), writes=["ones_f"])
        sc.op("dve", lambda e: e.memset(negpi, -PI), writes=["negpi"])

        posi = tmpst.enter_context(nc.sbuf_tensor("c_posi", [128, NT], I32))[:]
        posf = tmpst.enter_context(nc.sbuf_tensor("c_posf", [128, NT], F32))[:]
        angt = tmpst.enter_context(nc.sbuf_tensor("c_angt", [128, NT, 16], F32))[:]
        ang2 = tmpst.enter_context(nc.sbuf_tensor("c_ang2", [128, NT, 16], F32))[:]
        rr_ki = tmpst.enter_context(nc.sbuf_tensor("c_rr_ki", [128, NT, 16], I32))[:]
        rr_kf = tmpst.enter_context(nc.sbuf_tensor("c_rr_kf", [128, NT, 16], F32))[:]
        rr_m = tmpst.enter_context(nc.sbuf_tensor("c_rr_m", [128, NT, 16], F32))[:]
        sc.dma("sp", posi, pos_d, writes=["posi"])
        sc.op("dve", lambda e: e.tensor_copy(out=posf, in_=posi), reads=["posi"], writes=["posf"])
        C1 = 6.28125
        C2 = 2 * PI - C1

        def reduced_sin(dst, nf, add):
            x = ang2[:, :, 0:nf]
            ki = rr_ki[:, :, 0:nf]
            kf = rr_kf[:, :, 0:nf]
            m = rr_m[:, :, 0:nf]
            sc.op("dve", lambda e: e.tensor_scalar(out=x, in0=angt[:, :, 0:nf], scalar1=float(add), scalar2=None,
                                                   op0=ALU.add), reads=[("angt", i_) for i_ in range(nf)], writes=["ang2"])
            sc.op("dve", lambda e: e.tensor_scalar(out=kf, in0=x, scalar1=float(1.0 / (2 * PI)), scalar2=None,
                                                   op0=ALU.mult), reads=["ang2"], writes=["rr_kf"])
            sc.op("dve", lambda e: e.tensor_copy(out=ki, in_=kf), reads=["rr_kf"], writes=["rr_ki"])
            sc.op("dve", lambda e: e.tensor_copy(out=kf, in_=ki), reads=["rr_ki"], writes=["rr_kf"])
            sc.op("dve", lambda e: e.scalar_tensor_tensor(out=x, in0=kf, scalar=float(-C1), in1=x,
                                                          op0=ALU.mult, op1=ALU.add),
                  reads=["rr_kf", "ang2"], writes=["ang2"])
            sc.op("dve", lambda e: e.scalar_tensor_tensor(out=x, in0=kf, scalar=float(-C2), in1=x,
                                                          op0=ALU.mult, op1=ALU.add),
                  reads=["rr_kf", "ang2"], writes=["ang2"])
            sc.op("dve", lambda e: e.tensor_scalar(out=m, in0=x, scalar1=float(PI), scalar2=float(-2 * PI),
                                                   op0=ALU.is_gt, op1=ALU.mult), reads=["ang2"], writes=["rr_m"])
            sc.op("dve", lambda e: e.tensor_tensor(out=x, in0=x, in1=m, op=ALU.add),
                  reads=["ang2", "rr_m"], writes=["ang2"])
            sc.op("dve", lambda e: e.tensor_scalar(out=m, in0=x, scalar1=float(-PI), scalar2=float(2 * PI),
                                                   op0=ALU.is_lt, op1=ALU.mult), reads=["ang2"], writes=["rr_m"])
            sc.op("dve", lambda e: e.tensor_tensor(out=x, in0=x, in1=m, op=ALU.add),
                  reads=["ang2", "rr_m"], writes=["ang2"])
            sc.op("act", lambda e: e.activation(out=dst, in_=x, func=AF.Sin), reads=["ang2"], writes=["ropetab"])

        def rope_tables(cos_t, sin_t, freqs, name):
            nf = len(freqs)
            for i, f in enumerate(freqs):
                sc.op("dve", lambda e, i=i, f=f: e.tensor_scalar(
                    out=angt[:, :, i], in0=posf, scalar1=float(f), scalar2=None, op0=ALU.mult),
                    reads=["posf", "ropetab", "ang2"], writes=[("angt", i)])
            reduced_sin(sin_t, nf, 0.0)
            reduced_sin(cos_t, nf, 0.5 * PI)

        rope_tables(cosP, sinP, inv_freq_p, "ropeP")
        rope_tables(cosM, sinM, inv_freq_m, "ropeM")
        sc.barrier()
        tmpst.close()

        def norm_part1(ph, i, src_d, xt, hn, ss, rs, ldq="sp"):
            r = i % 2
            sc.dma(ldq, xt[r], src_d[i * 128:(i + 1) * 128, :], reads=[("dram", src_d.tensor.name, i)],
                   writes=[(ph, "xt", r)])
            sc.op("act", lambda e: e.activation(out=hn[r], in_=xt[r], func=AF.Square, accum_out=ss[r]),
                  reads=[(ph, "xt", r)], writes=[(ph, "hn", r), (ph, "ss", r)])
            sc.op("act", lambda e: e.activation(out=rs[r], in_=ss[r], func=AF.Sqrt, bias=epsc[:, 0:1]),
                  reads=[(ph, "ss", r)], writes=[(ph, "rs", r)])
            sc.op("dve", lambda e: e.reciprocal(out=rs[r], in_=rs[r]),
                  reads=[(ph, "rs", r)], writes=[(ph, "rs", r)])
            sc.op("act", lambda e: e.activation(out=hn[r], in_=xt[r], func=AF.Copy, scale=rs[r]),
                  reads=[(ph, "xt", r), (ph, "rs", r)], writes=[(ph, "hn", r)])

        def norm_part2(ph, i, hn, hT, gcol, tb):
            r = i % 2
            tbv = bank_bf(tb)
            for kc in range(8):
                sc.op("pe", lambda e, kc=kc: e.transpose(out=tbv[:, kc * 128:(kc + 1) * 128],
                                                         in_=hn[r][:, kc * 128:(kc + 1) * 128], identity=ident),
                      reads=[(ph, "hn", r), "ident"], writes=[("ps", tb)], obs=(kc == 7))
            sc.op("dve", lambda e: e.tensor_tensor(
                out=hT[r], in0=tbv.rearrange("p (c t) -> p c t", c=8),
                in1=mkap(gcol, 0, [[8, 128], [1, 8], [0, 128]]), op=ALU.mult),
                reads=[("ps", tb), (ph, "gcol")], writes=[(ph, "hT", r)])

        def norm_tile(ph, i, src_d, xt, hn, ss, rs, hT, gcol, tb, ldq="sp"):
            norm_part1(ph, i, src_d, xt, hn, ss, rs, ldq)
            norm_part2(ph, i, hn, hT, gcol, tb)

        def load_gcol(ph, gcol, norm_row_ap, mult):
            sc.dma("sp", gcol, norm_row_ap, writes=[(ph, "gcol")])
            sc.op("dve", lambda e: e.tensor_scalar(out=gcol, in0=gcol, scalar1=float(mult), scalar2=None,
                                                   op0=ALU.mult),
                  reads=[(ph, "gcol")], writes=[(ph, "gcol")])

        def rope_free(ph, src, dst, nslot, half, cos_ap, sin_ap, tmp, r, src_tok, dst_tok):
            cb = mkap(cos_ap, 0, [list(cos_ap.ap[0]), [0, nslot], [1, half]])
            sb = mkap(sin_ap, 0, [list(sin_ap.ap[0]), [0, nslot], [1, half]])
            x1 = src[:, :, 0:half]
            x2 = src[:, :, half:2 * half]
            n = nslot * half
            t = [tmp[r][:, k, 0:n].rearrange("p (s h) -> p s h", s=nslot) for k in range(4)]
            tt = [(ph, "ropetmp", r, k) for k in range(4)]
            sc.op("dve", lambda e: e.tensor_tensor(out=t[0], in0=x1, in1=cb, op=ALU.mult),
                  reads=src_tok + ["ropeP", "ropeM"], writes=[tt[0]])
            sc.op("dve", lambda e: e.tensor_tensor(out=t[1], in0=x2, in1=sb, op=ALU.mult),
                  reads=src_tok, writes=[tt[1]])
            sc.op("dve", lambda e: e.tensor_tensor(out=t[2], in0=x2, in1=cb, op=ALU.mult),
                  reads=src_tok, writes=[tt[2]])
            sc.op("dve", lambda e: e.tensor_tensor(out=t[3], in0=x1, in1=sb, op=ALU.mult),
                  reads=src_tok, writes=[tt[3]])
            sc.op("dve", lambda e: e.tensor_tensor(out=dst[:, :, 0:half], in0=t[0], in1=t[1], op=ALU.subtract),
                  reads=[tt[0], tt[1]], writes=dst_tok)
            sc.op("dve", lambda e: e.tensor_tensor(out=dst[:, :, half:2 * half], in0=t[2], in1=t[3], op=ALU.add),
                  reads=[tt[2], tt[3]], writes=dst_tok)

        def phase_P0():
            ph = "P0"
            with ExitStack() as st:
                def sb(name, shape, dt):
                    return st.enter_context(nc.sbuf_tensor(ph + name, shape, dt))[:]
                win = sb("win", [128, 8, 3072], BF16)
                gcol = sb("gcol", [128, 8], F32)
                xt = [sb(f"xt{r}", [128, D], F32) for r in range(2)]
                hn = [sb(f"hn{r}", [128, D], BF16) for r in range(2)]
                ss = [sb(f"ss{r}", [128, 1], F32) for r in range(2)]
                rs = [sb(f"rs{r}", [128, 1], F32) for r in range(2)]
                hT = [sb(f"hT{r}", [128, 8, 128], BF16) for r in range(2)]
                qk = [sb(f"qk{r}", [128, 2048], BF16) for r in range(2)]
                vs = [sb(f"vs{r}", [128, 1024], BF16) for r in range(2)]
                tmp = [sb(f"tmp{r}", [128, 4, 256], F32) for r in range(2)]
                stage = [sb(f"stage{r}", [128, 16, 512], BF16) for r in range(2)]
                ropein = [sb(f"ropein{r}", [128, 32, 16], F32) for r in range(2)]
                for kc in range(8):
                    sc.dma("pool", win[:, kc, :], hyb_w_in_d[kc * 128:(kc + 1) * 128, :], writes=[(ph, "win", kc)])
                load_gcol(ph, gcol, attn_norm_d[0], math.sqrt(D))
                ntile = int(os.environ.get("DBG_NT", NT))

                def stage_A1(i):
                    norm_part1(ph, i, x_d, xt, hn, ss, rs, ldq="pool")

                def stage_A2(i):
                    norm_part2(ph, i, hn, hT, gcol, 0)

                def stage_MM(i):
                    r = i % 2
                    for kc in range(8):
                        for n in range(6):
                            sc.op("pe", lambda e, kc=kc, n=n: e.matmul(
                                bank(2 + n), lhsT=hT[r][:, kc, :], rhs=win[:, kc, n * 512:(n + 1) * 512],
                                start=(kc == 0), stop=(kc == 7)),
                                reads=[(ph, "hT", r), (ph, "win", kc)], writes=[("ps", 2 + n)], obs=(kc == 7))

                def stage_B(i):
                    r = i % 2
                    rbanks = [(2, 0), (3, 512), (5, 1024), (6, 1536)]
                    for bi, (b0, off) in enumerate(rbanks):
                        src = bank(b0).rearrange("p (s d) -> p s d", d=64)[:, :, 0:16]
                        sc.op("dve", lambda e, bi=bi, src=src: e.tensor_copy(out=ropein[r][:, bi * 8:(bi + 1) * 8, :], in_=src),
                              reads=[("ps", b0)], writes=[(ph, "ropein", r)])
                    for n, (dst, off) in enumerate([(qk, 0), (qk, 512), (vs, 0), (qk, 1024), (qk, 1536), (vs, 512)]):
                        tok = (ph, "qk", r) if dst is qk else (ph, "vs", r)
                        if n % 2 == 0:
                            sc.op("act", lambda e, n=n, dst=dst, off=off: e.activation(
                                out=dst[r][:, off:off + 512], in_=bank(2 + n), func=AF.Copy),
                                reads=[("ps", 2 + n)], writes=[tok])
                        else:
                            sc.op("dve", lambda e, n=n, dst=dst, off=off: e.tensor_copy(
                                out=dst[r][:, off:off + 512], in_=bank(2 + n)),
                                reads=[("ps", 2 + n)], writes=[tok])
                    dst = qk[r].rearrange("p (s d) -> p s d", d=64)[:, :, 0:16]
                    rope_free(ph, ropein[r], dst, 32, 8, cosP[:, i, :], sinP[:, i, :], tmp, r,
                              [(ph, "ropein", r)], [(ph, "qk", r)])
                    sc.dma("sp", v0_d[i * 128:(i + 1) * 128, :], vs[r], reads=[(ph, "vs", r)],
                           writes=[("dram", "v0", i)])

                def stage_C(i):
                    r = i % 2
                    sr = (i // 4) % 2
                    for hb in (1, 0):
                        tbv = bank_bf(hb)
                        for c in range(8):
                            ch = hb * 8 + c
                            sc.op("pe", lambda e, c=c, ch=ch, tbv=tbv: e.transpose(
                                out=tbv[:, c * 128:(c + 1) * 128], in_=qk[r][:, ch * 128:(ch + 1) * 128],
                                identity=ident),
                                reads=[(ph, "qk", r), "ident"], writes=[("ps", hb)], obs=(c == 7))
                        dstv = stage[sr][:, hb * 8:(hb + 1) * 8, (i % 4) * 128:(i % 4 + 1) * 128]
                        srcv = tbv.rearrange("p (c t) -> p c t", c=8)
                        if hb == 0:
                            sc.op("act", lambda e, dstv=dstv, srcv=srcv: e.activation(out=dstv, in_=srcv, func=AF.Copy),
                                  reads=[("ps", hb)], writes=[(ph, "stage", sr)])
                        else:
                            sc.op("dve", lambda e, dstv=dstv, srcv=srcv: e.tensor_copy(out=dstv, in_=srcv),
                                  reads=[("ps", hb)], writes=[(ph, "stage", sr)])
                    if i % 4 == 3:
                        c0 = (i // 4) * 512
                        sc.dma("sp", qkT0_d[:, :, c0:c0 + 512].rearrange("c p t -> p c t"), stage[sr],
                               reads=[(ph, "stage", sr)], writes=[("dram", "qkT0", i // 4)])

                stage_A1(0)
                stage_A2(0)
                if ntile > 1:
                    stage_A1(1)
                for i in range(ntile):
                    if i + 2 < ntile:
                        stage_A1(i + 2)
                    stage_MM(i)
                    if i + 1 < ntile:
                        stage_A2(i + 1)
                    stage_B(i)
                    if i >= 1:
                        stage_C(i - 1)
                stage_C(ntile - 1)
                sc.barrier()

        class AttnCtx:
            pass

        def attention_units(c, causal_only):
            res = []
            if causal_only:
                for kb in range(0, 4 * c + 4):
                    o = kb - 4 * c
                    if o < 0:
                        res.append((kb, None, 0, 512))
                    else:
                        res.append((kb, 20 + o, 128 * o, 512))
            else:
                full, part = [], []
                for kb in range(max(0, 4 * c - 16), 4 * c + 4):
                    o = kb - 4 * c
                    c0 = 128 * o if o > 0 else 0
                    c1 = min(512, 2048 + 128 * o + 128)
                    (full if (c0 == 0 and c1 == 512) else part).append((kb, o + 16, c0, c1))
                res = full + part
            assert res[0][2] == 0 and res[0][3] == 512
            return res

        fin_pending = []

        def norm64_pieces(ph, ab, k, recip, dst_fn):
            for j in range(4):
                cs = slice(j * 128, (j + 1) * 128)
                fin_pending.append((ab, lambda cs=cs, j=j: sc.op(
                    "dve", lambda e: e.reciprocal(out=recip[k][0:64, cs], in_=ps[64:128, ab, cs]),
                    reads=[("ps", ab)], writes=[(ph, "recip", k, j)])))
            for j2 in range(2):
                cs = slice(j2 * 256, (j2 + 1) * 256)
                fin_pending.append((ab, lambda cs=cs, j2=j2: sc.op(
                    "dve", lambda e: e.tensor_tensor(out=dst_fn(cs), in0=ps[0:64, ab, cs], in1=recip[k][0:64, cs],
                                                     op=ALU.mult),
                    reads=[("ps", ab), (ph, "recip", k, 2 * j2), (ph, "recip", k, 2 * j2 + 1)],
                    writes=[(ph, "ostw", k, j2)])))

        def run_attention(ph, streams, masks_sb, pbuf, scale, finalize, sbanks=(0, 1, 2, 3)):
            units = []
            for (grp, c, items) in streams:
                for it_i, it in enumerate(items):
                    ul = it["units"](c)
                    for ui, (kb, mid, c0, c1) in enumerate(ul):
                        units.append(dict(it=it, c=c, kb=kb, mid=mid, c0=c0, c1=c1, first=(ui == 0), last=(ui == len(ul) - 1),
                                          fin=(ui == len(ul) - 1 and it_i == len(items) - 1), grp=grp, items=items))
            LA = len(sbanks) - 1
            NP = len(pbuf)
            n = len(units)
            pending = fin_pending

            def flush_bank(pb):
                keep = []
                for (bk, fn_) in pending:
                    if bk == pb:
                        fn_()
                    else:
                        keep.append((bk, fn_))
                pending[:] = keep

            for i in range(n + LA):
                if i < n:
                    u = units[i]
                    it = u["it"]
                    sbk = sbanks[i % len(sbanks)]
                    pi = i % NP
                    c = u["c"]
                    kb = u["kb"]
                    c0, c1 = u["c0"], u["c1"]
                    sc.op("pe", lambda e, it=it, c=c, kb=kb, sbk=sbk, c0=c0, c1=c1: e.matmul(
                        ps[:, sbk, c0:c1], lhsT=it["kT"][:, kb * 128:(kb + 1) * 128],
                        rhs=it["qT"][:, c * 512 + c0:c * 512 + c1], start=True, stop=True),
                        reads=it["toks"], writes=[("ps", sbk)])
                    sc.op("act", lambda e, sbk=sbk, pi=pi, c0=c0, c1=c1: e.activation(
                        out=pbuf[pi][:, c0:c1], in_=ps[:, sbk, c0:c1], func=AF.Exp, scale=float(scale)),
                        reads=[("ps", sbk)], writes=[(ph, "pbuf", pi)])
                    if u["mid"] is not None:
                        mid = u["mid"]
                        sc.op("pool" if (i % 3 == 2) else "dve", lambda e, pi=pi, mid=mid, c0=c0, c1=c1: e.tensor_tensor(
                            out=pbuf[pi][:, c0:c1], in0=pbuf[pi][:, c0:c1], in1=masks_sb[:, mid, c0:c1], op=ALU.mult),
                            reads=[(ph, "pbuf", pi), (ph, "masks")], writes=[(ph, "pbuf", pi)])
                    if pending:
                        pending.pop(0)[1]()
                j = i - LA
                if j >= 0:
                    u = units[j]
                    it = u["it"]
                    pi = j % NP
                    kb = u["kb"]
                    c0, c1 = u["c0"], u["c1"]
                    for (lfn, pb) in it["pv"]:
                        if u["first"]:
                            flush_bank(pb)
                        sc.op("pe", lambda e, lfn=lfn, pb=pb, kb=kb, pi=pi, u=u, c0=c0, c1=c1: e.matmul(
                            ps[:, pb, c0:c1], lhsT=lfn(kb), rhs=pbuf[pi][:, c0:c1], start=u["first"], stop=u["last"]),
                            reads=[(ph, "pbuf", pi)] + it["vtoks"], writes=[("ps", pb)], obs=u["last"])
                    if u["last"] and it.get("fin_item") is not None:
                        it["fin_item"](u["c"], it)
                    if u["fin"]:
                        finalize(u["grp"], u["c"], u["items"])
            while pending:
                pending.pop(0)[1]()

        def phase_A0():
            ph = "A0"
            with ExitStack() as st:
                def sb(name, shape, dt):
                    return st.enter_context(nc.sbuf_tensor(ph + name, shape, dt))[:]
                masks_sb = sb("masks", [128, 24, 512], BF16)
                pbuf = [sb(f"pbuf{k}", [128, 512], BF16) for k in range(8)]
                qTc = [sb(f"qT{r}", [128, S], BF16) for r in range(2)]
                kz = [[sb(f"kz{r}{e_}", [128, S], BF16) for e_ in range(2)] for r in range(2)]
                vaug = [sb(f"va{r}", [128, NT, 2, 128], BF16) for r in range(2)]
                ost = [sb(f"ost{r}", [128, S], BF16) for r in range(2)]
                recip = [sb(f"rc{r}", [128, 512], F32) for r in range(2)]
                o1 = sb("o1", [128, 512], F32)
                t2 = sb("t2", [128, 512], F32)
                r2 = sb("r2", [128, 512], F32)
                sq = sb("sq", [128, 512], F32)
                lam_in = sb("lam_in", [128, 4, 64], F32)
                lam_p = sb("lam_p", [128, 2, 64], F32)
                lam_s = sb("lam_s", [128, 2], F32)
                lam_e = sb("lam_e", [128, 2], F32)
                neglam = sb("neglam", [128, 1], F32)
                sgain = sb("sgain", [128, 1], F32)

                sc.dma("sp", masks_sb, masks_d.rearrange("m p t -> p m t"), writes=[(ph, "masks")])
                for r in range(2):
                    sc.op("pool", lambda e, r=r: e.memset(vaug[r][:, :, :, 64:128], 1.0), writes=[(ph, "vaug1", r)])
                    sc.op("pool", lambda e, r=r: e.memset(kz[r][0][64:128, :], 0.0), writes=[(ph, "kzz", r, 0)])
                    sc.op("pool", lambda e, r=r: e.memset(kz[r][1][0:64, :], 0.0), writes=[(ph, "kzz", r, 1)])
                for k, d_ in enumerate([lq1_d, lk1_d, lq2_d, lk2_d]):
                    sc.dma("sp", lam_in[:, k, :], mkap(d_, 0, [[0, 128], [1, 64]]), writes=[(ph, "lam_in")])
                sc.dma("sp", sgain, mkap(subln_d, 0, [[1, 128], [1, 1]]), writes=[(ph, "sgain")])
                sc.op("dve", lambda e: e.tensor_tensor(out=lam_p[:, 0, :], in0=lam_in[:, 0, :], in1=lam_in[:, 1, :],
                                                       op=ALU.mult), reads=[(ph, "lam_in")], writes=[(ph, "lam_p")])
                sc.op("dve", lambda e: e.tensor_tensor(out=lam_p[:, 1, :], in0=lam_in[:, 2, :], in1=lam_in[:, 3, :],
                                                       op=ALU.mult), reads=[(ph, "lam_in")], writes=[(ph, "lam_p")])
                sc.op("dve", lambda e: e.reduce_sum(out=lam_s, in_=lam_p, axis=AX.X),
                      reads=[(ph, "lam_p")], writes=[(ph, "lam_s")])
                sc.op("act", lambda e: e.activation(out=lam_e, in_=lam_s, func=AF.Exp),
                      reads=[(ph, "lam_s")], writes=[(ph, "lam_e")])
                sc.op("dve", lambda e: e.tensor_tensor(out=neglam, in0=lam_e[:, 1:2], in1=lam_e[:, 0:1],
                                                       op=ALU.subtract), reads=[(ph, "lam_e")], writes=[(ph, "neglam")])
                sc.op("dve", lambda e: e.tensor_scalar(out=neglam, in0=neglam, scalar1=-0.2, scalar2=None,
                                                       op0=ALU.add), reads=[(ph, "neglam")], writes=[(ph, "neglam")])
                sc.op("dve", lambda e: e.tensor_scalar(out=sgain, in0=sgain, scalar1=float(math.sqrt(128.0) * 0.8),
                                                       scalar2=None, op0=ALU.mult),
                      reads=[(ph, "sgain")], writes=[(ph, "sgain")])

                for pr in range(4):
                    r = pr % 2
                    sc.dma("sp", qTc[r], qkT0_d[pr], reads=[("dram", "qkT0", k) for k in range(8)],
                           writes=[(ph, "qT", r)])
                    for e_ in range(2):
                        sc.dma("sp", kz[r][e_][e_ * 64:(e_ + 1) * 64, :], qkT0_d[4 + pr, e_ * 64:(e_ + 1) * 64, :],
                               reads=[("dram", "qkT0", k) for k in range(8)], writes=[(ph, "kT", r, e_)])
                    for e_ in range(2):
                        h = pr * 2 + e_
                        sc.dma("sp", vaug[r][:, :, e_, 0:64],
                               v0_d[:, h * 64:(h + 1) * 64].rearrange("(t p) d -> p t d", p=128),
                               reads=[("dram", "v0", k) for k in range(NT)], writes=[(ph, "vaug", r, e_)])
                    streams = []
                    for e_ in range(2):
                        rows = slice(e_ * 64, (e_ + 1) * 64)
                        for c in range(NCH):
                            ab = 6 + (len(streams) % 2)
                            it = dict(qT=qTc[r], kT=kz[r][e_],
                                      pv=[((lambda kb, r=r, e_=e_: vaug[r][:, kb, e_, :]), ab)],
                                      units=lambda c: attention_units(c, False),
                                      toks=[(ph, "qT", r), (ph, "kT", r, e_), (ph, "kzz", r, e_)],
                                      vtoks=[(ph, "vaug", r, e_), (ph, "vaug1", r)], ab=ab, e=e_, r=r)
                            streams.append(((pr, e_), c, [it]))

                    def fin_a(grp, c, items):
                        it = items[0]
                        ab = it["ab"]
                        e_ = it["e"]
                        rr = it["r"]
                        k = ab - 6
                        norm64_pieces(ph, ab, k, recip,
                                      lambda cs: ost[rr][e_ * 64:(e_ + 1) * 64, c * 512 + cs.start:c * 512 + cs.stop])
                    run_attention(ph, streams, masks_sb, pbuf, 0.125, fin_a, sbanks=(0, 1, 2, 3, 4, 5))
                    sc.dma("pool", oT_d[pr], ost[r], reads=[(ph, "ost", r)] + [(ph, "ostw", k_, j_) for k_ in range(2) for j_ in range(2)],
                           writes=[("dram", "oT", pr)])

                for h in range(4):
                    r = h % 2
                    sc.dma("sp", qTc[r], qkT0_d[8 + h], reads=[("dram", "qkT0", k) for k in range(8)],
                           writes=[(ph, "qT", r)])
                    for m in range(2):
                        sc.dma("sp", kz[r][m][m * 64:(m + 1) * 64, :], qkT0_d[12 + h, m * 64:(m + 1) * 64, :],
                               reads=[("dram", "qkT0", k) for k in range(8)], writes=[(ph, "kT", r, m)])
                    vfull = vaug[r][:, :, 0, :]
                    sc.dma("sp", vfull, v0_d[:, 512 + h * 128:512 + (h + 1) * 128].rearrange("(t p) d -> p t d", p=128),
                           reads=[("dram", "v0", k) for k in range(NT)], writes=[(ph, "vaug", r, 0)])
                    streams = []
                    for c in range(NCH):
                        items = []
                        for m in range(2):
                            rows = slice(m * 64, (m + 1) * 64)
                            items.append(dict(qT=qTc[r], kT=kz[r][m],
                                              pv=[((lambda kb, r=r: vaug[r][:, kb, 0, :]), 4 + 2 * m),
                                                  ((lambda kb: ones_bf), 5 + 2 * m)],
                                              units=lambda c: attention_units(c, True),
                                              toks=[(ph, "qT", r), (ph, "kT", r, m), (ph, "kzz", r, m)],
                                              vtoks=[(ph, "vaug", r, 0), "ones_bf"], r=r, m=m))
                        streams.append((h, c, items))

                    def fin_item_d(c, it):
                        m = it["m"]
                        nb, lb = 4 + 2 * m, 5 + 2 * m
                        rc = recip[0] if m == 0 else r2
                        dst = o1 if m == 0 else t2
                        sc.op("dve", lambda e: e.reciprocal(out=rc, in_=bank(lb)),
                              reads=[("ps", lb)], writes=[(ph, "rc", m)])
                        sc.op("dve", lambda e: e.tensor_tensor(out=dst, in0=bank(nb), in1=rc, op=ALU.mult),
                              reads=[("ps", nb), (ph, "rc", m)], writes=[(ph, "on", m)])

                    def fin_d(grp, c, items):
                        rr = items[0]["r"]
                        sc.op("dve", lambda e: e.scalar_tensor_tensor(out=o1, in0=t2, scalar=neglam, in1=o1,
                                                                      op0=ALU.mult, op1=ALU.add),
                              reads=[(ph, "on", 1), (ph, "on", 0), (ph, "neglam")], writes=[(ph, "on", 0)])
                        sc.op("act", lambda e: e.activation(out=sq, in_=o1, func=AF.Square),
                              reads=[(ph, "on", 0)], writes=[(ph, "sq")])
                        sc.op("pe", lambda e: e.matmul(bank(7), lhsT=ones_f, rhs=sq, start=True, stop=True),
                              reads=[(ph, "sq"), "ones_f"], writes=[("ps", 7)])
                        sc.op("act", lambda e: e.activation(out=r2, in_=bank(7), func=AF.Ln, bias=epsc[:, 2:3]),
                              reads=[("ps", 7)], writes=[(ph, "rc", 1)])
                        sc.op("act", lambda e: e.activation(out=r2, in_=r2, func=AF.Exp, scale=-0.5),
                              reads=[(ph, "rc", 1)], writes=[(ph, "rc", 1)])
                        sc.op("dve", lambda e: e.scalar_tensor_tensor(
                            out=ost[rr][:, c * 512:(c + 1) * 512], in0=o1, scalar=sgain, in1=r2,
                            op0=ALU.mult, op1=ALU.mult),
                            reads=[(ph, "on", 0), (ph, "rc", 1), (ph, "sgain")], writes=[(ph, "ost", rr)])
                    for (_g, _c, _items) in streams:
                        for _it in _items:
                            _it["fin_item"] = fin_item_d
                    run_attention(ph, streams, masks_sb, pbuf, 0.125, fin_d)
                    sc.dma("pool", oT_d[4 + h], ost[r], reads=[(ph, "ost", r)], writes=[("dram", "oT", 4 + h)])
                sc.barrier()

        def phase_O(ph, w_out_ap, xsrc_d, after_loads=None, ss_all=None):
            with ExitStack() as st:
                def sb(name, shape, dt):
                    return st.enter_context(nc.sbuf_tensor(ph + name, shape, dt))[:]
                sqj = sb("sqj", [128, D], BF16)
                wout = sb("wout", [128, 8, D], BF16)
                oTc = [sb(f"oTc{r}", [128, 8, 512], BF16) for r in range(2)]
                xt = [sb(f"xt{r}", [128, D], F32) for r in range(2)]
                for kc in range(8):
                    sc.dma("pool", wout[:, kc, :], w_out_ap[kc * 128:(kc + 1) * 128, :], writes=[(ph, "wout", kc)])
                if after_loads is not None:
                    after_loads()
                def load_oT(c):
                    sc.dma("sp", oTc[c % 2], oT_d[:, :, c * 512:(c + 1) * 512].rearrange("c p t -> p c t"),
                           reads=[("dram", "oT", k) for k in range(8)], writes=[(ph, "oTc", c % 2)])

                load_oT(0)
                for c in range(NCH):
                    cr = c % 2
                    if c + 1 < NCH:
                        load_oT(c + 1)
                    for tt in range(4):
                        i = c * 4 + tt
                        r = i % 2
                        sc.dma("sp", xt[r], xsrc_d[i * 128:(i + 1) * 128, :],
                               reads=[("dram", xsrc_d.tensor.name, i)], writes=[(ph, "xt", r)])
                        for kc in range(8):
                            for hf in range(2):
                                sc.op("pe", lambda e, kc=kc, hf=hf: e.matmul(
                                    bank(2 * r + hf), lhsT=oTc[cr][:, kc, tt * 128:(tt + 1) * 128],
                                    rhs=wout[:, kc, hf * 512:(hf + 1) * 512], start=(kc == 0), stop=(kc == 7)),
                                    reads=[(ph, "oTc", cr), (ph, "wout", kc)], writes=[("ps", 2 * r + hf)],
                                    obs=(kc == 7))
                        for hf in range(2):
                            sc.op("dve", lambda e, hf=hf: e.tensor_tensor(
                                out=xt[r][:, hf * 512:(hf + 1) * 512], in0=bank(2 * r + hf),
                                in1=xt[r][:, hf * 512:(hf + 1) * 512], op=ALU.add),
                                reads=[("ps", 2 * r + hf), (ph, "xt", r)], writes=[(ph, "xt", r)])
                        sc.dma("act", xres_d[i * 128:(i + 1) * 128, :], xt[r], reads=[(ph, "xt", r)],
                               writes=[("dram", "xres", i)])
                        if ss_all is not None:
                            sc.op("act", lambda e: e.activation(out=sqj, in_=xt[r], func=AF.Square,
                                                                accum_out=ss_all[:, i:i + 1]),
                                  reads=[(ph, "xt", r)], writes=[(ph, "sqj"), (ph, "ssall")])
                sc.barrier()

        def phase_F(ph, L, final, opre=None):
            with ExitStack() as st:
                def sb(name, shape, dt):
                    return st.enter_context(nc.sbuf_tensor(ph + name, shape, dt))[:]
                wup = sb("wup", [128, 8, 2 * FFN], BF16)
                wdn = sb("wdn", [128, 22, D], BF16)

                def load_weights():
                    for kc in range(8):
                        for hh in range(2):
                            sc.dma("pool", wup[:, kc, hh * FFN:(hh + 1) * FFN],
                                   ffn_w_up_d[L, kc * 128:(kc + 1) * 128, hh * FFN:(hh + 1) * FFN],
                                   writes=[(ph, "wup", kc, hh)])
                    for j in range(22):
                        sc.dma("pool", wdn[:, j, :], ffn_w_down_d[L, j * 128:(j + 1) * 128, :],
                               writes=[(ph, "wdn", j)])
                ss_all = sb("ssall", [128, NT], F32)
                rs_all = sb("rsall", [128, NT], F32)
                assert opre is not None
                phase_O(*opre, after_loads=load_weights, ss_all=ss_all)
                sc.op("act", lambda e: e.activation(out=rs_all, in_=ss_all, func=AF.Sqrt, bias=epsc[:, 0:1]),
                      writes=[(ph, "rsall")])
                sc.op("dve", lambda e: e.reciprocal(out=rs_all, in_=rs_all), reads=[(ph, "rsall")], writes=[(ph, "rsall")])
                gcol = sb("gcol", [128, 8], F32)
                cw = sb("cw", [128, 3, 44], F32)
                cb = sb("cb", [128, 44], F32)
                halo = sb("halo", [128, 2, 44, 2], F32)
                xt = [sb(f"xt{r}", [128, D], F32) for r in range(2)]
                xe = xt
                hn = [sb(f"hn{r}", [128, D], BF16) for r in range(2)]
                ss = [sb(f"ss{r}", [128, 1], F32) for r in range(2)]
                rs = [sb(f"rs{r}", [128, 1], F32) for r in range(2)]
                hTt = [sb(f"hTt{r}", [128, 8, 128], BF16) for r in range(2)]
                hTc = [sb(f"hTc{q}", [128, 8, 512], BF16) for q in range(2)]
                aT = sb("aT", [128, 22, 512], BF16)
                acc = [sb(f"acc{k}", [128, 512], F32) for k in range(4)]
                ht = [sb(f"ht{k}", [128, 4], F32) for k in range(4)]
                if final:
                    gfin = sb("gfin", [128, D], F32)
                    sc.dma("sp", gfin, mkap(final_norm_d, 0, [[0, 128], [1, D]]), writes=[(ph, "gfin")])
                load_gcol(ph, gcol, ffn_norm_d[L], math.sqrt(D))
                sc.dma("sp", cw, ffn_conv_w_d[L], writes=[(ph, "cw")])
                sc.dma("sp", cb, ffn_conv_b_d[L], writes=[(ph, "cb")])
                sc.op("pool", lambda e: e.memset(halo, 0.0),
                      writes=[(ph, "halo", hp, m) for m in range(44) for hp in range(2)])
                nchunk = int(os.environ.get("DBG_FCH", NCH))

                def emit_norm1(cn, tt):
                    i = cn * 4 + tt
                    r = i % 2
                    sc.dma("sp", xt[r], xres_d[i * 128:(i + 1) * 128, :], reads=[("dram", "xres", i)],
                           writes=[(ph, "xt", r)])
                    sc.op("act", lambda e: e.activation(out=hn[r], in_=xt[r], func=AF.Copy, scale=rs_all[:, i:i + 1]),
                          reads=[(ph, "xt", r), (ph, "rsall")], writes=[(ph, "hn", r)])

                def emit_norm2(cn, tt):
                    i = cn * 4 + tt
                    r = i % 2
                    norm_part2(ph, i, hn, hTt, gcol, 0)
                    sc.op("pool", lambda e: e.tensor_copy(out=hTc[cn % 2][:, :, tt * 128:(tt + 1) * 128], in_=hTt[r]),
                          reads=[(ph, "hT", r)], writes=[(ph, "hTc", cn % 2)])

                for tt in range(4):
                    emit_norm1(0, tt)
                    emit_norm2(0, tt)
                ybank = [7, 0]
                for c in range(nchunk):
                    hq = c % 2
                    for j in range(22):
                        pr = j % 2
                        pr3 = j % 3
                        if j in (1, 6, 11, 16) and c + 1 < nchunk:
                            emit_norm1(c + 1, (j - 1) // 5)
                        if j in (4, 9, 14, 19) and c + 1 < nchunk:
                            emit_norm2(c + 1, (j - 4) // 5)
                        for gv in range(2):
                            m = j + 22 * gv
                            b = 1 + 2 * pr3 + gv
                            for kc in range(8):
                                sc.op("pe", lambda e, kc=kc, m=m, b=b: e.matmul(
                                    bank(b), lhsT=wup[:, kc, m * 128:(m + 1) * 128], rhs=hTc[hq][:, kc, :],
                                    start=(kc == 0), stop=(kc == 7)),
                                    reads=[(ph, "hTc", hq), (ph, "wup", kc, gv)], writes=[("ps", b)], obs=(kc == 7))
                        hw_, hr_ = c % 2, (c + 1) % 2
                        for gv in range(2):
                            m = j + 22 * gv
                            b = 1 + 2 * pr3 + gv
                            k = 2 * pr + gv
                            sc.op("act", lambda e, k=k, b=b, m=m: e.activation(
                                out=acc[k], in_=bank(b), func=AF.Identity, scale=cw[:, 2, m:m + 1], bias=cb[:, m:m + 1]),
                                reads=[("ps", b), (ph, "cw"), (ph, "cb")], writes=[(ph, "acc", k)])
                            sc.op("act", lambda e, b=b, m=m: e.activation(out=halo[:, hw_, m, :], in_=ps[:, b, 510:512],
                                                                          func=AF.Copy),
                                  reads=[("ps", b)], writes=[(ph, "halo", hw_, m)])
                            sc.op("dve", lambda e, k=k, m=m, b=b: e.scalar_tensor_tensor(
                                out=acc[k][:, 1:512], in0=ps[:, b, 0:511], scalar=cw[:, 1, m:m + 1], in1=acc[k][:, 1:512],
                                op0=ALU.mult, op1=ALU.add),
                                reads=[("ps", b), (ph, "acc", k), (ph, "cw")], writes=[(ph, "acc", k)])
                            sc.op("dve", lambda e, k=k, m=m, b=b: e.scalar_tensor_tensor(
                                out=acc[k][:, 2:512], in0=ps[:, b, 0:510], scalar=cw[:, 0, m:m + 1], in1=acc[k][:, 2:512],
                                op0=ALU.mult, op1=ALU.add),
                                reads=[("ps", b), (ph, "acc", k), (ph, "cw")], writes=[(ph, "acc", k)])
                            sc.op("pool", lambda e, k=k, m=m: e.tensor_scalar(
                                out=ht[k][:, 0:2], in0=halo[:, hr_, m, 0:2], scalar1=cw[:, 0, m:m + 1], scalar2=None,
                                op0=ALU.mult),
                                reads=[(ph, "halo", hr_, m), (ph, "cw")], writes=[(ph, "ht", k)])
                            sc.op("pool", lambda e, k=k, m=m: e.tensor_scalar(
                                out=ht[k][:, 2:3], in0=halo[:, hr_, m, 1:2], scalar1=cw[:, 1, m:m + 1], scalar2=None,
                                op0=ALU.mult),
                                reads=[(ph, "halo", hr_, m), (ph, "cw")], writes=[(ph, "ht", k)])
                            sc.op("pool", lambda e, k=k: e.tensor_tensor(
                                out=acc[k][:, 0:2], in0=acc[k][:, 0:2], in1=ht[k][:, 0:2], op=ALU.add),
                                reads=[(ph, "ht", k), (ph, "acc", k)], writes=[(ph, "acc", k)])
                            sc.op("pool", lambda e, k=k: e.tensor_tensor(
                                out=acc[k][:, 0:1], in0=acc[k][:, 0:1], in1=ht[k][:, 2:3], op=ALU.add),
                                reads=[(ph, "ht", k), (ph, "acc", k)], writes=[(ph, "acc", k)])
                        kg = 2 * pr
                        kv = 2 * pr + 1
                        sc.op("act", lambda e, kg=kg: e.activation(out=acc[kg], in_=acc[kg], func=AF.Silu),
                              reads=[(ph, "acc", kg)], writes=[(ph, "acc", kg)])
                        sc.op("pool", lambda e, kg=kg, kv=kv, j=j: e.tensor_tensor(
                            out=aT[:, j, :], in0=acc[kg], in1=acc[kv], op=ALU.mult),
                            reads=[(ph, "acc", kg), (ph, "acc", kv)], writes=[(ph, "aT", j)])
                    for tt in range(4):
                        i = c * 4 + tt
                        r = i % 2
                        sc.dma("sp", xe[r], xres_d[i * 128:(i + 1) * 128, :], reads=[("dram", "xres", i)],
                               writes=[(ph, "xt", r)])
                        for j in range(22):
                            for hf in range(2):
                                sc.op("pe", lambda e, j=j, hf=hf: e.matmul(
                                    bank(ybank[hf]), lhsT=aT[:, j, tt * 128:(tt + 1) * 128],
                                    rhs=wdn[:, j, hf * 512:(hf + 1) * 512], start=(j == 0), stop=(j == 21)),
                                    reads=[(ph, "aT", j), (ph, "wdn", j)], writes=[("ps", ybank[hf])], obs=(j == 21))
                        for hf in range(2):
                            sc.op("dve", lambda e, hf=hf: e.tensor_tensor(
                                out=xe[r][:, hf * 512:(hf + 1) * 512], in0=bank(ybank[hf]),
                                in1=xe[r][:, hf * 512:(hf + 1) * 512], op=ALU.add),
                                reads=[("ps", ybank[hf]), (ph, "xt", r)], writes=[(ph, "xt", r)])
                        if not final:
                            sc.dma("sp", xres_d[i * 128:(i + 1) * 128, :], xe[r], reads=[(ph, "xt", r)],
                                   writes=[("dram", "xres", i)])
                        else:
                            sc.op("act", lambda e: e.activation(out=hn[r], in_=xe[r], func=AF.Square, accum_out=ss[r]),
                                  reads=[(ph, "xt", r)], writes=[(ph, "hn", r), (ph, "ss", r)])
                            sc.op("act", lambda e: e.activation(out=rs[r], in_=ss[r], func=AF.Sqrt, bias=epsc[:, 3:4],
                                                                scale=float(1.0 / D)),
                                  reads=[(ph, "ss", r)], writes=[(ph, "rs", r)])
                            sc.op("dve", lambda e: e.reciprocal(out=rs[r], in_=rs[r]),
                                  reads=[(ph, "rs", r)], writes=[(ph, "rs", r)])
                            sc.op("dve", lambda e: e.scalar_tensor_tensor(out=xe[r], in0=xe[r], scalar=rs[r], in1=gfin,
                                                                          op0=ALU.mult, op1=ALU.mult),
                                  reads=[(ph, "xt", r), (ph, "rs", r), (ph, "gfin")], writes=[(ph, "xt", r)])
                            sc.dma("sp", out_d[i * 128:(i + 1) * 128, :], xe[r], reads=[(ph, "xt", r)],
                                   writes=[("dram", "out", i)])
                sc.barrier()

        def phase_P1():
            ph = "P1"
            with ExitStack() as st:
                def sb(name, shape, dt):
                    return st.enter_context(nc.sbuf_tensor(ph + name, shape, dt))[:]
                win = sb("win", [128, 8, 416], BF16)
                wuq = sb("wuq", [128, 2, 1536], BF16)
                wukv = sb("wukv", [128, 2048], BF16)
                gcol = sb("gcol", [128, 8], F32)
                gq = sb("gq", [128, 384], F32)
                xt = [sb(f"xt{r}", [128, D], F32) for r in range(2)]
                hn = [sb(f"hn{r}", [128, D], BF16) for r in range(2)]
                ss = [sb(f"ss{r}", [128, 1], F32) for r in range(2)]
                rs = [sb(f"rs{r}", [128, 1], F32) for r in range(2)]
                hT = [sb(f"hT{r}", [128, 8, 128], BF16) for r in range(2)]
                junk = sb("junk", [128, 256], F32)
                ssl = [sb(f"ssl{r}", [128, 2], F32) for r in range(2)]
                rsl = [sb(f"rsl{r}", [128, 2], F32) for r in range(2)]
                lat = [sb(f"lat{r}", [128, 384], BF16) for r in range(2)]
                latT = [sb(f"latT{r}", [128, 3, 128], BF16) for r in range(2)]
                qs = [sb(f"qs{r}", [128, 16, 96], BF16) for r in range(2)]
                ks = [sb(f"ks{r}", [128, 16, 96], BF16) for r in range(2)]
                vs = [sb(f"vs{r}", [128, 16, 64], BF16) for r in range(2)]
                kpe = [sb(f"kpe{r}", [128, 1, 32], BF16) for r in range(2)]
                tmp = [sb(f"tmp{r}", [128, 4, 256], F32) for r in range(2)]
                qst = [sb(f"qst{r}", [128, 16, 512], BF16) for r in range(2)]
                kst = [sb(f"kst{r}", [128, 16, 512], BF16) for r in range(2)]
                for kc in range(8):
                    sc.dma("pool", win[:, kc, :], mla_w_in_d[kc * 128:(kc + 1) * 128, :], writes=[(ph, "win", kc)])
                for kc in range(2):
                    sc.dma("pool", wuq[:, kc, :], mla_w_uq_d[kc * 128:(kc + 1) * 128, :], writes=[(ph, "wuq")])
                sc.dma("pool", wukv, mla_w_ukv_d, writes=[(ph, "wukv")])
                load_gcol(ph, gcol, attn_norm_d[1], math.sqrt(D))
                sc.dma("sp", gq[:, 0:256], mkap(mla_q_norm_d, 0, [[0, 128], [1, 256]]), writes=[(ph, "gq")])
                sc.dma("sp", gq[:, 256:384], mkap(mla_kv_norm_d, 0, [[0, 128], [1, 128]]), writes=[(ph, "gq")])
                sc.op("dve", lambda e: e.tensor_scalar(out=gq[:, 0:256], in0=gq[:, 0:256], scalar1=16.0, scalar2=None,
                                                       op0=ALU.mult), reads=[(ph, "gq")], writes=[(ph, "gq")])
                sc.op("dve", lambda e: e.tensor_scalar(out=gq[:, 256:384], in0=gq[:, 256:384],
                                                       scalar1=float(math.sqrt(128.0)), scalar2=None, op0=ALU.mult),
                      reads=[(ph, "gq")], writes=[(ph, "gq")])
                kvb = [5, 6, 7, 1]
                ropq = [sb(f"ropq{r}", [128, 16, 32], F32) for r in range(2)]
                ropk = [sb(f"ropk{r}", [128, 1, 32], F32) for r in range(2)]
                ntile = int(os.environ.get("DBG_NT", NT))

                def stage_A1(i):
                    norm_part1(ph, i, xres_d, xt, hn, ss, rs, ldq="pool")

                def stage_A2(i):
                    norm_part2(ph, i, hn, hT, gcol, 0)

                def stage_B1(i):
                    r = i % 2
                    for kc in range(8):
                        sc.op("pe", lambda e, kc=kc: e.matmul(ps[:, 1, 0:416], lhsT=hT[r][:, kc, :], rhs=win[:, kc, :],
                                                             start=(kc == 0), stop=(kc == 7)),
                              reads=[(ph, "hT", r), (ph, "win", kc)], writes=[("ps", 1)], obs=(kc == 7))
                    sc.op("act", lambda e: e.activation(out=junk[:, 0:256], in_=ps[:, 1, 0:256], func=AF.Square,
                                                        accum_out=ssl[r][:, 0:1]),
                          reads=[("ps", 1)], writes=[(ph, "junk"), (ph, "ssl", r)])
                    sc.op("act", lambda e: e.activation(out=junk[:, 0:128], in_=ps[:, 1, 256:384], func=AF.Square,
                                                        accum_out=ssl[r][:, 1:2]),
                          reads=[("ps", 1)], writes=[(ph, "junk"), (ph, "ssl", r)])
                    sc.op("dve", lambda e: e.tensor_copy(out=ropk[r], in_=ps[:, 1:2, 384:416]),
                          reads=[("ps", 1)], writes=[(ph, "ropk", r)])
                    sc.op("act", lambda e: e.activation(out=rsl[r][:, 0:1], in_=ssl[r][:, 0:1], func=AF.Sqrt,
                                                        bias=epsc[:, 1:2]),
                          reads=[(ph, "ssl", r)], writes=[(ph, "rsl", r)])
                    sc.op("act", lambda e: e.activation(out=rsl[r][:, 1:2], in_=ssl[r][:, 1:2], func=AF.Sqrt,
                                                        bias=epsc[:, 2:3]),
                          reads=[(ph, "ssl", r)], writes=[(ph, "rsl", r)])
                    sc.op("dve", lambda e: e.reciprocal(out=rsl[r], in_=rsl[r]),
                          reads=[(ph, "rsl", r)], writes=[(ph, "rsl", r)])
                    sc.op("dve", lambda e: e.scalar_tensor_tensor(out=lat[r][:, 0:256], in0=ps[:, 1, 0:256],
                                                                  scalar=rsl[r][:, 0:1], in1=gq[:, 0:256],
                                                                  op0=ALU.mult, op1=ALU.mult),
                          reads=[("ps", 1), (ph, "rsl", r), (ph, "gq")], writes=[(ph, "lat", r)])
                    sc.op("dve", lambda e: e.scalar_tensor_tensor(out=lat[r][:, 256:384], in0=ps[:, 1, 256:384],
                                                                  scalar=rsl[r][:, 1:2], in1=gq[:, 256:384],
                                                                  op0=ALU.mult, op1=ALU.mult),
                          reads=[("ps", 1), (ph, "rsl", r), (ph, "gq")], writes=[(ph, "lat", r)])
                    rope_free(ph, ropk[r], kpe[r], 1, 16, cosM[:, i, :], sinM[:, i, :], tmp, r,
                              [(ph, "ropk", r)], [(ph, "kpe", r)])

                def stage_B2(i):
                    r = i % 2
                    tbv = bank_bf(0)
                    for k3 in range(3):
                        sc.op("pe", lambda e, k3=k3: e.transpose(out=tbv[:, k3 * 128:(k3 + 1) * 128],
                                                                 in_=lat[r][:, k3 * 128:(k3 + 1) * 128], identity=ident),
                              reads=[(ph, "lat", r), "ident"], writes=[("ps", 0)], obs=(k3 == 2))
                    sc.op("act", lambda e: e.activation(out=latT[r], in_=tbv[:, 0:384].rearrange("p (c t) -> p c t", c=3),
                                                        func=AF.Copy),
                          reads=[("ps", 0)], writes=[(ph, "latT", r)])
                    for n in range(3):
                        for kc in range(2):
                            sc.op("pe", lambda e, n=n, kc=kc: e.matmul(
                                bank(2 + n), lhsT=latT[r][:, kc, :], rhs=wuq[:, kc, n * 512:(n + 1) * 512],
                                start=(kc == 0), stop=(kc == 1)),
                                reads=[(ph, "latT", r), (ph, "wuq")], writes=[("ps", 2 + n)], obs=(kc == 1))
                    for n in range(4):
                        sc.op("pe", lambda e, n=n: e.matmul(bank(kvb[n]), lhsT=latT[r][:, 2, :],
                                                            rhs=wukv[:, n * 512:(n + 1) * 512], start=True, stop=True),
                              reads=[(ph, "latT", r), (ph, "wukv")], writes=[("ps", kvb[n])])
                    for (bq, h0, nh) in [(2, 0, 5), (3, 5, 5), (4, 10, 6)]:
                        qsrc_r = mkap(ps, 2 * 512 + 96 * h0 + 64, [list(ps.ap[0]), [96, nh], [1, 32]])
                        sc.op("dve", lambda e, qsrc_r=qsrc_r, h0=h0, nh=nh: e.tensor_copy(out=ropq[r][:, h0:h0 + nh, :],
                                                                                         in_=qsrc_r),
                              reads=[("ps", bq)], writes=[(ph, "ropq", r)])
                    qflat = qs[r].rearrange("p h d -> p (h d)")
                    for n in range(3):
                        sc.op("act", lambda e, n=n: e.activation(out=qflat[:, n * 512:(n + 1) * 512], in_=bank(2 + n),
                                                                 func=AF.Copy),
                              reads=[("ps", 2 + n)], writes=[(ph, "qs", r)])
                    for n in range(4):
                        kvv = bank(kvb[n]).rearrange("p (h d) -> p h d", d=128)
                        sc.op("act", lambda e, n=n, kvv=kvv: e.activation(out=ks[r][:, 4 * n:4 * n + 4, 0:64],
                                                                          in_=kvv[:, :, 0:64], func=AF.Copy),
                              reads=[("ps", kvb[n])], writes=[(ph, "ks", r)])
                        sc.op("dve", lambda e, n=n, kvv=kvv: e.tensor_copy(out=vs[r][:, 4 * n:4 * n + 4, :],
                                                                           in_=kvv[:, :, 64:128]),
                              reads=[("ps", kvb[n])], writes=[(ph, "vs", r)])
                    rope_free(ph, ropq[r], qs[r][:, :, 64:96], 16, 16, cosM[:, i, :], sinM[:, i, :], tmp, r,
                              [(ph, "ropq", r)], [(ph, "qs", r)])
                    sc.op("pool", lambda e: e.tensor_copy(
                        out=ks[r][:, :, 64:96], in_=mkap(kpe[r], 0, [list(kpe[r].ap[0]), [0, 16], [1, 32]])),
                        reads=[(ph, "kpe", r)], writes=[(ph, "ks", r)])
                    sc.dma("sp", v1_d[i * 128:(i + 1) * 128, :], vs[r].rearrange("p h d -> p (h d)"),
                           reads=[(ph, "vs", r)], writes=[("dram", "v1", i)])

                def stage_C(i):
                    r = i % 2
                    sr = (i // 4) % 2
                    cbanks = [0, 2, 3, 4]
                    rnd = 0
                    for (src, dstst, nm) in [(qs, qst, "qst"), (ks, kst, "kst")]:
                        for hb in range(2):
                            tbk = cbanks[rnd]
                            rnd += 1
                            tbv = bank_bf(tbk)
                            for hh in range(8):
                                h = hb * 8 + hh
                                sc.op("pe", lambda e, h=h, hh=hh, src=src, tbv=tbv: e.transpose(
                                    out=tbv[0:96, hh * 128:(hh + 1) * 128], in_=src[r][:, h, :], identity=ident),
                                    reads=[(ph, nm[0] + "s", r), "ident"], writes=[("ps", tbk)], obs=(hh == 7))
                            dstv = dstst[sr][0:96, hb * 8:(hb + 1) * 8, (i % 4) * 128:(i % 4 + 1) * 128]
                            srcv = tbv[0:96, :].rearrange("p (c t) -> p c t", c=8)
                            if hb == 0:
                                sc.op("act", lambda e, dstv=dstv, srcv=srcv: e.activation(out=dstv, in_=srcv, func=AF.Copy),
                                      reads=[("ps", tbk)], writes=[(ph, nm, sr)])
                            else:
                                sc.op("dve", lambda e, dstv=dstv, srcv=srcv: e.tensor_copy(out=dstv, in_=srcv),
                                      reads=[("ps", tbk)], writes=[(ph, nm, sr)])
                    if i % 4 == 3:
                        c0 = (i // 4) * 512
                        sc.dma("sp", qT1_d[:, :, c0:c0 + 512].rearrange("h p t -> p h t"), qst[sr][0:96],
                               reads=[(ph, "qst", sr)], writes=[("dram", "qT1", i // 4)])
                        sc.dma("sp", kT1_d[:, :, c0:c0 + 512].rearrange("h p t -> p h t"), kst[sr][0:96],
                               reads=[(ph, "kst", sr)], writes=[("dram", "kT1", i // 4)])

                stage_A1(0)
                stage_A2(0)
                if ntile > 1:
                    stage_A1(1)
                    stage_A2(1)
                if ntile > 2:
                    stage_A1(2)
                stage_B1(0)
                for i in range(ntile):
                    if i + 3 < ntile:
                        stage_A1(i + 3)
                    if i + 2 < ntile:
                        stage_A2(i + 2)
                    if i + 1 < ntile:
                        stage_B1(i + 1)
                    stage_B2(i)
                    if i >= 1:
                        stage_C(i - 1)
                stage_C(ntile - 1)
                sc.barrier()

        def phase_A1():
            ph = "A1"
            with ExitStack() as st:
                def sb(name, shape, dt):
                    return st.enter_context(nc.sbuf_tensor(ph + name, shape, dt))[:]
                masks_sb = sb("masks", [128, 24, 512], BF16)
                pbuf = [sb(f"pbuf{k}", [128, 512], BF16) for k in range(8)]
                qTh = [sb(f"qT{r}", [128, S], BF16) for r in range(2)]
                kTh = [sb(f"kT{r}", [128, S], BF16) for r in range(2)]
                vaug = [sb(f"va{r}", [128, NT, 128], BF16) for r in range(2)]
                ost = [sb(f"ost{r}", [128, S], BF16) for r in range(2)]
                recip = [sb(f"rc{r}", [128, 512], F32) for r in range(2)]
                sc.dma("sp", masks_sb, masks_d.rearrange("m p t -> p m t"), writes=[(ph, "masks")])
                for r in range(2):
                    sc.op("pool", lambda e, r=r: e.memset(vaug[r][:, :, 64:128], 1.0), writes=[(ph, "vaug1", r)])
                scale = 96.0 ** -0.5
                for h in range(int(os.environ.get("DBG_NH", 16))):
                    r = h % 2
                    pr = h // 2
                    orr = pr % 2
                    sc.dma("sp", qTh[r][0:96, :], qT1_d[h], reads=[("dram", "qT1", k) for k in range(8)],
                           writes=[(ph, "qT", r)])
                    sc.dma("sp", kTh[r][0:96, :], kT1_d[h], reads=[("dram", "kT1", k) for k in range(8)],
                           writes=[(ph, "kT", r)])
                    sc.dma("sp", vaug[r][:, :, 0:64], v1_d[:, h * 64:(h + 1) * 64].rearrange("(t p) d -> p t d", p=128),
                           reads=[("dram", "v1", k) for k in range(NT)], writes=[(ph, "vaug", r)])
                    streams = []
                    for c in range(int(os.environ.get("DBG_NC", NCH))):
                        ab = 6 + (c % 2)
                        it = dict(qT=qTh[r][0:96, :], kT=kTh[r][0:96, :],
                                  pv=[((lambda kb, r=r: vaug[r][:, kb, :]), ab)],
                                  units=lambda c: attention_units(c, True),
                                  toks=[(ph, "qT", r), (ph, "kT", r)],
                                  vtoks=[(ph, "vaug", r), (ph, "vaug1", r)], ab=ab, e=h % 2, r=orr)
                        streams.append((h, c, [it]))

                    def fin_m(grp, c, items):
                        it = items[0]
                        ab = it["ab"]
                        e_ = it["e"]
                        rr = it["r"]
                        k = ab - 6
                        norm64_pieces(ph, ab, k, recip,
                                      lambda cs: ost[rr][e_ * 64:(e_ + 1) * 64, c * 512 + cs.start:c * 512 + cs.stop])
                    run_attention(ph, streams, masks_sb, pbuf, scale, fin_m, sbanks=(0, 1, 2, 3, 4, 5))
                    if h % 2 == 1:
                        sc.dma("pool", oT_d[pr], ost[orr], reads=[(ph, "ost", orr)] + [(ph, "ostw", k_, j_) for k_ in range(2) for j_ in range(2)],
                               writes=[("dram", "oT", pr)])
                sc.barrier()

        phases = [
            ("P0", phase_P0),
            ("A0", phase_A0),
            ("F0", lambda: phase_F("F0", 0, False, opre=("O0", hyb_w_out_d, x_d))),
            ("P1", phase_P1),
            ("A1", phase_A1),
            ("F1", lambda: phase_F("F1", 1, True, opre=("O1", mla_w_out_d, xres_d))),
        ]
        for name, fn in phases:
            if stop_after == "init":
                break
            if only is not None and name not in only:
                continue
            fn()
            if stop_after == name:
                break
        sc.barrier()
    return nc


_MASKS = None


def _make_masks():
    global _MASKS
    if _MASKS is not None:
        return _MASKS
    m = np.zeros((24, 128, 512), dtype=np.float32)
    j = np.arange(128)[:, None]
    i = np.arange(512)[None, :]
    for idx in range(20):
        off = idx - 16
        delta = (i - j) - 128 * off
        cnt = ((delta >= 0) & (delta <= 128)).astype(np.float32)
        cnt += ((delta >= 0) & (delta <= 512) & (delta % 4 == 0)).astype(np.float32)
        cnt += ((delta >= 0) & (delta <= 2048) & (delta % 16 == 0)).astype(np.float32)
        m[idx] = cnt
    for jj in range(4):
        delta = (i - j) - 128 * jj
        m[20 + jj] = (delta >= 0).astype(np.float32)
    _MASKS = m.astype(ml_dtypes.bfloat16)
    return _MASKS


def make_in_maps(inputs):
    f = lambda a: np.ascontiguousarray(np.asarray(a, dtype=np.float32))
    shared = {
        "attn_norm": f(np.asarray(inputs["attn_norm"]).reshape(2, 8, 128).transpose(0, 2, 1)),
        "ffn_norm": f(np.asarray(inputs["ffn_norm"]).reshape(2, 8, 128).transpose(0, 2, 1)),
        "final_norm": f(inputs["final_norm"]),
        "hyb_w_in": f(inputs["hyb_w_in"][0]),
        "hyb_w_out": f(inputs["hyb_w_out"][0]),
        "lq1": f(inputs["diff_lambda_q1"][0]),
        "lk1": f(inputs["diff_lambda_k1"][0]),
        "lq2": f(inputs["diff_lambda_q2"][0]),
        "lk2": f(inputs["diff_lambda_k2"][0]),
        "subln": f(inputs["diff_subln"][0]),
        "mla_w_in": f(inputs["mla_w_in"][0]),
        "mla_q_norm": f(inputs["mla_q_norm"][0]),
        "mla_w_uq": f(inputs["mla_w_uq"][0]),
        "mla_kv_norm": f(inputs["mla_kv_norm"][0]),
        "mla_w_ukv": f(inputs["mla_w_ukv"][0]),
        "mla_w_out": f(inputs["mla_w_out"][0]),
        "ffn_w_up": f(inputs["ffn_w_up"]),
        "ffn_conv_w": f(np.asarray(inputs["ffn_conv_w"]).reshape(2, 3, 44, 128).transpose(0, 3, 1, 2)),
        "ffn_conv_b": f(np.asarray(inputs["ffn_conv_b"]).reshape(2, 44, 128).transpose(0, 2, 1)),
        "ffn_w_down": f(inputs["ffn_w_down"]),
        "masks": _make_masks(),
        "ident": np.eye(128, dtype=np.float32).astype(ml_dtypes.bfloat16),
    }
    x = np.asarray(inputs["x"], dtype=np.float32)
    pos = np.asarray(inputs["positions"], dtype=np.int32)
    maps = []
    for b in range(8):
        m = dict(shared)
        m["x"] = np.ascontiguousarray(x[b])
        m["pos"] = np.ascontiguousarray(pos[b].reshape(NT, 128).T)
        maps.append(m)
    return maps


def kernel(**inputs):
    nc = build_program()
    in_maps = make_in_maps(inputs)
    res = run_bass_kernel_spmd(nc, in_maps, core_ids=list(range(8)))
    out = np.stack([np.asarray(r["out"], dtype=np.float32) for r in res.results], axis=0)
    return out
```

```python
import math
import os
import numpy as np
import ml_dtypes
import concourse.bass as bass
import concourse.mybir as mybir
from concourse.bass_utils import run_bass_kernel_spmd

F32 = mybir.dt.float32
BF16 = mybir.dt.bfloat16
I32 = mybir.dt.int32
AF = mybir.ActivationFunctionType
ALU = mybir.AluOpType
AX = mybir.AxisListType

S = 4096
D = 1024
NT = 32
NCH = 8
FFN = 2816
EPS = 1e-6
PI = math.pi


class Sched:
    def __init__(self, nc, stack):
        self.nc = nc
        self.stack = stack
        self.eng = {"pe": nc.tensor, "act": nc.scalar, "dve": nc.vector, "pool": nc.gpsimd, "sp": nc.sync}
        self.sem = {}
        self.cnt = {}
        for e in self.eng:
            self._new_sem(e)
        self.nslots = 12
        self.dslots = {}
        for e in ("sp", "pool", "act"):
            self.dslots[e] = [[stack.enter_context(nc.semaphore(f"d_{e}_{i}")), 0] for i in range(self.nslots)]
        self.dnext = {"sp": 0, "pool": 0, "act": 0}
        self.sig = []
        self.seen = {e: {} for e in self.eng}
        self.last_writer = {}
        self.readers = {}
        self.ps_readers = {}
        self.last_op = {e: None for e in self.eng}
        self.dma_ops = []
        self.nsem = 0
        self.nwaits = 0

    def _new_sem(self, e):
        self.nsem = getattr(self, "nsem", 0) + 1
        self.sem[e] = self.stack.enter_context(self.nc.semaphore(f"s_{e}_{self.nsem}"))
        self.cnt[e] = 0

    def _deps(self, reads, writes, e=None):
        deps = set()
        for t in reads:
            w = self.last_writer.get(t)
            if w is not None:
                deps.add(w)
            if isinstance(t, tuple) and t[0] == "ps":
                for (rid, re_) in self.ps_readers.get(t, []):
                    if re_ != e:
                        deps.add(rid)
        for t in writes:
            w = self.last_writer.get(t)
            if w is not None:
                deps.add(w)
            r = self.readers.get(t)
            if r:
                deps.update(r)
        return deps

    def _commit(self, oid, reads, writes, e=None):
        for t in reads:
            self.readers.setdefault(t, []).append(oid)
            if isinstance(t, tuple) and t[0] == "ps":
                self.ps_readers.setdefault(t, []).append((oid, e))
        for t in writes:
            self.last_writer[t] = oid
            self.readers[t] = []
            if isinstance(t, tuple) and t[0] == "ps":
                self.ps_readers[t] = []

    def _emit_waits(self, e, deps, attach=False):
        engobj = self.eng[e]
        need = {}
        for d in deps:
            s = self.sig[d]
            if s is None:
                continue
            sem, val, seng = s
            if seng == "pe" and e == "pe":
                continue
            k = id(sem)
            if self.seen[e].get(k, 0) >= val:
                continue
            if k not in need or need[k][1] < val:
                need[k] = (sem, val)
        items = list(need.items())
        last = None
        if attach and items:
            last = items.pop()
        for k, (sem, val) in items:
            engobj.wait_ge(sem, val)
            self.seen[e][k] = val
            self.nwaits += 1
        if last is not None:
            self.seen[e][last[0]] = last[1][1]
            return last[1]
        return None

    def op(self, e, fn, reads=(), writes=(), obs=True, after=()):
        deps = self._deps(reads, writes, e)
        deps.update(after)
        last = self._emit_waits(e, deps, attach=True)
        ins = fn(self.eng[e])
        if last is not None:
            ins._wait_ge(last[0], last[1])
        oid = len(self.sig)
        if obs:
            if self.cnt[e] >= 30000:
                self._new_sem(e)
            self.cnt[e] += 1
            ins.then_inc(self.sem[e], 1)
            self.sig.append((self.sem[e], self.cnt[e], e))
        else:
            self.sig.append(None)
        self._commit(oid, reads, writes, e)
        if obs:
            self.last_op[e] = oid
        return oid

    def dma(self, e, out, in_, reads=(), writes=(), after=()):
        deps = self._deps(reads, writes)
        deps.update(after)
        self._emit_waits(e, deps)
        si = self.dnext[e]
        self.dnext[e] = (si + 1) % self.nslots
        slot = self.dslots[e][si]
        sem, uses = slot
        engobj = self.eng[e]
        k = id(sem)
        if uses > 0 and self.seen[e].get(k, 0) < 16 * uses:
            engobj.wait_ge(sem, 16 * uses)
            self.seen[e][k] = 16 * uses
        if uses >= 1800:
            raise RuntimeError("dma sem overflow")
        ins = engobj.dma_start(out=out, in_=in_)
        slot[1] = uses + 1
        ins.then_inc(sem, 16)
        oid = len(self.sig)
        self.sig.append((sem, 16 * (uses + 1), "dma_" + e))
        self._commit(oid, reads, writes)
        self.dma_ops.append(oid)
        return oid

    def barrier(self):
        deps = set(self.dma_ops)
        for e in self.eng:
            if self.last_op[e] is not None:
                deps.add(self.last_op[e])
        self.dma_ops = []
        self._emit_waits_barrier("sp", deps)
        ins = self.eng["sp"].nop()
        self.cnt["sp"] += 1
        ins.then_inc(self.sem["sp"], 1)
        oid = len(self.sig)
        self.sig.append((self.sem["sp"], self.cnt["sp"], "sp"))
        self.last_op["sp"] = oid
        for e in self.eng:
            if e == "sp":
                continue
            self.eng[e].wait_ge(self.sem["sp"], self.cnt["sp"])
            self.seen[e][id(self.sem["sp"])] = self.cnt["sp"]
        self.last_writer = {}
        self.readers = {}
        self.ps_readers = {}

    def _emit_waits_barrier(self, e, deps):
        engobj = self.eng[e]
        need = {}
        for d in deps:
            s = self.sig[d]
            if s is None:
                continue
            sem, val, seng = s
            k = id(sem)
            if self.seen[e].get(k, 0) >= val:
                continue
            if k not in need or need[k][1] < val:
                need[k] = (sem, val)
        for k, (sem, val) in need.items():
            engobj.wait_ge(sem, val)
            self.seen[e][k] = val


def mkap(base, off, dims):
    return bass.AP(tensor=base.tensor, offset=base.offset + off, ap=[list(d) for d in dims])


def build_program(debug=False, stop_after=None, only=None):
    nc = bass.Bass("TRN2", target_bir_lowering=False)
    from contextlib import ExitStack

    def din(name, shape, dt=F32):
        return nc.dram_tensor(name, list(shape), dt, kind="ExternalInput").ap()

    def dscr(name, shape, dt):
        skind = "ExternalOutput" if (debug and name in debug) else "Internal"
        return nc.dram_tensor(name, list(shape), dt, kind=skind).ap()

    x_d = din("x", [S, D])
    pos_d = din("pos", [128, NT], I32)
    attn_norm_d = din("attn_norm", [2, 128, 8])
    ffn_norm_d = din("ffn_norm", [2, 128, 8])
    final_norm_d = din("final_norm", [D])
    hyb_w_in_d = din("hyb_w_in", [D, 3072])
    hyb_w_out_d = din("hyb_w_out", [D, D])
    lq1_d = din("lq1", [64])
    lk1_d = din("lk1", [64])
    lq2_d = din("lq2", [64])
    lk2_d = din("lk2", [64])
    subln_d = din("subln", [128])
    mla_w_in_d = din("mla_w_in", [D, 416])
    mla_q_norm_d = din("mla_q_norm", [256])
    mla_w_uq_d = din("mla_w_uq", [256, 1536])
    mla_kv_norm_d = din("mla_kv_norm", [128])
    mla_w_ukv_d = din("mla_w_ukv", [128, 2048])
    mla_w_out_d = din("mla_w_out", [D, D])
    ffn_w_up_d = din("ffn_w_up", [2, D, 2 * FFN])
    ffn_conv_w_d = din("ffn_conv_w", [2, 128, 3, 44])
    ffn_conv_b_d = din("ffn_conv_b", [2, 128, 44])
    ffn_w_down_d = din("ffn_w_down", [2, FFN, D])
    masks_d = din("masks", [24, 128, 512], BF16)
    ident_d = din("ident", [128, 128], BF16)

    out_d = nc.dram_tensor("out", [S, D], F32, kind="ExternalOutput").ap()

    xres_d = dscr("xres", [S, D], F32)
    qkT0_d = dscr("qkT0", [16, 128, S], BF16)
    v0_d = dscr("v0", [S, 1024], BF16)
    oT_d = dscr("oT", [8, 128, S], BF16)
    qT1_d = dscr("qT1", [16, 96, S], BF16)
    kT1_d = dscr("kT1", [16, 96, S], BF16)
    v1_d = dscr("v1", [S, 1024], BF16)

    inv_freq_p = [500000.0 ** (-(2 * i) / 16.0) for i in range(8)]
    inv_freq_m = [10000.0 ** (-(2 * i) / 32.0) for i in range(16)]

    with ExitStack() as top:
        sc = Sched(nc, top)
        ps_t = top.enter_context(nc.psum_tensor("ps", [128, 8, 512], F32))
        ps = ps_t[:]

        def bank(b):
            return ps[:, b, :]

        def bank_bf(b, n=1):
            return ps[:, b, :].bitcast(BF16)

        tmpst = ExitStack()
        ident = top.enter_context(nc.sbuf_tensor("c_ident", [128, 128], BF16))[:]
        cosP = top.enter_context(nc.sbuf_tensor("c_cosP", [128, NT, 8], F32))[:]
        sinP = top.enter_context(nc.sbuf_tensor("c_sinP", [128, NT, 8], F32))[:]
        cosM = top.enter_context(nc.sbuf_tensor("c_cosM", [128, NT, 16], F32))[:]
        sinM = top.enter_context(nc.sbuf_tensor("c_sinM", [128, NT, 16], F32))[:]
        ones_bf = top.enter_context(nc.sbuf_tensor("c_ones_bf", [128, 128], BF16))[:]
        ones_f = top.enter_context(nc.sbuf_tensor("c_ones_f", [128, 128], F32))[:]
        negpi = top.enter_context(nc.sbuf_tensor("c_negpi", [128, 1], F32))[:]

        epsc = top.enter_context(nc.sbuf_tensor("c_epsc", [128, 4], F32))[:]
        for k_, v_ in enumerate([D * EPS, 256 * EPS, 128 * EPS, EPS]):
            sc.op("dve", lambda e, k_=k_, v_=v_: e.memset(epsc[:, k_:k_ + 1], float(v_)), writes=["epsc"])
        sc.dma("sp", ident, ident_d, writes=["ident"])
        sc.op("dve", lambda e: e.memset(ones_bf, 1.0), writes=["ones_bf"])
        sc.op("dve", lambda e: e.memset(ones_f, 1.0), writes=["ones_f"])
        sc.op("dve", lambda e: e.memset(negpi, -PI), writes=["negpi"])

        posi = tmpst.enter_context(nc.sbuf_tensor("c_posi", [128, NT], I32))[:]
        posf = tmpst.enter_context(nc.sbuf_tensor("c_posf", [128, NT], F32))[:]
        angt = tmpst.enter_context(nc.sbuf_tensor("c_angt", [128, NT, 16], F32))[:]
        ang2 = tmpst.enter_context(nc.sbuf_tensor("c_ang2", [128, NT, 16], F32))[:]
        rr_ki = tmpst.enter_context(nc.sbuf_tensor("c_rr_ki", [128, NT, 16], I32))[:]
        rr_kf = tmpst.enter_context(nc.sbuf_tensor("c_rr_kf", [128, NT, 16], F32))[:]
        rr_m = tmpst.enter_context(nc.sbuf_tensor("c_rr_m", [128, NT, 16], F32))[:]
        sc.dma("sp", posi, pos_d, writes=["posi"])
        sc.op("dve", lambda e: e.tensor_copy(out=posf, in_=posi), reads=["posi"], writes=["posf"])
        C1 = 6.28125
        C2 = 2 * PI - C1

        def reduced_sin(dst, nf, add):
            x = ang2[:, :, 0:nf]
            ki = rr_ki[:, :, 0:nf]
            kf = rr_kf[:, :, 0:nf]
            m = rr_m[:, :, 0:nf]
            sc.op("dve", lambda e: e.tensor_scalar(out=x, in0=angt[:, :, 0:nf], scalar1=float(add), scalar2=None,
                                                   op0=ALU.add), reads=[("angt", i_) for i_ in range(nf)], writes=["ang2"])
            sc.op("dve", lambda e: e.tensor_scalar(out=kf, in0=x, scalar1=float(1.0 / (2 * PI)), scalar2=None,
                                                   op0=ALU.mult), reads=["ang2"], writes=["rr_kf"])
            sc.op("dve", lambda e: e.tensor_copy(out=ki, in_=kf), reads=["rr_kf"], writes=["rr_ki"])
            sc.op("dve", lambda e: e.tensor_copy(out=kf, in_=ki), reads=["rr_ki"], writes=["rr_kf"])
            sc.op("dve", lambda e: e.scalar_tensor_tensor(out=x, in0=kf, scalar=float(-C1), in1=x,
                                                          op0=ALU.mult, op1=ALU.add),
                  reads=["rr_kf", "ang2"], writes=["ang2"])
            sc.op("dve", lambda e: e.scalar_tensor_tensor(out=x, in0=kf, scalar=float(-C2), in1=x,
                                                          op0=ALU.mult, op1=ALU.add),
                  reads=["rr_kf", "ang2"], writes=["ang2"])
            sc.op("dve", lambda e: e.tensor_scalar(out=m, in0=x, scalar1=float(PI), scalar2=float(-2 * PI),
                                                   op0=ALU.is_gt, op1=ALU.mult), reads=["ang2"], writes=["rr_m"])
            sc.op("dve", lambda e: e.tensor_tensor(out=x, in0=x, in1=m, op=ALU.add),
                  reads=["ang2", "rr_m"], writes=["ang2"])
            sc.op("dve", lambda e: e.tensor_scalar(out=m, in0=x, scalar1=float(-PI), scalar2=float(2 * PI),
                                                   op0=ALU.is_lt, op1=ALU.mult), reads=["ang2"], writes=["rr_m"])
            sc.op("dve", lambda e: e.tensor_tensor(out=x, in0=x, in1=m, op=ALU.add),
                  reads=["ang2", "rr_m"], writes=["ang2"])
            sc.op("act", lambda e: e.activation(out=dst, in_=x, func=AF.Sin), reads=["ang2"], writes=["ropetab"])

        def rope_tables(cos_t, sin_t, freqs, name):
            nf = len(freqs)
            for i, f in enumerate(freqs):
                sc.op("dve", lambda e, i=i, f=f: e.tensor_scalar(
                    out=angt[:, :, i], in0=posf, scalar1=float(f), scalar2=None, op0=ALU.mult),
                    reads=["posf", "ropetab", "ang2"], writes=[("angt", i)])
            reduced_sin(sin_t, nf, 0.0)
            reduced_sin(cos_t, nf, 0.5 * PI)

        rope_tables(cosP, sinP, inv_freq_p, "ropeP")
        rope_tables(cosM, sinM, inv_freq_m, "ropeM")
        sc.barrier()
        tmpst.close()

        def norm_part1(ph, i, src_d, xt, hn, ss, rs, ldq="sp"):
            r = i % 2
            sc.dma(ldq, xt[r], src_d[i * 128:(i + 1) * 128, :], reads=[("dram", src_d.tensor.name, i)],
                   writes=[(ph, "xt", r)])
            sc.op("act", lambda e: e.activation(out=hn[r], in_=xt[r], func=AF.Square, accum_out=ss[r]),
                  reads=[(ph, "xt", r)], writes=[(ph, "hn", r), (ph, "ss", r)])
            sc.op("act", lambda e: e.activation(out=rs[r], in_=ss[r], func=AF.Sqrt, bias=epsc[:, 0:1]),
                  reads=[(ph, "ss", r)], writes=[(ph, "rs", r)])
            sc.op("dve", lambda e: e.reciprocal(out=rs[r], in_=rs[r]),
                  reads=[(ph, "rs", r)], writes=[(ph, "rs", r)])
            sc.op("act", lambda e: e.activation(out=hn[r], in_=xt[r], func=AF.Copy, scale=rs[r]),
                  reads=[(ph, "xt", r), (ph, "rs", r)], writes=[(ph, "hn", r)])

        def norm_part2(ph, i, hn, hT, gcol, tb):
            r = i % 2
            tbv = bank_bf(tb)
            for kc in range(8):
                sc.op("pe", lambda e, kc=kc: e.transpose(out=tbv[:, kc * 128:(kc + 1) * 128],
                                                         in_=hn[r][:, kc * 128:(kc + 1) * 128], identity=ident),
                      reads=[(ph, "hn", r), "ident"], writes=[("ps", tb)], obs=(kc == 7))
            sc.op("dve", lambda e: e.tensor_tensor(
                out=hT[r], in0=tbv.rearrange("p (c t) -> p c t", c=8),
                in1=mkap(gcol, 0, [[8, 128], [1, 8], [0, 128]]), op=ALU.mult),
                reads=[("ps", tb), (ph, "gcol")], writes=[(ph, "hT", r)])

        def norm_tile(ph, i, src_d, xt, hn, ss, rs, hT, gcol, tb, ldq="sp"):
            norm_part1(ph, i, src_d, xt, hn, ss, rs, ldq)
            norm_part2(ph, i, hn, hT, gcol, tb)

        def load_gcol(ph, gcol, norm_row_ap, mult):
            sc.dma("sp", gcol, norm_row_ap, writes=[(ph, "gcol")])
            sc.op("dve", lambda e: e.tensor_scalar(out=gcol, in0=gcol, scalar1=float(mult), scalar2=None,
                                                   op0=ALU.mult),
                  reads=[(ph, "gcol")], writes=[(ph, "gcol")])

        def rope_free(ph, src, dst, nslot, half, cos_ap, sin_ap, tmp, r, src_tok, dst_tok):
            cb = mkap(cos_ap, 0, [list(cos_ap.ap[0]), [0, nslot], [1, half]])
            sb = mkap(sin_ap, 0, [list(sin_ap.ap[0]), [0, nslot], [1, half]])
            x1 = src[:, :, 0:half]
            x2 = src[:, :, half:2 * half]
            n = nslot * half
            t = [tmp[r][:, k, 0:n].rearrange("p (s h) -> p s h", s=nslot) for k in range(4)]
            tt = [(ph, "ropetmp", r, k) for k in range(4)]
            sc.op("dve", lambda e: e.tensor_tensor(out=t[0], in0=x1, in1=cb, op=ALU.mult),
                  reads=src_tok + ["ropeP", "ropeM"], writes=[tt[0]])
            sc.op("dve", lambda e: e.tensor_tensor(out=t[1], in0=x2, in1=sb, op=ALU.mult),
                  reads=src_tok, writes=[tt[1]])
            sc.op("dve", lambda e: e.tensor_tensor(out=t[2], in0=x2, in1=cb, op=ALU.mult),
                  reads=src_tok, writes=[tt[2]])
            sc.op("dve", lambda e: e.tensor_tensor(out=t[3], in0=x1, in1=sb, op=ALU.mult),
                  reads=src_tok, writes=[tt[3]])
            sc.op("dve", lambda e: e.tensor_tensor(out=dst[:, :, 0:half], in0=t[0], in1=t[1], op=ALU.subtract),
                  reads=[tt[0], tt[1]], writes=dst_tok)
            sc.op("dve", lambda e: e.tensor_tensor(out=dst[:, :, half:2 * half], in0=t[2], in1=t[3], op=ALU.add),
                  reads=[tt[2], tt[3]], writes=dst_tok)

        def phase_P0():
            ph = "P0"
            with ExitStack() as st:
                def sb(name, shape, dt):
                    return st.enter_context(nc.sbuf_tensor(ph + name, shape, dt))[:]
                win = sb("win", [128, 8, 3072], BF16)
                gcol = sb("gcol", [128, 8], F32)
                xt = [sb(f"xt{r}", [128, D], F32) for r in range(2)]
                hn = [sb(f"hn{r}", [128, D], BF16) for r in range(2)]
                ss = [sb(f"ss{r}", [128, 1], F32) for r in range(2)]
                rs = [sb(f"rs{r}", [128, 1], F32) for r in range(2)]
                hT = [sb(f"hT{r}", [128, 8, 128], BF16) for r in range(2)]
                qk = [sb(f"qk{r}", [128, 2048], BF16) for r in range(2)]
                vs = [sb(f"vs{r}", [128, 1024], BF16) for r in range(2)]
                tmp = [sb(f"tmp{r}", [128, 4, 256], F32) for r in range(2)]
                stage = [sb(f"stage{r}", [128, 16, 512], BF16) for r in range(2)]
                ropein = [sb(f"ropein{r}", [128, 32, 16], F32) for r in range(2)]
                for kc in range(8):
                    sc.dma("pool", win[:, kc, :], hyb_w_in_d[kc * 128:(kc + 1) * 128, :], writes=[(ph, "win", kc)])
                load_gcol(ph, gcol, attn_norm_d[0], math.sqrt(D))
                ntile = int(os.environ.get("DBG_NT", NT))

                def stage_A1(i):
                    norm_part1(ph, i, x_d, xt, hn, ss, rs, ldq="pool")

                def stage_A2(i):
                    norm_part2(ph, i, hn, hT, gcol, 0)

                def stage_MM(i):
                    r = i % 2
                    for kc in range(8):
                        for n in range(6):
                            sc.op("pe", lambda e, kc=kc, n=n: e.matmul(
                                bank(2 + n), lhsT=hT[r][:, kc, :], rhs=win[:, kc, n * 512:(n + 1) * 512],
                                start=(kc == 0), stop=(kc == 7)),
                                reads=[(ph, "hT", r), (ph, "win", kc)], writes=[("ps", 2 + n)], obs=(kc == 7))

                def stage_B(i):
                    r = i % 2
                    rbanks = [(2, 0), (3, 512), (5, 1024), (6, 1536)]
                    for bi, (b0, off) in enumerate(rbanks):
                        src = bank(b0).rearrange("p (s d) -> p s d", d=64)[:, :, 0:16]
                        sc.op("dve", lambda e, bi=bi, src=src: e.tensor_copy(out=ropein[r][:, bi * 8:(bi + 1) * 8, :], in_=src),
                              reads=[("ps", b0)], writes=[(ph, "ropein", r)])
                    for n, (dst, off) in enumerate([(qk, 0), (qk, 512), (vs, 0), (qk, 1024), (qk, 1536), (vs, 512)]):
                        tok = (ph, "qk", r) if dst is qk else (ph, "vs", r)
                        if n % 2 == 0:
                            sc.op("act", lambda e, n=n, dst=dst, off=off: e.activation(
                                out=dst[r][:, off:off + 512], in_=bank(2 + n), func=AF.Copy),
                                reads=[("ps", 2 + n)], writes=[tok])
                        else:
                            sc.op("dve", lambda e, n=n, dst=dst, off=off: e.tensor_copy(
                                out=dst[r][:, off:off + 512], in_=bank(2 + n)),
                                reads=[("ps", 2 + n)], writes=[tok])
                    dst = qk[r].rearrange("p (s d) -> p s d", d=64)[:, :, 0:16]
                    rope_free(ph, ropein[r], dst, 32, 8, cosP[:, i, :], sinP[:, i, :], tmp, r,
                              [(ph, "ropein", r)], [(ph, "qk", r)])
                    sc.dma("sp", v0_d[i * 128:(i + 1) * 128, :], vs[r], reads=[(ph, "vs", r)],
                           writes=[("dram", "v0", i)])

                def stage_C(i):
                    r = i % 2
                    sr = (i // 4) % 2
                    for hb in (1, 0):
                        tbv = bank_bf(hb)
                        for c in range(8):
                            ch = hb * 8 + c
                            sc.op("pe", lambda e, c=c, ch=ch, tbv=tbv: e.transpose(
                                out=tbv[:, c * 128:(c + 1) * 128], in_=qk[r][:, ch * 128:(ch + 1) * 128],
                                identity=ident),
                                reads=[(ph, "qk", r), "ident"], writes=[("ps", hb)], obs=(c == 7))
                        dstv = stage[sr][:, hb * 8:(hb + 1) * 8, (i % 4) * 128:(i % 4 + 1) * 128]
                        srcv = tbv.rearrange("p (c t) -> p c t", c=8)
                        if hb == 0:
                            sc.op("act", lambda e, dstv=dstv, srcv=srcv: e.activation(out=dstv, in_=srcv, func=AF.Copy),
                                  reads=[("ps", hb)], writes=[(ph, "stage", sr)])
                        else:
                            sc.op("dve", lambda e, dstv=dstv, srcv=srcv: e.tensor_copy(out=dstv, in_=srcv),
                                  reads=[("ps", hb)], writes=[(ph, "stage", sr)])
                    if i % 4 == 3:
                        c0 = (i // 4) * 512
                        sc.dma("sp", qkT0_d[:, :, c0:c0 + 512].rearrange("c p t -> p c t"), stage[sr],
                               reads=[(ph, "stage", sr)], writes=[("dram", "qkT0", i // 4)])

                stage_A1(0)
                stage_A2(0)
                if ntile > 1:
                    stage_A1(1)
                for i in range(ntile):
                    if i + 2 < ntile:
                        stage_A1(i + 2)
                    stage_MM(i)
                    if i + 1 < ntile:
                        stage_A2(i + 1)
                    stage_B(i)
                    if i >= 1:
                        stage_C(i - 1)
                stage_C(ntile - 1)
                sc.barrier()

        class AttnCtx:
            pass

        def attention_units(c, causal_only):
            res = []
            if causal_only:
                for kb in range(0, 4 * c + 4):
                    o = kb - 4 * c
                    if o < 0:
                        res.append((kb, None, 0, 512))
                    else:
                        res.append((kb, 20 + o, 128 * o, 512))
            else:
                full, part = [], []
                for kb in range(max(0, 4 * c - 16), 4 * c + 4):
                    o = kb - 4 * c
                    c0 = 128 * o if o > 0 else 0
                    c1 = min(512, 2048 + 128 * o + 128)
                    (full if (c0 == 0 and c1 == 512) else part).append((kb, o + 16, c0, c1))
                res = full + part
            assert res[0][2] == 0 and res[0][3] == 512
            return res

        fin_pending = []

        def norm64_pieces(ph, ab, k, recip, dst_fn):
            for j in range(4):
                cs = slice(j * 128, (j + 1) * 128)
                fin_pending.append((ab, lambda cs=cs, j=j: sc.op(
                    "dve", lambda e: e.reciprocal(out=recip[k][0:64, cs], in_=ps[64:128, ab, cs]),
                    reads=[("ps", ab)], writes=[(ph, "recip", k, j)])))
            for j2 in range(2):
                cs = slice(j2 * 256, (j2 + 1) * 256)
                fin_pending.append((ab, lambda cs=cs, j2=j2: sc.op(
                    "dve", lambda e: e.tensor_tensor(out=dst_fn(cs), in0=ps[0:64, ab, cs], in1=recip[k][0:64, cs],
                                                     op=ALU.mult),
                    reads=[("ps", ab), (ph, "recip", k, 2 * j2), (ph, "recip", k, 2 * j2 + 1)],
                    writes=[(ph, "ostw", k, j2)])))

        def run_attention(ph, streams, masks_sb, pbuf, scale, finalize, sbanks=(0, 1, 2, 3)):
            units = []
            for (grp, c, items) in streams:
                for it_i, it in enumerate(items):
                    ul = it["units"](c)
                    for ui, (kb, mid, c0, c1) in enumerate(ul):
                        units.append(dict(it=it, c=c, kb=kb, mid=mid, c0=c0, c1=c1, first=(ui == 0), last=(ui == len(ul) - 1),
                                          fin=(ui == len(ul) - 1 and it_i == len(items) - 1), grp=grp, items=items))
            LA = len(sbanks) - 1
            NP = len(pbuf)
            n = len(units)
            pending = fin_pending

            def flush_bank(pb):
                keep = []
                for (bk, fn_) in pending:
                    if bk == pb:
                        fn_()
                    else:
                        keep.append((bk, fn_))
                pending[:] = keep

            for i in range(n + LA):
                if i < n:
                    u = units[i]
                    it = u["it"]
                    sbk = sbanks[i % len(sbanks)]
                    pi = i % NP
                    c = u["c"]
                    kb = u["kb"]
                    c0, c1 = u["c0"], u["c1"]
                    sc.op("pe", lambda e, it=it, c=c, kb=kb, sbk=sbk, c0=c0, c1=c1: e.matmul(
                        ps[:, sbk, c0:c1], lhsT=it["kT"][:, kb * 128:(kb + 1) * 128],
                        rhs=it["qT"][:, c * 512 + c0:c * 512 + c1], start=True, stop=True),
                        reads=it["toks"], writes=[("ps", sbk)])
                    sc.op("act", lambda e, sbk=sbk, pi=pi, c0=c0, c1=c1: e.activation(
                        out=pbuf[pi][:, c0:c1], in_=ps[:, sbk, c0:c1], func=AF.Exp, scale=float(scale)),
                        reads=[("ps", sbk)], writes=[(ph, "pbuf", pi)])
                    if u["mid"] is not None:
                        mid = u["mid"]
                        sc.op("pool" if (i % 3 == 2) else "dve", lambda e, pi=pi, mid=mid, c0=c0, c1=c1: e.tensor_tensor(
                            out=pbuf[pi][:, c0:c1], in0=pbuf[pi][:, c0:c1], in1=masks_sb[:, mid, c0:c1], op=ALU.mult),
                            reads=[(ph, "pbuf", pi), (ph, "masks")], writes=[(ph, "pbuf", pi)])
                    if pending:
                        pending.pop(0)[1]()
                j = i - LA
                if j >= 0:
                    u = units[j]
                    it = u["it"]
                    pi = j % NP
                    kb = u["kb"]
                    c0, c1 = u["c0"], u["c1"]
                    for (lfn, pb) in it["pv"]:
                        if u["first"]:
                            flush_bank(pb)
                        sc.op("pe", lambda e, lfn=lfn, pb=pb, kb=kb, pi=pi, u=u, c0=c0, c1=c1: e.matmul(
                            ps[:, pb, c0:c1], lhsT=lfn(kb), rhs=pbuf[pi][:, c0:c1], start=u["first"], stop=u["last"]),
                            reads=[(ph, "pbuf", pi)] + it["vtoks"], writes=[("ps", pb)], obs=u["last"])
                    if u["last"] and it.get("fin_item") is not None:
                        it["fin_item"](u["c"], it)
                    if u["fin"]:
                        finalize(u["grp"], u["c"], u["items"])
            while pending:
                pending.pop(0)[1]()

        def phase_A0():
            ph = "A0"
            with ExitStack() as st:
                def sb(name, shape, dt):
                    return st.enter_context(nc.sbuf_tensor(ph + name, shape, dt))[:]
                masks_sb = sb("masks", [128, 24, 512], BF16)
                pbuf = [sb(f"pbuf{k}", [128, 512], BF16) for k in range(8)]
                qTc = [sb(f"qT{r}", [128, S], BF16) for r in range(2)]
                kz = [[sb(f"kz{r}{e_}", [128, S], BF16) for e_ in range(2)] for r in range(2)]
                vaug = [sb(f"va{r}", [128, NT, 2, 128], BF16) for r in range(2)]
                ost = [sb(f"ost{r}", [128, S], BF16) for r in range(2)]
                recip = [sb(f"rc{r}", [128, 512], F32) for r in range(2)]
                o1 = sb("o1", [128, 512], F32)
                t2 = sb("t2", [128, 512], F32)
                r2 = sb("r2", [128, 512], F32)
                sq = sb("sq", [128, 512], F32)
                lam_in = sb("lam_in", [128, 4, 64], F32)
                lam_p = sb("lam_p", [128, 2, 64], F32)
                lam_s = sb("lam_s", [128, 2], F32)
                lam_e = sb("lam_e", [128, 2], F32)
                neglam = sb("neglam", [128, 1], F32)
                sgain = sb("sgain", [128, 1], F32)

                sc.dma("sp", masks_sb, masks_d.rearrange("m p t -> p m t"), writes=[(ph, "masks")])
                for r in range(2):
                    sc.op("pool", lambda e, r=r: e.memset(vaug[r][:, :, :, 64:128], 1.0), writes=[(ph, "vaug1", r)])
                    sc.op("pool", lambda e, r=r: e.memset(kz[r][0][64:128, :], 0.0), writes=[(ph, "kzz", r, 0)])
                    sc.op("pool", lambda e, r=r: e.memset(kz[r][1][0:64, :], 0.0), writes=[(ph, "kzz", r, 1)])
                for k, d_ in enumerate([lq1_d, lk1_d, lq2_d, lk2_d]):
                    sc.dma("sp", lam_in[:, k, :], mkap(d_, 0, [[0, 128], [1, 64]]), writes=[(ph, "lam_in")])
                sc.dma("sp", sgain, mkap(subln_d, 0, [[1, 128], [1, 1]]), writes=[(ph, "sgain")])
                sc.op("dve", lambda e: e.tensor_tensor(out=lam_p[:, 0, :], in0=lam_in[:, 0, :], in1=lam_in[:, 1, :],
                                                       op=ALU.mult), reads=[(ph, "lam_in")], writes=[(ph, "lam_p")])
                sc.op("dve", lambda e: e.tensor_tensor(out=lam_p[:, 1, :], in0=lam_in[:, 2, :], in1=lam_in[:, 3, :],
                                                       op=ALU.mult), reads=[(ph, "lam_in")], writes=[(ph, "lam_p")])
                sc.op("dve", lambda e: e.reduce_sum(out=lam_s, in_=lam_p, axis=AX.X),
                      reads=[(ph, "lam_p")], writes=[(ph, "lam_s")])
                sc.op("act", lambda e: e.activation(out=lam_e, in_=lam_s, func=AF.Exp),
                      reads=[(ph, "lam_s")], writes=[(ph, "lam_e")])
                sc.op("dve", lambda e: e.tensor_tensor(out=neglam, in0=lam_e[:, 1:2], in1=lam_e[:, 0:1],
                                                       op=ALU.subtract), reads=[(ph, "lam_e")], writes=[(ph, "neglam")])
                sc.op("dve", lambda e: e.tensor_scalar(out=neglam, in0=neglam, scalar1=-0.2, scalar2=None,
                                                       op0=ALU.add), reads=[(ph, "neglam")], writes=[(ph, "neglam")])
                sc.op("dve", lambda e: e.tensor_scalar(out=sgain, in0=sgain, scalar1=float(math.sqrt(128.0) * 0.8),
                                                       scalar2=None, op0=ALU.mult),
                      reads=[(ph, "sgain")], writes=[(ph, "sgain")])

                for pr in range(4):
                    r = pr % 2
                    sc.dma("sp", qTc[r], qkT0_d[pr], reads=[("dram", "qkT0", k) for k in range(8)],
                           writes=[(ph, "qT", r)])
                    for e_ in range(2):
                        sc.dma("sp", kz[r][e_][e_ * 64:(e_ + 1) * 64, :], qkT0_d[4 + pr, e_ * 64:(e_ + 1) * 64, :],
                               reads=[("dram", "qkT0", k) for k in range(8)], writes=[(ph, "kT", r, e_)])
                    for e_ in range(2):
                        h = pr * 2 + e_
                        sc.dma("sp", vaug[r][:, :, e_, 0:64],
                               v0_d[:, h * 64:(h + 1) * 64].rearrange("(t p) d -> p t d", p=128),
                               reads=[("dram", "v0", k) for k in range(NT)], writes=[(ph, "vaug", r, e_)])
                    streams = []
                    for e_ in range(2):
                        rows = slice(e_ * 64, (e_ + 1) * 64)
                        for c in range(NCH):
                            ab = 6 + (len(streams) % 2)
                            it = dict(qT=qTc[r], kT=kz[r][e_],
                                      pv=[((lambda kb, r=r, e_=e_: vaug[r][:, kb, e_, :]), ab)],
                                      units=lambda c: attention_units(c, False),
                                      toks=[(ph, "qT", r), (ph, "kT", r, e_), (ph, "kzz", r, e_)],
                                      vtoks=[(ph, "vaug", r, e_), (ph, "vaug1", r)], ab=ab, e=e_, r=r)
                            streams.append(((pr, e_), c, [it]))

                    def fin_a(grp, c, items):
                        it = items[0]
                        ab = it["ab"]
                        e_ = it["e"]
                        rr = it["r"]
                        k = ab - 6
                        norm64_pieces(ph, ab, k, recip,
                                      lambda cs: ost[rr][e_ * 64:(e_ + 1) * 64, c * 512 + cs.start:c * 512 + cs.stop])
                    run_attention(ph, streams, masks_sb, pbuf, 0.125, fin_a, sbanks=(0, 1, 2, 3, 4, 5))
                    sc.dma("pool", oT_d[pr], ost[r], reads=[(ph, "ost", r)] + [(ph, "ostw", k_, j_) for k_ in range(2) for j_ in range(2)],
                           writes=[("dram", "oT", pr)])

                for h in range(4):
                    r = h % 2
                    sc.dma("sp", qTc[r], qkT0_d[8 + h], reads=[("dram", "qkT0", k) for k in range(8)],
                           writes=[(ph, "qT", r)])
                    for m in range(2):
                        sc.dma("sp", kz[r][m][m * 64:(m + 1) * 64, :], qkT0_d[12 + h, m * 64:(m + 1) * 64, :],
                               reads=[("dram", "qkT0", k) for k in range(8)], writes=[(ph, "kT", r, m)])
                    vfull = vaug[r][:, :, 0, :]
                    sc.dma("sp", vfull, v0_d[:, 512 + h * 128:512 + (h + 1) * 128].rearrange("(t p) d -> p t d", p=128),
                           reads=[("dram", "v0", k) for k in range(NT)], writes=[(ph, "vaug", r, 0)])
                    streams = []
                    for c in range(NCH):
                        items = []
                        for m in range(2):
                            rows = slice(m * 64, (m + 1) * 64)
                            items.append(dict(qT=qTc[r], kT=kz[r][m],
                                              pv=[((lambda kb, r=r: vaug[r][:, kb, 0, :]), 4 + 2 * m),
                                                  ((lambda kb: ones_bf), 5 + 2 * m)],
                                              units=lambda c: attention_units(c, True),
                                              toks=[(ph, "qT", r), (ph, "kT", r, m), (ph, "kzz", r, m)],
                                              vtoks=[(ph, "vaug", r, 0), "ones_bf"], r=r, m=m))
                        streams.append((h, c, items))

                    def fin_item_d(c, it):
                        m = it["m"]
                        nb, lb = 4 + 2 * m, 5 + 2 * m
                        rc = recip[0] if m == 0 else r2
                        dst = o1 if m == 0 else t2
                        sc.op("dve", lambda e: e.reciprocal(out=rc, in_=bank(lb)),
                              reads=[("ps", lb)], writes=[(ph, "rc", m)])
                        sc.op("dve", lambda e: e.tensor_tensor(out=dst, in0=bank(nb), in1=rc, op=ALU.mult),
                              reads=[("ps", nb), (ph, "rc", m)], writes=[(ph, "on", m)])

                    def fin_d(grp, c, items):
                        rr = items[0]["r"]
                        sc.op("dve", lambda e: e.scalar_tensor_tensor(out=o1, in0=t2, scalar=neglam, in1=o1,
                                                                      op0=ALU.mult, op1=ALU.add),
                              reads=[(ph, "on", 1), (ph, "on", 0), (ph, "neglam")], writes=[(ph, "on", 0)])
                        sc.op("act", lambda e: e.activation(out=sq, in_=o1, func=AF.Square),
                              reads=[(ph, "on", 0)], writes=[(ph, "sq")])
                        sc.op("pe", lambda e: e.matmul(bank(7), lhsT=ones_f, rhs=sq, start=True, stop=True),
                              reads=[(ph, "sq"), "ones_f"], writes=[("ps", 7)])
                        sc.op("act", lambda e: e.activation(out=r2, in_=bank(7), func=AF.Ln, bias=epsc[:, 2:3]),
                              reads=[("ps", 7)], writes=[(ph, "rc", 1)])
                        sc.op("act", lambda e: e.activation(out=r2, in_=r2, func=AF.Exp, scale=-0.5),
                              reads=[(ph, "rc", 1)], writes=[(ph, "rc", 1)])
                        sc.op("dve", lambda e: e.scalar_tensor_tensor(
                            out=ost[rr][:, c * 512:(c + 1) * 512], in0=o1, scalar=sgain, in1=r2,
                            op0=ALU.mult, op1=ALU.mult),
                            reads=[(ph, "on", 0), (ph, "rc", 1), (ph, "sgain")], writes=[(ph, "ost", rr)])
                    for (_g, _c, _items) in streams:
                        for _it in _items:
                            _it["fin_item"] = fin_item_d
                    run_attention(ph, streams, masks_sb, pbuf, 0.125, fin_d)
                    sc.dma("pool", oT_d[4 + h], ost[r], reads=[(ph, "ost", r)], writes=[("dram", "oT", 4 + h)])
                sc.barrier()

        def phase_O(ph, w_out_ap, xsrc_d, after_loads=None, ss_all=None):
            with ExitStack() as st:
                def sb(name, shape, dt):
                    return st.enter_context(nc.sbuf_tensor(ph + name, shape, dt))[:]
                sqj = sb("sqj", [128, D], BF16)
                wout = sb("wout", [128, 8, D], BF16)
                oTc = [sb(f"oTc{r}", [128, 8, 512], BF16) for r in range(2)]
                xt = [sb(f"xt{r}", [128, D], F32) for r in range(2)]
                for kc in range(8):
                    sc.dma("pool", wout[:, kc, :], w_out_ap[kc * 128:(kc + 1) * 128, :], writes=[(ph, "wout", kc)])
                if after_loads is not None:
                    after_loads()
                def load_oT(c):
                    sc.dma("sp", oTc[c % 2], oT_d[:, :, c * 512:(c + 1) * 512].rearrange("c p t -> p c t"),
                           reads=[("dram", "oT", k) for k in range(8)], writes=[(ph, "oTc", c % 2)])

                load_oT(0)
                for c in range(NCH):
                    cr = c % 2
                    if c + 1 < NCH:
                        load_oT(c + 1)
                    for tt in range(4):
                        i = c * 4 + tt
                        r = i % 2
                        sc.dma("sp", xt[r], xsrc_d[i * 128:(i + 1) * 128, :],
                               reads=[("dram", xsrc_d.tensor.name, i)], writes=[(ph, "xt", r)])
                        for kc in range(8):
                            for hf in range(2):
                                sc.op("pe", lambda e, kc=kc, hf=hf: e.matmul(
                                    bank(2 * r + hf), lhsT=oTc[cr][:, kc, tt * 128:(tt + 1) * 128],
                                    rhs=wout[:, kc, hf * 512:(hf + 1) * 512], start=(kc == 0), stop=(kc == 7)),
                                    reads=[(ph, "oTc", cr), (ph, "wout", kc)], writes=[("ps", 2 * r + hf)],
                                    obs=(kc == 7))
                        for hf in range(2):
                            sc.op("dve", lambda e, hf=hf: e.tensor_tensor(
                                out=xt[r][:, hf * 512:(hf + 1) * 512], in0=bank(2 * r + hf),
                                in1=xt[r][:, hf * 512:(hf + 1) * 512], op=ALU.add),
                                reads=[("ps", 2 * r + hf), (ph, "xt", r)], writes=[(ph, "xt", r)])
                        sc.dma("act", xres_d[i * 128:(i + 1) * 128, :], xt[r], reads=[(ph, "xt", r)],
                               writes=[("dram", "xres", i)])
                        if ss_all is not None:
                            sc.op("act", lambda e: e.activation(out=sqj, in_=xt[r], func=AF.Square,
                                                                accum_out=ss_all[:, i:i + 1]),
                                  reads=[(ph, "xt", r)], writes=[(ph, "sqj"), (ph, "ssall")])
                sc.barrier()

        def phase_F(ph, L, final, opre=None):
            with ExitStack() as st:
                def sb(name, shape, dt):
                    return st.enter_context(nc.sbuf_tensor(ph + name, shape, dt))[:]
                wup = sb("wup", [128, 8, 2 * FFN], BF16)
                wdn = sb("wdn", [128, 22, D], BF16)

                def load_weights():
                    for kc in range(8):
                        for hh in range(2):
                            sc.dma("pool", wup[:, kc, hh * FFN:(hh + 1) * FFN],
                                   ffn_w_up_d[L, kc * 128:(kc + 1) * 128, hh * FFN:(hh + 1) * FFN],
                                   writes=[(ph, "wup", kc, hh)])
                    for j in range(22):
                        sc.dma("pool", wdn[:, j, :], ffn_w_down_d[L, j * 128:(j + 1) * 128, :],
                               writes=[(ph, "wdn", j)])
                ss_all = sb("ssall", [128, NT], F32)
                rs_all = sb("rsall", [128, NT], F32)
                assert opre is not None
                phase_O(*opre, after_loads=load_weights, ss_all=ss_all)
                sc.op("act", lambda e: e.activation(out=rs_all, in_=ss_all, func=AF.Sqrt, bias=epsc[:, 0:1]),
                      writes=[(ph, "rsall")])
                sc.op("dve", lambda e: e.reciprocal(out=rs_all, in_=rs_all), reads=[(ph, "rsall")], writes=[(ph, "rsall")])
                gcol = sb("gcol", [128, 8], F32)
                cw = sb("cw", [128, 3, 44], F32)
                cb = sb("cb", [128, 44], F32)
                halo = sb("halo", [128, 2, 44, 2], F32)
                xt = [sb(f"xt{r}", [128, D], F32) for r in range(2)]
                xe = xt
                hn = [sb(f"hn{r}", [128, D], BF16) for r in range(2)]
                ss = [sb(f"ss{r}", [128, 1], F32) for r in range(2)]
                rs = [sb(f"rs{r}", [128, 1], F32) for r in range(2)]
                hTt = [sb(f"hTt{r}", [128, 8, 128], BF16) for r in range(2)]
                hTc = [sb(f"hTc{q}", [128, 8, 512], BF16) for q in range(2)]
                aT = sb("aT", [128, 22, 512], BF16)
                acc = [sb(f"acc{k}", [128, 512], F32) for k in range(4)]
                ht = [sb(f"ht{k}", [128, 4], F32) for k in range(4)]
                if final:
                    gfin = sb("gfin", [128, D], F32)
                    sc.dma("sp", gfin, mkap(final_norm_d, 0, [[0, 128], [1, D]]), writes=[(ph, "gfin")])
                load_gcol(ph, gcol, ffn_norm_d[L], math.sqrt(D))
                sc.dma("sp", cw, ffn_conv_w_d[L], writes=[(ph, "cw")])
                sc.dma("sp", cb, ffn_conv_b_d[L], writes=[(ph, "cb")])
                sc.op("pool", lambda e: e.memset(halo, 0.0),
                      writes=[(ph, "halo", hp, m) for m in range(44) for hp in range(2)])
                nchunk = int(os.environ.get("DBG_FCH", NCH))

                def emit_norm1(cn, tt):
                    i = cn * 4 + tt
                    r = i % 2
                    sc.dma("sp", xt[r], xres_d[i * 128:(i + 1) * 128, :], reads=[("dram", "xres", i)],
                           writes=[(ph, "xt", r)])
                    sc.op("act", lambda e: e.activation(out=hn[r], in_=xt[r], func=AF.Copy, scale=rs_all[:, i:i + 1]),
                          reads=[(ph, "xt", r), (ph, "rsall")], writes=[(ph, "hn", r)])

                def emit_norm2(cn, tt):
                    i = cn * 4 + tt
                    r = i % 2
                    norm_part2(ph, i, hn, hTt, gcol, 0)
                    sc.op("pool", lambda e: e.tensor_copy(out=hTc[cn % 2][:, :, tt * 128:(tt + 1) * 128], in_=hTt[r]),
                          reads=[(ph, "hT", r)], writes=[(ph, "hTc", cn % 2)])

                for tt in range(4):
                    emit_norm1(0, tt)
                    emit_norm2(0, tt)
                ybank = [7, 0]
                for c in range(nchunk):
                    hq = c % 2
                    for j in range(22):
                        pr = j % 2
                        pr3 = j % 3
                        if j in (1, 6, 11, 16) and c + 1 < nchunk:
                            emit_norm1(c + 1, (j - 1) // 5)
                        if j in (4, 9, 14, 19) and c + 1 < nchunk:
                            emit_norm2(c + 1, (j - 4) // 5)
                        for gv in range(2):
                            m = j + 22 * gv
                            b = 1 + 2 * pr3 + gv
                            for kc in range(8):
                                sc.op("pe", lambda e, kc=kc, m=m, b=b: e.matmul(
                                    bank(b), lhsT=wup[:, kc, m * 128:(m + 1) * 128], rhs=hTc[hq][:, kc, :],
                                    start=(kc == 0), stop=(kc == 7)),
                                    reads=[(ph, "hTc", hq), (ph, "wup", kc, gv)], writes=[("ps", b)], obs=(kc == 7))
                        hw_, hr_ = c % 2, (c + 1) % 2
                        for gv in range(2):
                            m = j + 22 * gv
                            b = 1 + 2 * pr3 + gv
                            k = 2 * pr + gv
                            sc.op("act", lambda e, k=k, b=b, m=m: e.activation(
                                out=acc[k], in_=bank(b), func=AF.Identity, scale=cw[:, 2, m:m + 1], bias=cb[:, m:m + 1]),
                                reads=[("ps", b), (ph, "cw"), (ph, "cb")], writes=[(ph, "acc", k)])
                            sc.op("act", lambda e, b=b, m=m: e.activation(out=halo[:, hw_, m, :], in_=ps[:, b, 510:512],
                                                                          func=AF.Copy),
                                  reads=[("ps", b)], writes=[(ph, "halo", hw_, m)])
                            sc.op("dve", lambda e, k=k, m=m, b=b: e.scalar_tensor_tensor(
                                out=acc[k][:, 1:512], in0=ps[:, b, 0:511], scalar=cw[:, 1, m:m + 1], in1=acc[k][:, 1:512],
                                op0=ALU.mult, op1=ALU.add),
                                reads=[("ps", b), (ph, "acc", k), (ph, "cw")], writes=[(ph, "acc", k)])
                            sc.op("dve", lambda e, k=k, m=m, b=b: e.scalar_tensor_tensor(
                                out=acc[k][:, 2:512], in0=ps[:, b, 0:510], scalar=cw[:, 0, m:m + 1], in1=acc[k][:, 2:512],
                                op0=ALU.mult, op1=ALU.add),
                                reads=[("ps", b), (ph, "acc", k), (ph, "cw")], writes=[(ph, "acc", k)])
                            sc.op("pool", lambda e, k=k, m=m: e.tensor_scalar(
                                out=ht[k][:, 0:2], in0=halo[:, hr_, m, 0:2], scalar1=cw[:, 0, m:m + 1], scalar2=None,
                                op0=ALU.mult),
                                reads=[(ph, "halo", hr_, m), (ph, "cw")], writes=[(ph, "ht", k)])
                            sc.op("pool", lambda e, k=k, m=m: e.tensor_scalar(
                                out=ht[k][:, 2:3], in0=halo[:, hr_, m, 1:2], scalar1=cw[:, 1, m:m + 1], scalar2=None,
                                op0=ALU.mult),
                                reads=[(ph, "halo", hr_, m), (ph, "cw")], writes=[(ph, "ht", k)])
                            sc.op("pool", lambda e, k=k: e.tensor_tensor(
                                out=acc[k][:, 0:2], in0=acc[k][:, 0:2], in1=ht[k][:, 0:2], op=ALU.add),
                                reads=[(ph, "ht", k), (ph, "acc", k)], writes=[(ph, "acc", k)])
                            sc.op("pool", lambda e, k=k: e.tensor_tensor(
                                out=acc[k][:, 0:1], in0=acc[k][:, 0:1], in1=ht[k][:, 2:3], op=ALU.add),
                                reads=[(ph, "ht", k), (ph, "acc", k)], writes=[(ph, "acc", k)])
                        kg = 2 * pr
                        kv = 2 * pr + 1
                        sc.op("act", lambda e, kg=kg: e.activation(out=acc[kg], in_=acc[kg], func=AF.Silu),
                              reads=[(ph, "acc", kg)], writes=[(ph, "acc", kg)])
                        sc.op("pool", lambda e, kg=kg, kv=kv, j=j: e.tensor_tensor(
                            out=aT[:, j, :], in0=acc[kg], in1=acc[kv], op=ALU.mult),
                            reads=[(ph, "acc", kg), (ph, "acc", kv)], writes=[(ph, "aT", j)])
                    for tt in range(4):
                        i = c * 4 + tt
                        r = i % 2
                        sc.dma("sp", xe[r], xres_d[i * 128:(i + 1) * 128, :], reads=[("dram", "xres", i)],
                               writes=[(ph, "xt", r)])
                        for j in range(22):
                            for hf in range(2):
                                sc.op("pe", lambda e, j=j, hf=hf: e.matmul(
                                    bank(ybank[hf]), lhsT=aT[:, j, tt * 128:(tt + 1) * 128],
                                    rhs=wdn[:, j, hf * 512:(hf + 1) * 512], start=(j == 0), stop=(j == 21)),
                                    reads=[(ph, "aT", j), (ph, "wdn", j)], writes=[("ps", ybank[hf])], obs=(j == 21))
                        for hf in range(2):
                            sc.op("dve", lambda e, hf=hf: e.tensor_tensor(
                                out=xe[r][:, hf * 512:(hf + 1) * 512], in0=bank(ybank[hf]),
                                in1=xe[r][:, hf * 512:(hf + 1) * 512], op=ALU.add),
                                reads=[("ps", ybank[hf]), (ph, "xt", r)], writes=[(ph, "xt", r)])
                        if not final:
                            sc.dma("sp", xres_d[i * 128:(i + 1) * 128, :], xe[r], reads=[(ph, "xt", r)],
                                   writes=[("dram", "xres", i)])
                        else:
                            sc.op("act", lambda e: e.activation(out=hn[r], in_=xe[r], func=AF.Square, accum_out=ss[r]),
                                  reads=[(ph, "xt", r)], writes=[(ph, "hn", r), (ph, "ss", r)])
                            sc.op("act", lambda e: e.activation(out=rs[r], in_=ss[r], func=AF.Sqrt, bias=epsc[:, 3:4],
                                                                scale=float(1.0 / D)),
                                  reads=[(ph, "ss", r)], writes=[(ph, "rs", r)])
                            sc.op("dve", lambda e: e.reciprocal(out=rs[r], in_=rs[r]),
                                  reads=[(ph, "rs", r)], writes=[(ph, "rs", r)])
                            sc.op("dve", lambda e: e.scalar_tensor_tensor(out=xe[r], in0=xe[r], scalar=rs[r], in1=gfin,
                                                                          op0=ALU.mult, op1=ALU.mult),
                                  reads=[(ph, "xt", r), (ph, "rs", r), (ph, "gfin")], writes=[(ph, "xt", r)])
                            sc.dma("sp", out_d[i * 128:(i + 1) * 128, :], xe[r], reads=[(ph, "xt", r)],
                                   writes=[("dram", "out", i)])
                sc.barrier()

        def phase_P1():
            ph = "P1"
            with ExitStack() as st:
                def sb(name, shape, dt):
                    return st.enter_context(nc.sbuf_tensor(ph + name, shape, dt))[:]
                win = sb("win", [128, 8, 416], BF16)
                wuq = sb("wuq", [128, 2, 1536], BF16)
                wukv = sb("wukv", [128, 2048], BF16)
                gcol = sb("gcol", [128, 8], F32)
                gq = sb("gq", [128, 384], F32)
                xt = [sb(f"xt{r}", [128, D], F32) for r in range(2)]
                hn = [sb(f"hn{r}", [128, D], BF16) for r in range(2)]
                ss = [sb(f"ss{r}", [128, 1], F32) for r in range(2)]
                rs = [sb(f"rs{r}", [128, 1], F32) for r in range(2)]
                hT = [sb(f"hT{r}", [128, 8, 128], BF16) for r in range(2)]
                junk = sb("junk", [128, 256], F32)
                ssl = [sb(f"ssl{r}", [128, 2], F32) for r in range(2)]
                rsl = [sb(f"rsl{r}", [128, 2], F32) for r in range(2)]
                lat = [sb(f"lat{r}", [128, 384], BF16) for r in range(2)]
                latT = [sb(f"latT{r}", [128, 3, 128], BF16) for r in range(2)]
                qs = [sb(f"qs{r}", [128, 16, 96], BF16) for r in range(2)]
                ks = [sb(f"ks{r}", [128, 16, 96], BF16) for r in range(2)]
                vs = [sb(f"vs{r}", [128, 16, 64], BF16) for r in range(2)]
                kpe = [sb(f"kpe{r}", [128, 1, 32], BF16) for r in range(2)]
                tmp = [sb(f"tmp{r}", [128, 4, 256], F32) for r in range(2)]
                qst = [sb(f"qst{r}", [128, 16, 512], BF16) for r in range(2)]
                kst = [sb(f"kst{r}", [128, 16, 512], BF16) for r in range(2)]
                for kc in range(8):
                    sc.dma("pool", win[:, kc, :], mla_w_in_d[kc * 128:(kc + 1) * 128, :], writes=[(ph, "win", kc)])
                for kc in range(2):
                    sc.dma("pool", wuq[:, kc, :], mla_w_uq_d[kc * 128:(kc + 1) * 128, :], writes=[(ph, "wuq")])
                sc.dma("pool", wukv, mla_w_ukv_d, writes=[(ph, "wukv")])
                load_gcol(ph, gcol, attn_norm_d[1], math.sqrt(D))
                sc.dma("sp", gq[:, 0:256], mkap(mla_q_norm_d, 0, [[0, 128], [1, 256]]), writes=[(ph, "gq")])
                sc.dma("sp", gq[:, 256:384], mkap(mla_kv_norm_d, 0, [[0, 128], [1, 128]]), writes=[(ph, "gq")])
                sc.op("dve", lambda e: e.tensor_scalar(out=gq[:, 0:256], in0=gq[:, 0:256], scalar1=16.0, scalar2=None,
                                                       op0=ALU.mult), reads=[(ph, "gq")], writes=[(ph, "gq")])
                sc.op("dve", lambda e: e.tensor_scalar(out=gq[:, 256:384], in0=gq[:, 256:384],
                                                       scalar1=float(math.sqrt(128.0)), scalar2=None, op0=ALU.mult),
                      reads=[(ph, "gq")], writes=[(ph, "gq")])
                kvb = [5, 6, 7, 1]
                ropq = [sb(f"ropq{r}", [128, 16, 32], F32) for r in range(2)]
                ropk = [sb(f"ropk{r}", [128, 1, 32], F32) for r in range(2)]
                ntile = int(os.environ.get("DBG_NT", NT))

                def stage_A1(i):
                    norm_part1(ph, i, xres_d, xt, hn, ss, rs, ldq="pool")

                def stage_A2(i):
                    norm_part2(ph, i, hn, hT, gcol, 0)

                def stage_B1(i):
                    r = i % 2
                    for kc in range(8):
                        sc.op("pe", lambda e, kc=kc: e.matmul(ps[:, 1, 0:416], lhsT=hT[r][:, kc, :], rhs=win[:, kc, :],
                                                             start=(kc == 0), stop=(kc == 7)),
                              reads=[(ph, "hT", r), (ph, "win", kc)], writes=[("ps", 1)], obs=(kc == 7))
                    sc.op("act", lambda e: e.activation(out=junk[:, 0:256], in_=ps[:, 1, 0:256], func=AF.Square,
                                                        accum_out=ssl[r][:, 0:1]),
                          reads=[("ps", 1)], writes=[(ph, "junk"), (ph, "ssl", r)])
                    sc.op("act", lambda e: e.activation(out=junk[:, 0:128], in_=ps[:, 1, 256:384], func=AF.Square,
                                                        accum_out=ssl[r][:, 1:2]),
                          reads=[("ps", 1)], writes=[(ph, "junk"), (ph, "ssl", r)])
                    sc.op("dve", lambda e: e.tensor_copy(out=ropk[r], in_=ps[:, 1:2, 384:416]),
                          reads=[("ps", 1)], writes=[(ph, "ropk", r)])
                    sc.op("act", lambda e: e.activation(out=rsl[r][:, 0:1], in_=ssl[r][:, 0:1], func=AF.Sqrt,
                                                        bias=epsc[:, 1:2]),
                          reads=[(ph, "ssl", r)], writes=[(ph, "rsl", r)])
                    sc.op("act", lambda e: e.activation(out=rsl[r][:, 1:2], in_=ssl[r][:, 1:2], func=AF.Sqrt,
                                                        bias=epsc[:, 2:3]),
                          reads=[(ph, "ssl", r)], writes=[(ph, "rsl", r)])
                    sc.op("dve", lambda e: e.reciprocal(out=rsl[r], in_=rsl[r]),
                          reads=[(ph, "rsl", r)], writes=[(ph, "rsl", r)])
                    sc.op("dve", lambda e: e.scalar_tensor_tensor(out=lat[r][:, 0:256], in0=ps[:, 1, 0:256],
                                                                  scalar=rsl[r][:, 0:1], in1=gq[:, 0:256],
                                                                  op0=ALU.mult, op1=ALU.mult),
                          reads=[("ps", 1), (ph, "rsl", r), (ph, "gq")], writes=[(ph, "lat", r)])
                    sc.op("dve", lambda e: e.scalar_tensor_tensor(out=lat[r][:, 256:384], in0=ps[:, 1, 256:384],
                                                                  scalar=rsl[r][:, 1:2], in1=gq[:, 256:384],
                                                                  op0=ALU.mult, op1=ALU.mult),
                          reads=[("ps", 1), (ph, "rsl", r), (ph, "gq")], writes=[(ph, "lat", r)])
                    rope_free(ph, ropk[r], kpe[r], 1, 16, cosM[:, i, :], sinM[:, i, :], tmp, r,
                              [(ph, "ropk", r)], [(ph, "kpe", r)])

                def stage_B2(i):
                    r = i % 2
                    tbv = bank_bf(0)
                    for k3 in range(3):
                        sc.op("pe", lambda e, k3=k3: e.transpose(out=tbv[:, k3 * 128:(k3 + 1) * 128],
                                                                 in_=lat[r][:, k3 * 128:(k3 + 1) * 128], identity=ident),
                              reads=[(ph, "lat", r), "ident"], writes=[("ps", 0)], obs=(k3 == 2))
                    sc.op("act", lambda e: e.activation(out=latT[r], in_=tbv[:, 0:384].rearrange("p (c t) -> p c t", c=3),
                                                        func=AF.Copy),
                          reads=[("ps", 0)], writes=[(ph, "latT", r)])
                    for n in range(3):
                        for kc in range(2):
                            sc.op("pe", lambda e, n=n, kc=kc: e.matmul(
                                bank(2 + n), lhsT=latT[r][:, kc, :], rhs=wuq[:, kc, n * 512:(n + 1) * 512],
                                start=(kc == 0), stop=(kc == 1)),
                                reads=[(ph, "latT", r), (ph, "wuq")], writes=[("ps", 2 + n)], obs=(kc == 1))
                    for n in range(4):
                        sc.op("pe", lambda e, n=n: e.matmul(bank(kvb[n]), lhsT=latT[r][:, 2, :],
                                                            rhs=wukv[:, n * 512:(n + 1) * 512], start=True, stop=True),
                              reads=[(ph, "latT", r), (ph, "wukv")], writes=[("ps", kvb[n])])
                    for (bq, h0, nh) in [(2, 0, 5), (3, 5, 5), (4, 10, 6)]:
                        qsrc_r = mkap(ps, 2 * 512 + 96 * h0 + 64, [list(ps.ap[0]), [96, nh], [1, 32]])
                        sc.op("dve", lambda e, qsrc_r=qsrc_r, h0=h0, nh=nh: e.tensor_copy(out=ropq[r][:, h0:h0 + nh, :],
                                                                                         in_=qsrc_r),
                              reads=[("ps", bq)], writes=[(ph, "ropq", r)])
                    qflat = qs[r].rearrange("p h d -> p (h d)")
                    for n in range(3):
                        sc.op("act", lambda e, n=n: e.activation(out=qflat[:, n * 512:(n + 1) * 512], in_=bank(2 + n),
                                                                 func=AF.Copy),
                              reads=[("ps", 2 + n)], writes=[(ph, "qs", r, n)])
                    for n in range(4):
                        kvv = bank(kvb[n]).rearrange("p (h d) -> p h d", d=128)
                        sc.op("act", lambda e, n=n, kvv=kvv: e.activation(out=ks[r][:, 4 * n:4 * n + 4, 0:64],
                                                                          in_=kvv[:, :, 0:64], func=AF.Copy),
                              reads=[("ps", kvb[n])], writes=[(ph, "ks", r, n)])
                        sc.op("dve", lambda e, n=n, kvv=kvv: e.tensor_copy(out=vs[r][:, 4 * n:4 * n + 4, :],
                                                                           in_=kvv[:, :, 64:128]),
                              reads=[("ps", kvb[n])], writes=[(ph, "vs", r, n)])
                    rope_free(ph, ropq[r], qs[r][:, :, 64:96], 16, 16, cosM[:, i, :], sinM[:, i, :], tmp, r,
                              [(ph, "ropq", r)], [(ph, "qs", r, n_) for n_ in range(3)])
                    sc.op("pool", lambda e: e.tensor_copy(
                        out=ks[r][:, :, 64:96], in_=mkap(kpe[r], 0, [list(kpe[r].ap[0]), [0, 16], [1, 32]])),
                        reads=[(ph, "kpe", r)], writes=[(ph, "ks", r, "pe")])
                    sc.dma("sp", v1_d[i * 128:(i + 1) * 128, :], vs[r].rearrange("p h d -> p (h d)"),
                           reads=[(ph, "vs", r, n_) for n_ in range(4)], writes=[("dram", "v1", i)])

                def stage_C(i):
                    r = i % 2
                    sr = (i // 4) % 2
                    cbanks = [0, 2, 3, 4]
                    rnd = 0
                    srctok = {"qst": [(ph, "qs", r, n_) for n_ in range(3)],
                              "kst": [(ph, "ks", r, n_) for n_ in range(4)] + [(ph, "ks", r, "pe")]}
                    for (src, dstst, nm) in [(qs, qst, "qst"), (ks, kst, "kst")]:
                        for hb in range(2):
                            tbk = cbanks[rnd]
                            rnd += 1
                            tbv = bank_bf(tbk)
                            for hh in range(8):
                                h = hb * 8 + hh
                                sc.op("pe", lambda e, h=h, hh=hh, src=src, tbv=tbv: e.transpose(
                                    out=tbv[0:96, hh * 128:(hh + 1) * 128], in_=src[r][:, h, :], identity=ident),
                                    reads=srctok[nm] + ["ident"], writes=[("ps", tbk)], obs=(hh == 7))
                            dstv = dstst[sr][0:96, hb * 8:(hb + 1) * 8, (i % 4) * 128:(i % 4 + 1) * 128]
                            srcv = tbv[0:96, :].rearrange("p (c t) -> p c t", c=8)
                            if hb == 0:
                                sc.op("act", lambda e, dstv=dstv, srcv=srcv: e.activation(out=dstv, in_=srcv, func=AF.Copy),
                                      reads=[("ps", tbk)], writes=[(ph, nm, sr)])
                            else:
                                sc.op("dve", lambda e, dstv=dstv, srcv=srcv: e.tensor_copy(out=dstv, in_=srcv),
                                      reads=[("ps", tbk)], writes=[(ph, nm, sr)])
                    if i % 4 == 3:
                        c0 = (i // 4) * 512
                        sc.dma("sp", qT1_d[:, :, c0:c0 + 512].rearrange("h p t -> p h t"), qst[sr][0:96],
                               reads=[(ph, "qst", sr)], writes=[("dram", "qT1", i // 4)])
                        sc.dma("sp", kT1_d[:, :, c0:c0 + 512].rearrange("h p t -> p h t"), kst[sr][0:96],
                               reads=[(ph, "kst", sr)], writes=[("dram", "kT1", i // 4)])

                stage_A1(0)
                stage_A2(0)
                if ntile > 1:
                    stage_A1(1)
                    stage_A2(1)
                if ntile > 2:
                    stage_A1(2)
                stage_B1(0)
                for i in range(ntile):
                    if i + 3 < ntile:
                        stage_A1(i + 3)
                    if i + 2 < ntile:
                        stage_A2(i + 2)
                    if i + 1 < ntile:
                        stage_B1(i + 1)
                    stage_B2(i)
                    if i >= 1:
                        stage_C(i - 1)
                stage_C(ntile - 1)
                sc.barrier()

        def phase_A1():
            ph = "A1"
            with ExitStack() as st:
                def sb(name, shape, dt):
                    return st.enter_context(nc.sbuf_tensor(ph + name, shape, dt))[:]
                masks_sb = sb("masks", [128, 24, 512], BF16)
                pbuf = [sb(f"pbuf{k}", [128, 512], BF16) for k in range(8)]
                qTh = [sb(f"qT{r}", [128, S], BF16) for r in range(2)]
                kTh = [sb(f"kT{r}", [128, S], BF16) for r in range(2)]
                vaug = [sb(f"va{r}", [128, NT, 128], BF16) for r in range(2)]
                ost = [sb(f"ost{r}", [128, S], BF16) for r in range(2)]
                recip = [sb(f"rc{r}", [128, 512], F32) for r in range(2)]
                sc.dma("sp", masks_sb, masks_d.rearrange("m p t -> p m t"), writes=[(ph, "masks")])
                for r in range(2):
                    sc.op("pool", lambda e, r=r: e.memset(vaug[r][:, :, 64:128], 1.0), writes=[(ph, "vaug1", r)])
                scale = 96.0 ** -0.5
                for h in range(int(os.environ.get("DBG_NH", 16))):
                    r = h % 2
                    pr = h // 2
                    orr = pr % 2
                    sc.dma("sp", qTh[r][0:96, :], qT1_d[h], reads=[("dram", "qT1", k) for k in range(8)],
                           writes=[(ph, "qT", r)])
                    sc.dma("sp", kTh[r][0:96, :], kT1_d[h], reads=[("dram", "kT1", k) for k in range(8)],
                           writes=[(ph, "kT", r)])
                    sc.dma("sp", vaug[r][:, :, 0:64], v1_d[:, h * 64:(h + 1) * 64].rearrange("(t p) d -> p t d", p=128),
                           reads=[("dram", "v1", k) for k in range(NT)], writes=[(ph, "vaug", r)])
                    streams = []
                    for c in range(int(os.environ.get("DBG_NC", NCH))):
                        ab = 6 + (c % 2)
                        it = dict(qT=qTh[r][0:96, :], kT=kTh[r][0:96, :],
                                  pv=[((lambda kb, r=r: vaug[r][:, kb, :]), ab)],
                                  units=lambda c: attention_units(c, True),
                                  toks=[(ph, "qT", r), (ph, "kT", r)],
                                  vtoks=[(ph, "vaug", r), (ph, "vaug1", r)], ab=ab, e=h % 2, r=orr)
                        streams.append((h, c, [it]))

                    def fin_m(grp, c, items):
                        it = items[0]
                        ab = it["ab"]
                        e_ = it["e"]
                        rr = it["r"]
                        k = ab - 6
                        norm64_pieces(ph, ab, k, recip,
                                      lambda cs: ost[rr][e_ * 64:(e_ + 1) * 64, c * 512 + cs.start:c * 512 + cs.stop])
                    run_attention(ph, streams, masks_sb, pbuf, scale, fin_m, sbanks=(0, 1, 2, 3, 4, 5))
                    if h % 2 == 1:
                        sc.dma("pool", oT_d[pr], ost[orr], reads=[(ph, "ost", orr)] + [(ph, "ostw", k_, j_) for k_ in range(2) for j_ in range(2)],
                               writes=[("dram", "oT", pr)])
                sc.barrier()

        phases = [
            ("P0", phase_P0),
            ("A0", phase_A0),
            ("F0", lambda: phase_F("F0", 0, False, opre=("O0", hyb_w_out_d, x_d))),
            ("P1", phase_P1),
            ("A1", phase_A1),
            ("F1", lambda: phase_F("F1", 1, True, opre=("O1", mla_w_out_d, xres_d))),
        ]
        for name, fn in phases:
            if stop_after == "init":
                break
            if only is not None and name not in only:
                continue
            fn()
            if stop_after == name:
                break
        sc.barrier()
    return nc


_MASKS = None


def _make_masks():
    global _MASKS
    if _MASKS is not None:
        return _MASKS
    m = np.zeros((24, 128, 512), dtype=np.float32)
    j = np.arange(128)[:, None]
    i = np.arange(512)[None, :]
    for idx in range(20):
        off = idx - 16
        delta = (i - j) - 128 * off
        cnt = ((delta >= 0) & (delta <= 128)).astype(np.float32)
        cnt += ((delta >= 0) & (delta <= 512) & (delta % 4 == 0)).astype(np.float32)
        cnt += ((delta >= 0) & (delta <= 2048) & (delta % 16 == 0)).astype(np.float32)
        m[idx] = cnt
    for jj in range(4):
        delta = (i - j) - 128 * jj
        m[20 + jj] = (delta >= 0).astype(np.float32)
    _MASKS = m.astype(ml_dtypes.bfloat16)
    return _MASKS


def make_in_maps(inputs):
    f = lambda a: np.ascontiguousarray(np.asarray(a, dtype=np.float32))
    shared = {
        "attn_norm": f(np.asarray(inputs["attn_norm"]).reshape(2, 8, 128).transpose(0, 2, 1)),
        "ffn_norm": f(np.asarray(inputs["ffn_norm"]).reshape(2, 8, 128).transpose(0, 2, 1)),
        "final_norm": f(inputs["final_norm"]),
        "hyb_w_in": f(inputs["hyb_w_in"][0]),
        "hyb_w_out": f(inputs["hyb_w_out"][0]),
        "lq1": f(inputs["diff_lambda_q1"][0]),
        "lk1": f(inputs["diff_lambda_k1"][0]),
        "lq2": f(inputs["diff_lambda_q2"][0]),
        "lk2": f(inputs["diff_lambda_k2"][0]),
        "subln": f(inputs["diff_subln"][0]),
        "mla_w_in": f(inputs["mla_w_in"][0]),
        "mla_q_norm": f(inputs["mla_q_norm"][0]),
        "mla_w_uq": f(inputs["mla_w_uq"][0]),
        "mla_kv_norm": f(inputs["mla_kv_norm"][0]),
        "mla_w_ukv": f(inputs["mla_w_ukv"][0]),
        "mla_w_out": f(inputs["mla_w_out"][0]),
        "ffn_w_up": f(inputs["ffn_w_up"]),
        "ffn_conv_w": f(np.asarray(inputs["ffn_conv_w"]).reshape(2, 3, 44, 128).transpose(0, 3, 1, 2)),
        "ffn_conv_b": f(np.asarray(inputs["ffn_conv_b"]).reshape(2, 44, 128).transpose(0, 2, 1)),
        "ffn_w_down": f(inputs["ffn_w_down"]),
        "masks": _make_masks(),
        "ident": np.eye(128, dtype=np.float32).astype(ml_dtypes.bfloat16),
    }
    x = np.asarray(inputs["x"], dtype=np.float32)
    pos = np.asarray(inputs["positions"], dtype=np.int32)
    maps = []
    for b in range(8):
        m = dict(shared)
        m["x"] = np.ascontiguousarray(x[b])
        m["pos"] = np.ascontiguousarray(pos[b].reshape(NT, 128).T)
        maps.append(m)
    return maps


def kernel(**inputs):
    nc = build_program()
    in_maps = make_in_maps(inputs)
    res = run_bass_kernel_spmd(nc, in_maps, core_ids=list(range(8)))
    out = np.stack([np.asarray(r["out"], dtype=np.float32) for r in res.results], axis=0)
    return out
```
